# Optimizing a Trainium2 kernel written in Bass

```python
import jax, jax.numpy as jnp
from jax import lax
import numpy as np

D_MODEL = 2048
BATCH = 2
SEQ = 4096
DEPTH = 1

CHUNK = 64
CONV_W = 3
D_CONV = D_MODEL
N_HEADS = 16
N_KV_HEADS = 4
HEAD_DIM = 128
GROUP = N_HEADS // N_KV_HEADS
D_ATTN = N_HEADS * HEAD_DIM
D_KV = N_KV_HEADS * HEAD_DIM
IDX_HEADS = 16
IDX_DIM = 64
IDX_W_SCALE = (IDX_HEADS * IDX_DIM) ** -0.5
TOPK_MAX = 256
Q_BLOCK = 128
D_FF = 5632
ALPHA = (2.0 * DEPTH) ** 0.25
BETA = (8.0 * DEPTH) ** -0.25
LN_EPS = 1e-5
N_MOD = 6

_IN_SIZES = (D_CONV, D_CONV, D_CONV,
             D_ATTN, D_KV, D_KV,
             IDX_HEADS * IDX_DIM, IDX_DIM, IDX_HEADS,
             D_MODEL, D_MODEL)
D_IN = sum(_IN_SIZES)

kernel_name = "hybrid_shortconv_dsa_convffn_deepnorm_adaln"


def layer_norm(x, g, b):
    xf = x.astype(jnp.float32)
    mu = jnp.mean(xf, axis=-1, keepdims=True)
    var = jnp.mean(jnp.square(xf - mu), axis=-1, keepdims=True)
    y = (xf - mu) * lax.rsqrt(var + LN_EPS)
    return (y * g.astype(jnp.float32) + b.astype(jnp.float32)).astype(x.dtype)


def causal_dwconv(h, w):
    S = h.shape[1]
    hp = jnp.pad(h, ((0, 0), (CONV_W - 1, 0), (0, 0)))
    out = hp[:, 0:S] * w[0]
    for j in range(1, CONV_W):
        out = out + hp[:, j:j + S] * w[j]
    return out


def dsa_attention(q, k, v, q_idx, k_idx, w_idx):
    B, S = q.shape[0], q.shape[1]
    topk = min(TOPK_MAX, S // 4)
    nblk = S // Q_BLOCK
    key_pos = jnp.arange(S)
    f32 = jnp.float32
    k_idx32 = k_idx.astype(f32)

    def to_blocks(a):
        return jnp.swapaxes(a.reshape((B, nblk, Q_BLOCK) + a.shape[2:]), 0, 1)

    def block_fn(args):
        blk, qb, qib, wb = args
        q_pos = blk * Q_BLOCK + jnp.arange(Q_BLOCK)
        limit = (q_pos // CHUNK + 1) * CHUNK
        admissible = key_pos[None, :] < limit[:, None]
        dots = jnp.einsum('bqhd,bsd->bqhs', qib.astype(f32), k_idx32)
        score = jnp.einsum('bqh,bqhs->bqs', wb.astype(f32), jax.nn.relu(dots))
        score = jnp.where(admissible[None], score, -jnp.inf)
        _, sel = lax.top_k(score, topk)
        valid = sel < limit[None, :, None]
        ks = jax.vmap(lambda kb, ib: kb[ib])(k, sel)
        vs = jax.vmap(lambda vb, ib: vb[ib])(v, sel)
        qg = qb.reshape(B, Q_BLOCK, N_KV_HEADS, GROUP, HEAD_DIM)
        logits = jnp.einsum('bqngd,bqknd->bqngk', qg.astype(f32), ks.astype(f32)) * (HEAD_DIM ** -0.5)
        logits = jnp.where(valid[:, :, None, None, :], logits, -jnp.inf)
        p = jax.nn.softmax(logits, axis=-1)
        o = jnp.einsum('bqngk,bqknd->bqngd', p, vs.astype(f32))
        return o.reshape(B, Q_BLOCK, D_ATTN).astype(q.dtype)

    out = lax.map(block_fn, (jnp.arange(nblk), to_blocks(q), to_blocks(q_idx), to_blocks(w_idx)))
    return jnp.swapaxes(out, 0, 1).reshape(B, S, D_ATTN)


def setup_inputs(seed: int = 0) -> dict:
    key = jax.random.key(seed)
    ks = jax.random.split(key, 20)
    f32 = jnp.float32
    L = DEPTH

    def nrm(k, shape, scale):
        return jax.random.normal(k, shape, f32) * scale

    return {
        "x": nrm(ks[0], (BATCH, SEQ, D_MODEL), 1.0),
        "c": nrm(ks[1], (BATCH, D_MODEL), 1.0),
        "w_cond": nrm(ks[2], (L, D_MODEL, N_MOD * D_MODEL), 0.2 * D_MODEL ** -0.5),
        "b_cond": nrm(ks[3], (L, N_MOD * D_MODEL), 0.01),
        "w_in": nrm(ks[4], (L, D_MODEL, D_IN), D_MODEL ** -0.5),
        "conv_a": nrm(ks[5], (L, CONV_W, D_CONV), CONV_W ** -0.5),
        "idx_kn_g": 1.0 + nrm(ks[6], (L, IDX_DIM), 0.02),
        "idx_kn_b": nrm(ks[7], (L, IDX_DIM), 0.02),
        "w_a": nrm(ks[8], (L, D_CONV, D_MODEL), D_CONV ** -0.5),
        "w_b": nrm(ks[9], (L, D_ATTN, D_MODEL), D_ATTN ** -0.5),
        "w_o": nrm(ks[10], (L, D_MODEL, D_MODEL), BETA * D_MODEL ** -0.5),
        "ln1_g": 1.0 + nrm(ks[11], (L, D_MODEL), 0.02),
        "ln1_b": nrm(ks[12], (L, D_MODEL), 0.02),
        "w_up": nrm(ks[13], (L, D_MODEL, 2 * D_FF), D_MODEL ** -0.5),
        "conv_f": nrm(ks[14], (L, CONV_W, D_FF), CONV_W ** -0.5),
        "w_down": nrm(ks[15], (L, D_FF, D_MODEL), BETA * D_FF ** -0.5),
        "ln2_g": 1.0 + nrm(ks[16], (L, D_MODEL), 0.02),
        "ln2_b": nrm(ks[17], (L, D_MODEL), 0.02),
    }


def reference(x, c, w_cond, b_cond, w_in, conv_a, idx_kn_g, idx_kn_b, w_a, w_b, w_o,
              ln1_g, ln1_b, w_up, conv_f, w_down, ln2_g, ln2_b):
    B, S, D = x.shape
    offsets = [int(o) for o in np.cumsum(_IN_SIZES)[:-1]]
    c_act = jax.nn.silu(c)
    for l in range(DEPTH):
        mod = (jnp.einsum('bd,de->be', c_act, w_cond[l]) + b_cond[l])[:, None, :]
        sh_m, sc_m, g_m, sh_f, sc_f, g_f = jnp.split(mod, N_MOD, axis=-1)

        u = x * (1.0 + sc_m) + sh_m
        proj = jnp.einsum('bsd,de->bse', u, w_in[l])
        cb, cc, ch, q, k, v, qi, ki, wi, ga, gb = jnp.split(proj, offsets, axis=-1)

        y_a = cb * causal_dwconv(cc * ch, conv_a[l])

        ki = layer_norm(ki, idx_kn_g[l], idx_kn_b[l])
        y_b = dsa_attention(q.reshape(B, S, N_HEADS, HEAD_DIM),
                            k.reshape(B, S, N_KV_HEADS, HEAD_DIM),
                            v.reshape(B, S, N_KV_HEADS, HEAD_DIM),
                            qi.reshape(B, S, IDX_HEADS, IDX_DIM),
                            ki,
                            wi * IDX_W_SCALE)

        merged = (jax.nn.sigmoid(ga) * jnp.einsum('bsc,cd->bsd', y_a, w_a[l])
                  + jax.nn.sigmoid(gb) * jnp.einsum('bsc,cd->bsd', y_b, w_b[l]))
        mix_out = jnp.einsum('bsd,de->bse', merged, w_o[l])
        x = layer_norm(ALPHA * x + (1.0 + g_m) * mix_out, ln1_g[l], ln1_b[l])

        u = x * (1.0 + sc_f) + sh_f
        h_act, h_gate = jnp.split(jnp.einsum('bsd,df->bsf', u, w_up[l]), 2, axis=-1)
        h_act = causal_dwconv(h_act, conv_f[l])
        y = jnp.einsum('bsf,fd->bsd', jax.nn.gelu(h_act) * h_gate, w_down[l])
        x = layer_norm(ALPHA * x + (1.0 + g_f) * y, ln2_g[l], ln2_b[l])
    return x
```

```python
import numpy as np
import ml_dtypes
from contextlib import ExitStack
import concourse.bass as bass
import concourse.mybir as mybir
from concourse.bass_utils import run_bass_kernel_spmd

F32 = mybir.dt.float32
BF16 = mybir.dt.bfloat16
AF = mybir.ActivationFunctionType
ALU = mybir.AluOpType
AX = mybir.AxisListType

D = 2048
S = 4096
NT = 1028
HALO = 4
TT = [(0, 343), (343, 686), (686, 1028)]
NQT = 9
QTS = 114
NQB = 3
QBS = 342
DFF = 5632
NFC = 44
ALPHA = 2.0 ** 0.25
LN_EPS = 1e-5
IDX_W_SCALE = 1024.0 ** -0.5
TOPK = 256.0
BIG = 1.0e6
NBIS = 16
WR = 8
O_B, O_C, O_H, O_Q, O_K, O_V, O_QI, O_KI, O_WI, O_GA, O_GB = 0, 2048, 4096, 6144, 8192, 8704, 9216, 10240, 10304, 10320, 12368


def stream_plan():
    items, seq = [], []

    def add(*d):
        items.append(d)
        seq.append(len(items) - 1)
    for h in range(16):
        add('w_in', O_Q + h * 128, 0, 16)
    for c in range(8):
        add('w_in', O_QI + c * 128, 0, 16)
    for c in range(16):
        add('w_b', c * 128, 0, 16)
        add('w_in', O_GB + c * 128, 0, 16)
    for c in range(16):
        add('w_in', O_C + c * 128, 0, 16)
        add('w_in', O_H + c * 128, 0, 16)
        add('w_in', O_B + c * 128, 0, 16)
    for c in range(16):
        add('w_a', c * 128, 0, 16)
        add('w_in', O_GA + c * 128, 0, 16)
    for c in range(16):
        add('w_o', c * 128, 0, 16)
    for j in range(NFC):
        add('w_up', j * 128, 0, 16)
        add('w_up', DFF + j * 128, 0, 16)
    base = len(items)
    for c in range(16):
        for part in range(3):
            items.append(('w_down', c * 128, part * 16, min(16, NFC - part * 16)))
    for tt in range(3):
        for c in range(16):
            for part in range(3):
                seq.append(base + c * 3 + part)
    return items, seq


class KB:
    def __init__(self, nc, es):
        self.nc, self.es = nc, es
        self.engs = {'pe': nc.tensor, 'act': nc.scalar, 'dve': nc.vector, 'pool': nc.gpsimd, 'sp': nc.sync}
        self.sems = {k: es.enter_context(nc.semaphore('s_' + k)) for k in self.engs}
        self.cnt = {k: 0 for k in self.engs}
        self.waited = {}
        self.res = {}
        self.pbank = 0

    def dsem(self, name):
        if name not in self.sems:
            self.sems[name] = self.es.enter_context(self.nc.semaphore('d_' + name))
            self.cnt[name] = 0
        return self.sems[name]

    def _deps(self, reads, writes):
        deps = []
        for r in reads:
            st = self.res.get(r)
            if st and st[0]:
                deps.append(st[0])
        for w in writes:
            st = self.res.get(w)
            if st:
                if st[0]:
                    deps.append(st[0])
                deps.extend(st[1])
        return deps

    def wait(self, eng, deps):
        for (key, val) in deps:
            if key == eng and eng == 'pe':
                continue
            if self.waited.get((eng, key), 0) >= val:
                continue
            self.engs[eng].wait_ge(self.sems[key], val)
            self.waited[(eng, key)] = val

    def _upd(self, tok, reads, writes):
        for r in reads:
            st = self.res.setdefault(r, [None, []])
            st[1].append(tok)
        for w in writes:
            self.res[w] = [tok, []]

    def op(self, eng, fn, reads=(), writes=(), signal=True):
        self.wait(eng, self._deps(reads, writes))
        ins = fn(self.engs[eng])
        if signal:
            self.cnt[eng] += 1
            ins.then_inc(self.sems[eng], 1)
            tok = (eng, self.cnt[eng])
        else:
            tok = (eng, self.cnt[eng] + 1)
        self._upd(tok, reads, writes)
        return tok

    def dma(self, queue, out, in_, sem, reads=(), writes=()):
        self.dsem(sem)
        self.wait(queue, self._deps(reads, writes))
        ins = self.engs[queue].dma_start(out=out, in_=in_)
        self.cnt[sem] += 16
        ins.then_inc(self.sems[sem], 16)
        tok = (sem, self.cnt[sem])
        self._upd(tok, reads, writes)
        return tok

    def barrier(self):
        toks = [(k, v) for k, v in self.cnt.items() if v > 0]
        for e in ('pe', 'act', 'dve', 'pool', 'sp'):
            self.wait(e, toks)

    def bank(self):
        b = self.pbank
        self.pbank = (self.pbank + 1) % 8
        return b


def build_nc(debug=False):
    nc = bass.Bass("TRN2", target_bir_lowering=False)
    items, seq = stream_plan()
    NI = len(items)

    def din(name, shape, dt=F32):
        return nc.dram_tensor("i_" + name, shape, dt, kind="ExternalInput").ap()
    xo_d = din("xo", [16, 128, NT])
    xs_d = din("xs", [16, 128, 16, 256])
    cT_d = din("cT", [128, 16])
    wc_d = din("wcond", [24, 128, 16, 512])
    bc_d = din("bcond", [1, 12288])
    ws_d = din("ws", [NI, 128, 16, 128])
    wk_d = din("wk", [128, 16, 512])
    wv_d = din("wv", [128, 16, 512])
    wki_d = din("wki", [128, 16, 64])
    wwi_d = din("wwi", [128, 16, 16])
    cva_d = din("cva", [128, 16, 3])
    cvf_d = din("cvf", [128, NFC, 3])
    lnp_d = din("lnp", [128, 4, 16])
    kng_d = din("kng", [128, 2, 64])
    qlim_d = din("qlim", [128, NQT])
    halo_d = din("halom", [128, 1])
    cidx_d = din("cidx", [128, 64])
    ident_d = din("ident", [128, 128])
    okind = "ExternalOutput" if debug else "Internal"
    out_d = nc.dram_tensor("out", [16, 128, NT], F32, kind="ExternalOutput").ap()
    kT_d = nc.dram_tensor("kT_s", [4, 128, S], BF16, kind=okind).ap()
    v_d = nc.dram_tensor("v_s", [4, 128, 32, 128], BF16, kind=okind).ap()
    yb_d = nc.dram_tensor("yb_s", [16, 128, NT], BF16, kind=okind).ap()
    x1_d = nc.dram_tensor("x1_s", [16, 128, NT], F32, kind=okind).ap()
    mod_o = nc.dram_tensor("mod_o", [128, 96], F32, kind=okind).ap()

    with ExitStack() as es:
        K = KB(nc, es)

        def sb(name, shape, dt=F32, scope=es):
            return scope.enter_context(nc.sbuf_tensor(name, shape, dt))
        ps = [es.enter_context(nc.psum_tensor("ps%d" % i, [128, 512], F32)) for i in range(8)]

        def PS(b):
            return "ps%d" % b

        ident = sb("ident", [128, 128], BF16)
        identf = sb("identf", [128, 128], F32)
        onesb = sb("onesb", [128, 128], BF16)
        onesf = sb("onesf", [128, 128], F32)
        one11 = sb("one11", [1, 2], F32)
        epst = sb("epst", [128, 1], F32)
        cva = sb("cva", [128, 16, 3])
        cvf = sb("cvf", [128, NFC, 3])
        lnp = sb("lnp", [128, 4, 16])
        kng = sb("kng", [128, 2, 64])
        qlim = sb("qlim", [128, NQT])
        halom = sb("halom", [128, 1])
        cidx = sb("cidx", [128, 64])
        modT = sb("modT", [128, 96])
        modP = sb("modP", [128, 96])
        wring = sb("wring", [128, WR, 16, 128], BF16)
        kiTE = sb("kiTE", [128, S], BF16)
        kiTO = sb("kiTO", [128, S], BF16)
        for (t, d_) in ((identf, ident_d), (cva, cva_d), (cvf, cvf_d), (lnp, lnp_d), (kng, kng_d), (qlim, qlim_d),
                        (halom, halo_d), (cidx, cidx_d)):
            K.dma('sp', t[:], d_, 'c_' + t.name, writes=[t.name])
        K.op('dve', lambda e: e.tensor_copy(out=ident[:], in_=identf[:]), reads=['identf'], writes=['ident'])
        K.op('dve', lambda e: e.memset(onesb[:], 1.0), writes=['onesb'])
        K.op('dve', lambda e: e.memset(onesf[:], 1.0), writes=['onesf'])
        K.op('dve', lambda e: e.memset(one11[:], 1.0), writes=['one11'])
        K.op('dve', lambda e: e.memset(epst[:], LN_EPS), writes=['epst'])

        wst = {'pos': 0, 'loaded': 0}

        def w_prefetch(upto):
            while wst['loaded'] < min(upto, len(seq)):
                i = wst['loaded']
                slot = i % WR
                K.dma('pool', wring[:, slot, :, :], ws_d[seq[i]], 'w%d' % slot, writes=['w%d' % slot])
                wst['loaded'] += 1

        def next_w():
            i = wst['pos']
            w_prefetch(i + 1)
            wst['pos'] += 1
            w_prefetch(i + WR - 2)
            return i % WR

        def mm_group(bank, M, n, pairs, reads_list, moff=0):
            last = len(pairs) - 1
            for i, (l, r) in enumerate(pairs):
                K.op('pe', lambda e, l=l, r=r, i=i: e.matmul(ps[bank][moff:moff + M, 0:n], lhsT=l, rhs=r,
                                                              start=(i == 0), stop=(i == last)),
                     reads=reads_list[i], writes=[PS(bank)], signal=(i == last))

        with ExitStack() as p0:
            cTs = sb("cTs", [128, 16], F32, p0)
            cact = sb("cact", [128, 16], BF16, p0)
            wcb = sb("wcb", [128, 2, 16, 512], BF16, p0)
            modrow = sb("modrow", [1, 12288], F32, p0)
            brow = sb("brow", [1, 12288], F32, p0)
            K.dma('sp', cTs[:], cT_d, 'c_cTs', writes=['cTs'])
            K.dma('sp', brow[:], bc_d, 'c_brow', writes=['brow'])
            K.op('act', lambda e: e.activation(out=cact[:], in_=cTs[:], func=AF.Silu), reads=['cTs'], writes=['cact'])
            for nb in range(24):
                sl = nb % 2
                K.dma('pool', wcb[:, sl, :, :], wc_d[nb], 'wc%d' % sl, writes=['wc%d' % sl])
                b = K.bank()
                mm_group(b, 1, 512, [(cact[:, kc:kc + 1], wcb[:, sl, kc, :]) for kc in range(16)],
                         [['cact', 'wc%d' % sl]] * 16)
                K.op('dve', lambda e, b=b, nb=nb: e.tensor_tensor(out=modrow[0:1, nb * 512:(nb + 1) * 512],
                                                                   in0=ps[b][0:1, 0:512],
                                                                   in1=brow[0:1, nb * 512:(nb + 1) * 512], op=ALU.add),
                     reads=[PS(b), 'brow'], writes=['modrow'])
            w_prefetch(WR - 2)
            b = K.bank()
            for c in range(96):
                K.op('pe', lambda e, c=c: e.matmul(ps[b][:, c:c + 1], lhsT=modrow[0:1, c * 128:(c + 1) * 128],
                                                   rhs=one11[0:1, 0:1], start=True, stop=True),
                     reads=['modrow', 'one11'], writes=[PS(b)], signal=(c == 95))
            K.op('dve', lambda e: e.tensor_copy(out=modT[:], in_=ps[b][:, 0:96]), reads=[PS(b)], writes=['modT'])
            K.op('dve', lambda e: e.tensor_scalar(out=modP[:], in0=modT[:], scalar1=1.0, scalar2=None, op0=ALU.add),
                 reads=['modT'], writes=['modP'])
            K.barrier()
        if debug:
            K.dma('sp', mod_o, modT[:], 'dbg', reads=['modT'])
        SH_M, SC_M, G_M, SH_F, SC_F, G_F = 0, 16, 32, 48, 64, 80

        def modulate(eng_i, out_ap, in_ap, kc, scb, shb, reads, writes):
            if eng_i % 2 == 0:
                K.op('dve', lambda e: e.tensor_scalar(out=out_ap, in0=in_ap, scalar1=modP[:, scb + kc:scb + kc + 1],
                                                      scalar2=modT[:, shb + kc:shb + kc + 1], op0=ALU.mult, op1=ALU.add),
                     reads=reads + ['modT', 'modP'], writes=writes)
            else:
                K.op('act', lambda e: e.activation(out=out_ap, in_=in_ap, func=AF.Identity,
                                                   bias=modT[:, shb + kc:shb + kc + 1],
                                                   scale=modP[:, scb + kc:scb + kc + 1]),
                     reads=reads + ['modT', 'modP'], writes=writes)

        store_toks = []
        with ExitStack() as p1:
            wk = sb("wk", [128, 16, 512], BF16, p1)
            wv = sb("wv", [128, 16, 512], BF16, p1)
            wki = sb("wki", [128, 16, 64], BF16, p1)
            xsb = sb("xsb", [128, 2, 16, 256], F32, p1)
            usb = sb("usb", [128, 2, 16, 256], BF16, p1)
            kst = sb("kst", [128, 2, 4, 256], BF16, p1)
            vst = sb("vst", [128, 2, 2, 512], BF16, p1)
            kraw = sb("kraw", [128, 2, 64], F32, p1)
            kn2 = sb("kn2", [128, 2, 2, 128], BF16, p1)
            K.op('dve', lambda e: e.memset(kn2[:], 0.0), writes=['ki0n', 'ki1n'])
            bst = sb("bst", [128, 2, 6], F32, p1)
            bmv = sb("bmv", [128, 2, 2], F32, p1)
            K.dma('pool', wk[:], wk_d, 'c_wk', writes=['wk'])
            K.dma('pool', wv[:], wv_d, 'c_wv', writes=['wv'])
            K.dma('pool', wki[:], wki_d, 'c_wki', writes=['wki'])
            K.dma('sp', xsb[:, 0, :, :], xs_d[0], 'xs0', writes=['xs0'])
            for tb in range(16):
                sl = tb % 2
                if tb + 1 < 16:
                    K.dma('sp', xsb[:, 1 - sl, :, :], xs_d[tb + 1], 'xs%d' % (1 - sl), writes=['xs%d' % (1 - sl)])
                for kc in range(16):
                    modulate(kc, usb[:, sl, kc, :], xsb[:, sl, kc, :], kc, SC_M, SH_M,
                             ['xs%d' % sl], ['us%d_%d' % (sl, kc)])
                for n in range(4):
                    b = K.bank()
                    mm_group(b, 128, 256, [(wk[:, kc, n * 128:(n + 1) * 128], usb[:, sl, kc, :]) for kc in range(16)],
                             [['wk', 'us%d_%d' % (sl, kc)] for kc in range(16)])
                    K.op('act', lambda e, b=b, n=n: e.copy(out=kst[:, sl, n, :], in_=ps[b][:, 0:256]),
                         reads=[PS(b)], writes=['kst%d' % sl])
                store_toks.append(K.dma('sp', kT_d[:, :, tb * 256:(tb + 1) * 256].rearrange("n p t -> p n t"),
                                        kst[:, sl, :, :], 'kst%d' % sl, reads=['kst%d' % sl]))
                for sub in range(2):
                    tsl = slice(sub * 128, (sub + 1) * 128)
                    b = K.bank()
                    mm_group(b, 128, 512, [(usb[:, sl, kc, tsl], wv[:, kc, :]) for kc in range(16)],
                             [['wv', 'us%d_%d' % (sl, kc)] for kc in range(16)])
                    K.op('dve', lambda e, b=b, sub=sub: e.tensor_copy(out=vst[:, sl, sub, :], in_=ps[b][:, 0:512]),
                         reads=[PS(b)], writes=['vst%d' % sl])
                    b = K.bank()
                    mm_group(b, 128, 64, [(usb[:, sl, kc, tsl], wki[:, kc, :]) for kc in range(16)],
                             [['wki', 'us%d_%d' % (sl, kc)] for kc in range(16)])
                    r_ = 'ki%d' % sub
                    K.op('act', lambda e, b=b, sub=sub: e.copy(out=kraw[:, sub, :], in_=ps[b][:, 0:64]),
                         reads=[PS(b)], writes=[r_])
                    K.op('dve', lambda e, sub=sub: e.bn_stats(out=bst[:, sub, :], in_=kraw[:, sub, :]),
                         reads=[r_], writes=[r_ + 's'])
                    K.op('dve', lambda e, sub=sub: e.bn_aggr(out=bmv[:, sub, :], in_=bst[:, sub, :]),
                         reads=[r_ + 's'], writes=[r_ + 'm'])
                    K.op('act', lambda e, sub=sub: e.activation(out=bmv[:, sub, 1:2], in_=bmv[:, sub, 1:2], func=AF.Sqrt,
                                                                bias=epst[:, 0:1], scale=1.0),
                         reads=[r_ + 'm', 'epst'], writes=[r_ + 'm'])
                    K.op('dve', lambda e, sub=sub: e.reciprocal(out=bmv[:, sub, 1:2], in_=bmv[:, sub, 1:2]),
                         reads=[r_ + 'm'], writes=[r_ + 'm'])
                    K.op('dve', lambda e, sub=sub: e.tensor_scalar(out=kraw[:, sub, :], in0=kraw[:, sub, :],
                                                                   scalar1=bmv[:, sub, 0:1], scalar2=bmv[:, sub, 1:2],
                                                                   op0=ALU.subtract, op1=ALU.mult),
                         reads=[r_, r_ + 'm'], writes=[r_])
                    K.op('dve', lambda e, sub=sub: e.tensor_tensor(out=kraw[:, sub, :], in0=kraw[:, sub, :],
                                                                   in1=kng[:, 0, :], op=ALU.mult),
                         reads=[r_, 'kng'], writes=[r_])
                    K.op('dve', lambda e, sub=sub: e.tensor_tensor(out=kn2[:, sub, 0, 0:64], in0=kraw[:, sub, :],
                                                                   in1=kng[:, 1, :], op=ALU.add),
                         reads=[r_, 'kng'], writes=[r_ + 'n'])
                    K.op('dve', lambda e, sub=sub: e.tensor_copy(out=kn2[:, sub, 1, 64:128], in_=kn2[:, sub, 0, 0:64]),
                         reads=[r_ + 'n'], writes=[r_ + 'n'])
                    b = K.bank()
                    pT = ps[b][:].bitcast(BF16)
                    for eo in range(2):
                        K.op('pe', lambda e, pT=pT, sub=sub, eo=eo: e.transpose(out=pT[:, eo * 128:(eo + 1) * 128],
                                                                                in_=kn2[:, sub, eo, :],
                                                                                identity=ident[:, :]),
                             reads=[r_ + 'n', 'ident'], writes=[PS(b)], signal=(eo == 1))
                    t0 = tb * 256 + sub * 128
                    K.op('act', lambda e, pT=pT, t0=t0: e.copy(out=kiTE[:, t0:t0 + 128], in_=pT[:, 0:128]),
                         reads=[PS(b)], writes=['kiT'])
                    K.op('act', lambda e, pT=pT, t0=t0: e.copy(out=kiTO[:, t0:t0 + 128], in_=pT[:, 128:256]),
                         reads=[PS(b)], writes=['kiT'])
                for n in range(4):
                    store_toks.append(K.dma('sp', v_d[n][:, tb * 2:tb * 2 + 2, :], vst[:, sl, :, n * 128:(n + 1) * 128],
                                            'vst%d' % sl, reads=['vst%d' % sl]))
            K.barrier()

        def build_u(uT, scope, scb, shb, src_d, tag):
            xr = sb("xr_" + tag, [128, 2, NT], F32, scope)
            for kc in range(16):
                sl = kc % 2
                K.dma('sp', xr[:, sl, :], src_d[kc], 'xr%d' % sl, writes=['xr%d' % sl])
                modulate(kc, uT[:, kc, :], xr[:, sl, :], kc, scb, shb, ['xr%d' % sl], ['u_%d' % kc])

        def dense(rhsT, rhs_res, nkc=16):
            slot = next_w()
            banks = []
            for (a, z) in TT:
                b = K.bank()
                mm_group(b, 128, z - a, [(wring[:, slot, kc, :], rhsT[:, kc, a:z]) for kc in range(nkc)],
                         [['w%d' % slot, rhs_res(kc)] for kc in range(nkc)])
                banks.append(b)
            return banks

        with ExitStack() as pA:
            qT = sb("qT", [128, 16, NT], BF16, pA)
            qiT = sb("qiT", [128, 8, NT], BF16, pA)
            wtm = sb("wtm", [128, NQT, 16], F32, pA)
            with ExitStack() as p2a:
                u1T = sb("u1T", [128, 16, NT], BF16, p2a)
                wwi = sb("wwi", [128, 16, 16], BF16, p2a)
                K.dma('pool', wwi[:], wwi_d, 'c_wwi', writes=['wwi'])
                build_u(u1T, p2a, SC_M, SH_M, xo_d, "a")
                for h in range(24):
                    banks = dense(u1T, lambda kc: 'u_%d' % kc)
                    for ti, (a, z) in enumerate(TT):
                        b = banks[ti]
                        dst = qT[:, h, a:z] if h < 16 else qiT[:, h - 16, a:z]
                        dres = 'q_%d' % h
                        if (h + ti) % 2 == 0:
                            K.op('act', lambda e, b=b, dst=dst, n=z - a: e.copy(out=dst, in_=ps[b][:, 0:n]),
                                 reads=[PS(b)], writes=[dres])
                        else:
                            K.op('dve', lambda e, b=b, dst=dst, n=z - a: e.tensor_copy(out=dst, in_=ps[b][:, 0:n]),
                                 reads=[PS(b)], writes=[dres])
                for qt in range(NQT):
                    j0 = 2 + qt * QTS
                    b = K.bank()
                    mm_group(b, QTS, 16, [(u1T[:, kc, j0:j0 + QTS], wwi[:, kc, :]) for kc in range(16)],
                             [['wwi', 'u_%d' % kc] for kc in range(16)])
                    K.op('act', lambda e, b=b, qt=qt: e.mul(out=wtm[0:QTS, qt, :], in_=ps[b][0:QTS, 0:16],
                                                            mul=IDX_W_SCALE),
                         reads=[PS(b)], writes=['wtm'])
                K.barrier()

            with ExitStack() as pt:
                score = sb("score", [128, 2, S], F32, pt)
                dg = sb("dg", [128, 16, QTS], BF16, pt)
                rb = sb("rb", [128, 4, 512], BF16, pt)
                mask = sb("mask", [128, S], BF16, pt)
                maskT = sb("maskT", [128, 32, QBS], BF16, pt)
                pen = sb("pen", [128, 2, 64], F32, pt)
                bis = sb("bis", [128, 8], F32, pt)
                nst = sb("nst", [128, NBIS], F32, pt)
                pwn = sb("pwn", [128, NBIS], F32, pt)
                c35 = sb("c35", [128, 1], F32, pt)
                knb = sb("knb", [128, 2, S], BF16, pt)
                vnb = sb("vnb", [128, 32, 128], BF16, pt)
                Eb = sb("Eb", [128, 4, QBS], BF16, pt)
                Em = sb("Em", [128, 4, QBS], BF16, pt)
                rz = sb("rz", [128, QBS], F32, pt)
                ybs = sb("ybs", [128, 2, QBS], BF16, pt)
                for k_ in range(NBIS):
                    K.op('dve', lambda e, k_=k_: e.memset(pwn[:, k_:k_ + 1], -(2.0 ** -(k_ + 1))), writes=['pwn'])
                K.op('dve', lambda e: e.memset(c35[:], float(S) - 2.0 * TOPK + 0.5), writes=['c35'])
                K.wait('sp', store_toks)
                kvload = [0]
                kvslot = {}
                yb_toks = []
                LOOK = 3
                SBK = [4, 5, 6, 7]
                P = slice(0, QTS)
                uctr = [0]

                def emit_acc(qt):
                    sb_ = qt % 2
                    j0 = 2 + qt * QTS
                    K.op('dve', lambda e: e.tensor_scalar(out=pen[P, sb_, :], in0=cidx[P, :],
                                                          scalar1=qlim[P, qt:qt + 1], scalar2=-BIG,
                                                          op0=ALU.is_ge, op1=ALU.mult),
                         reads=['cidx', 'qlim'], writes=['pen%d' % sb_])
                    for h in range(16):
                        K.op('dve', lambda e, h=h: e.tensor_scalar(out=dg[P, h, :], in0=ident[P, 0:QTS],
                                                                   scalar1=wtm[P, qt, h:h + 1], scalar2=None,
                                                                   op0=ALU.mult),
                             reads=['ident', 'wtm'], writes=['dg%d' % h])
                    for kb in range(8):
                        ks = slice(kb * 512, (kb + 1) * 512)
                        sbank = kb % 2
                        rings = {}

                        def emit_D(h):
                            u = uctr[0]
                            uctr[0] += 1
                            bank = 2 + (u % 6)
                            ring = u % 4
                            kz = kiTE if h % 2 == 0 else kiTO
                            K.op('pe', lambda e: e.matmul(ps[bank][P, 0:512], lhsT=qiT[:, h // 2, j0:j0 + QTS],
                                                          rhs=kz[:, ks], start=True, stop=True),
                                 reads=['q_%d' % (16 + h // 2), 'kiT'], writes=[PS(bank)])
                            if h % 8 == 0:
                                K.op('act', lambda e: e.activation(out=rb[P, ring, :], in_=ps[bank][P, 0:512],
                                                                   func=AF.Relu),
                                     reads=[PS(bank)], writes=['rb%d' % ring])
                            else:
                                K.op('dve', lambda e: e.tensor_scalar(out=rb[P, ring, :], in0=ps[bank][P, 0:512],
                                                                      scalar1=0.0, scalar2=None, op0=ALU.max),
                                     reads=[PS(bank)], writes=['rb%d' % ring])
                            rings[h] = ring
                        for h in range(2):
                            emit_D(h)
                        for h in range(16):
                            if h + 2 < 16:
                                emit_D(h + 2)
                            K.op('pe', lambda e, h=h, r_=rings[h]: e.matmul(
                                ps[sbank][P, 0:512], lhsT=dg[P, h, :], rhs=rb[P, r_, :],
                                start=(h == 0), stop=(h == 15)),
                                reads=['dg%d' % h, 'rb%d' % rings[h]], writes=[PS(sbank)], signal=(h == 15))
                        K.op('dve', lambda e: e.tensor_tensor(
                            out=score[P, sb_, ks].rearrange("p (c j) -> p c j", j=64),
                            in0=ps[sbank][P, 0:512].rearrange("p (c j) -> p c j", j=64),
                            in1=pen[P, sb_, kb * 8:(kb + 1) * 8].unsqueeze(2).to_broadcast([QTS, 8, 64]), op=ALU.add),
                            reads=[PS(sbank), 'pen%d' % sb_], writes=['sc%d_%d' % (sb_, kb)])

                def emit_bis(qt):
                    sb_ = qt % 2
                    qi3 = qt % 3
                    allsc = ['sc%d_%d' % (sb_, kb) for kb in range(8)]
                    K.op('dve', lambda e: e.tensor_reduce(out=bis[P, 5:6], in_=score[P, sb_, :], axis=AX.X, op=ALU.max),
                         reads=allsc, writes=['bisA'])
                    K.op('dve', lambda e: e.tensor_reduce(out=bis[P, 6:7], in_=score[P, sb_, 0:256], axis=AX.X,
                                                          op=ALU.min), reads=allsc, writes=['bisB'])
                    K.op('dve', lambda e: e.tensor_scalar(out=bis[P, 6:7], in0=bis[P, 6:7], scalar1=-1000.0,
                                                          scalar2=None, op0=ALU.max), reads=['bisB'], writes=['bisB'])
                    K.op('dve', lambda e: e.scalar_tensor_tensor(out=bis[P, 0:1], in0=bis[P, 5:6], scalar=-0.5,
                                                                 in1=bis[P, 6:7], op0=ALU.mult, op1=ALU.subtract),
                         reads=['bisA', 'bisB'], writes=['bisN'])
                    K.op('dve', lambda e: e.scalar_tensor_tensor(out=bis[P, 0:1], in0=bis[P, 6:7], scalar=0.5,
                                                                 in1=bis[P, 0:1], op0=ALU.mult, op1=ALU.add),
                         reads=['bisB', 'bisN'], writes=['bisN'])
                    K.op('dve', lambda e: e.tensor_tensor(out=bis[P, 1:2], in0=bis[P, 5:6], in1=bis[P, 6:7],
                                                          op=ALU.subtract), reads=['bisA', 'bisB'], writes=['bisH'])
                    K.op('dve', lambda e: e.tensor_scalar(out=bis[P, 1:2], in0=bis[P, 1:2], scalar1=0.5,
                                                          scalar2=1e-3, op0=ALU.mult, op1=ALU.add),
                         reads=['bisH'], writes=['bisH'])
                    K.op('dve', lambda e: e.tensor_scalar(out=nst[P, :], in0=pwn[P, :], scalar1=bis[P, 1:2],
                                                          scalar2=None, op0=ALU.mult),
                         reads=['bisH', 'pwn'], writes=['nst'])
                    K.op('dve', lambda e: e.tensor_scalar(out=bis[P, 2:3], in0=bis[P, 1:2],
                                                          scalar1=2.0 ** -NBIS, scalar2=None, op0=ALU.mult),
                         reads=['bisH'], writes=['bisL'])
                    for it in range(NBIS):
                        K.op('act', lambda e: e.activation(out=mask[P, :], in_=score[P, sb_, :], func=AF.Sign,
                                                           bias=bis[P, 0:1], scale=1.0, accum_out=bis[P, 3:4]),
                             reads=allsc + ['bisN'], writes=['mask', 'bisC'])
                        K.op('act', lambda e: e.activation(out=bis[P, 4:5], in_=bis[P, 3:4], func=AF.Sign,
                                                           bias=c35[P, 0:1], scale=1.0),
                             reads=['bisC', 'c35'], writes=['bisS'])
                        K.op('act', lambda e, it=it: e.activation(out=bis[P, 0:1], in_=bis[P, 4:5],
                                                                  func=AF.Identity, bias=bis[P, 0:1],
                                                                  scale=nst[P, it:it + 1]),
                             reads=['bisS', 'nst', 'bisN'], writes=['bisN'])
                    K.op('dve', lambda e: e.scalar_tensor_tensor(out=bis[P, 7:8], in0=bis[P, 0:1], scalar=-1.0,
                                                                 in1=bis[P, 2:3], op0=ALU.mult, op1=ALU.subtract),
                         reads=['bisN', 'bisL'], writes=['bisT'])
                    K.op('dve', lambda e: e.tensor_scalar(out=mask[P, :], in0=score[P, sb_, :], scalar1=bis[P, 7:8],
                                                          scalar2=None, op0=ALU.is_ge),
                         reads=allsc + ['bisT'], writes=['mask'])
                    for g in range(8):
                        b = K.bank()
                        pT = ps[b][:].bitcast(BF16)
                        for k4 in range(4):
                            kt = g * 4 + k4
                            K.op('pe', lambda e, pT=pT, k4=k4, kt=kt: e.transpose(
                                out=pT[:, k4 * QTS:(k4 + 1) * QTS], in_=mask[P, kt * 128:(kt + 1) * 128],
                                identity=ident[P, 0:QTS]),
                                reads=['mask', 'ident'], writes=[PS(b)], signal=(k4 == 3))
                        src = pT[:, 0:4 * QTS].rearrange("p (k q) -> p k q", q=QTS)
                        dst = maskT[:, g * 4:(g + 1) * 4, qi3 * QTS:(qi3 + 1) * QTS]
                        if g % 2 == 0:
                            K.op('act', lambda e, src=src, dst=dst: e.copy(out=dst, in_=src),
                                 reads=[PS(b)], writes=['mT%d' % qi3])
                        else:
                            K.op('dve', lambda e, src=src, dst=dst: e.tensor_copy(out=dst, in_=src),
                                 reads=[PS(b)], writes=['mT%d' % qi3])

                emit_acc(0)
                for qb in range(NQB):
                    q0 = 2 + qb * QBS
                    for qi3 in range(3):
                        qt = qb * 3 + qi3
                        if qt + 1 < NQT:
                            emit_acc(qt + 1)
                        emit_bis(qt)
                    its = [(h, kt) for h in range(16) for kt in range(32)]

                    def emit_S(i, q0=q0):
                        h, kt = its[i]
                        n = h // 4
                        if kt == 0 and h % 4 == 0:
                            ksl = kvload[0] % 2
                            kvload[0] += 1
                            kvslot[n] = ksl
                            K.dma('sp', knb[:, ksl, :], kT_d[n], 'kn%d' % ksl, writes=['kn%d' % ksl])
                        ksl = kvslot[n]
                        bs_ = SBK[i % 4]
                        es_ = i % 4
                        K.op('pe', lambda e: e.matmul(ps[bs_][:, 0:QBS], lhsT=knb[:, ksl, kt * 128:(kt + 1) * 128],
                                                      rhs=qT[:, h, q0:q0 + QBS], start=True, stop=True),
                             reads=['kn%d' % ksl, 'q_%d' % h], writes=[PS(bs_)])
                        K.op('act', lambda e: e.activation(out=Eb[:, es_, :], in_=ps[bs_][:, 0:QBS], func=AF.Exp,
                                                           scale=128.0 ** -0.5),
                             reads=[PS(bs_)], writes=['E%d' % es_])
                        K.op('dve', lambda e: e.tensor_tensor(out=Em[:, es_, :], in0=Eb[:, es_, :],
                                                              in1=maskT[:, kt, :], op=ALU.mult),
                             reads=['E%d' % es_, 'mT0', 'mT1', 'mT2'], writes=['Em%d' % es_])
                    for i in range(LOOK):
                        emit_S(i)
                    for i in range(len(its)):
                        if i + LOOK < len(its):
                            emit_S(i + LOOK)
                        h, kt = its[i]
                        n = h // 4
                        ksl = kvslot[n]
                        es_ = i % 4
                        bo, bz = (0, 1) if h % 2 == 0 else (2, 3)
                        if kt == 0 and h % 4 == 0:
                            K.dma('sp', vnb[:, :, :], v_d[n], 'vn0', writes=['vn0'])
                        K.op('pe', lambda e, bo=bo, kt=kt, ksl=ksl, es_=es_: e.matmul(
                            ps[bo][:, 0:QBS], lhsT=vnb[:, kt, :], rhs=Em[:, es_, :],
                            start=(kt == 0), stop=(kt == 31)),
                            reads=['vn0', 'Em%d' % es_], writes=[PS(bo)], signal=False)
                        K.op('pe', lambda e, bz=bz, kt=kt, es_=es_: e.matmul(
                            ps[bz][:, 0:QBS], lhsT=onesb[:, :], rhs=Em[:, es_, :],
                            start=(kt == 0), stop=(kt == 31)),
                            reads=['onesb', 'Em%d' % es_], writes=[PS(bz)], signal=True)
                        if kt == 31:
                            K.op('dve', lambda e, bz=bz: e.reciprocal(out=rz[:, :], in_=ps[bz][:, 0:QBS]),
                                 reads=[PS(bz)], writes=['rz'])
                            ys = h % 2
                            K.op('dve', lambda e, bo=bo, ys=ys: e.tensor_tensor(out=ybs[:, ys, :], in0=ps[bo][:, 0:QBS],
                                                                                in1=rz[:, :], op=ALU.mult),
                                 reads=[PS(bo), 'rz'], writes=['ybs%d' % ys])
                            yb_toks.append(K.dma('sp', yb_d[h][:, q0:q0 + QBS], ybs[:, ys, :], 'ybs%d' % ys,
                                                 reads=['ybs%d' % ys]))
                K.barrier()

        K.barrier()
        u2_d = nc.dram_tensor("u2_s", [16, 128, NT], BF16, kind="Internal").ap()
        with ExitStack() as pQ:
            QT = sb("QT", [128, 16, NT], BF16, pQ)
            with ExitStack() as pm:
                u1T = sb("u1Tb", [128, 16, NT], BF16, pm)
                build_u(u1T, pm, SC_M, SH_M, xo_d, "b")
                sg = sb("sg", [128, 2, 343], F32, pm)
                sgi = [0]

                def gated(c, banksA, banksG, accumulate):
                    for ti, (a, z) in enumerate(TT):
                        n = z - a
                        s_ = sgi[0] % 2
                        sgi[0] += 1
                        K.op('act', lambda e, b=banksG[ti], s_=s_, n=n: e.activation(out=sg[:, s_, 0:n],
                                                                                   in_=ps[b][:, 0:n], func=AF.Sigmoid),
                             reads=[PS(banksG[ti])], writes=['sg%d' % s_])
                        if not accumulate:
                            K.op('dve', lambda e, b=banksA[ti], s_=s_, n=n, a=a, z=z: e.tensor_tensor(
                                out=QT[:, c, a:z], in0=ps[b][:, 0:n], in1=sg[:, s_, 0:n], op=ALU.mult),
                                reads=[PS(banksA[ti]), 'sg%d' % s_], writes=['m_%d' % c])
                        else:
                            K.op('dve', lambda e, b=banksA[ti], s_=s_, n=n: e.tensor_tensor(
                                out=sg[:, s_, 0:n], in0=ps[b][:, 0:n], in1=sg[:, s_, 0:n], op=ALU.mult),
                                reads=[PS(banksA[ti]), 'sg%d' % s_], writes=['sg%d' % s_])
                            K.op('dve', lambda e, s_=s_, n=n, a=a, z=z: e.tensor_tensor(
                                out=QT[:, c, a:z], in0=sg[:, s_, 0:n], in1=QT[:, c, a:z], op=ALU.add),
                                reads=['sg%d' % s_, 'm_%d' % c], writes=['m_%d' % c])

                with ExitStack() as pm2:
                    ybT = sb("ybT", [128, 16, NT], BF16, pm2)
                    K.wait('sp', yb_toks)
                    for h in range(16):
                        K.op('pool', lambda e, h=h: e.memset(ybT[:, h, 0:2], 0.0), writes=['yb_%d' % h])
                    for h in range(16):
                        K.dma('sp', ybT[:, h, 2:NT], yb_d[h][:, 2:NT], 'ybl', writes=['yb_%d' % h])
                    for h in range(16):
                        K.res['yb_%d' % h][0] = ('ybl', K.cnt['ybl'])
                    for c in range(16):
                        bA = dense(ybT, lambda kc: 'yb_%d' % kc)
                        bG = dense(u1T, lambda kc: 'u_%d' % kc)
                        gated(c, bA, bG, False)
                    K.barrier()
                with ExitStack() as pm1:
                    yaT = sb("yaT", [128, 16, NT], BF16, pm1)
                    with ExitStack() as p2b:
                        tb_ = sb("tbuf", [128, 2, NT + 2], F32, p2b)
                        hs = sb("hs", [128, 2, 343], F32, p2b)
                        Bs = sb("Bs", [128, 2, NT], F32, p2b)
                        yc = sb("yc", [128, 2, NT], F32, p2b)
                        K.op('dve', lambda e: e.memset(tb_[:, :, 0:2], 0.0), writes=['t0', 't1'])
                        hi = 0
                        for c in range(16):
                            ts_ = c % 2
                            slC = next_w()
                            slH = next_w()
                            slB = next_w()
                            for ti, (a, z) in enumerate(TT):
                                n = z - a
                                bC = K.bank()
                                mm_group(bC, 128, n, [(wring[:, slC, kc, :], u1T[:, kc, a:z]) for kc in range(16)],
                                         [['w%d' % slC, 'u_%d' % kc] for kc in range(16)])
                                bH = K.bank()
                                mm_group(bH, 128, n, [(wring[:, slH, kc, :], u1T[:, kc, a:z]) for kc in range(16)],
                                         [['w%d' % slH, 'u_%d' % kc] for kc in range(16)])
                                h_ = hi % 2
                                hi += 1
                                K.op('act', lambda e, bH=bH, h_=h_, n=n: e.copy(out=hs[:, h_, 0:n], in_=ps[bH][:, 0:n]),
                                     reads=[PS(bH)], writes=['hs%d' % h_])
                                K.op('dve', lambda e, bC=bC, h_=h_, n=n, a=a, z=z, ts_=ts_: e.tensor_tensor(
                                    out=tb_[:, ts_, 2 + a:2 + z], in0=ps[bC][:, 0:n], in1=hs[:, h_, 0:n], op=ALU.mult),
                                    reads=[PS(bC), 'hs%d' % h_], writes=['t%d' % ts_])
                            for ti, (a, z) in enumerate(TT):
                                n = z - a
                                bB = K.bank()
                                mm_group(bB, 128, n, [(wring[:, slB, kc, :], u1T[:, kc, a:z]) for kc in range(16)],
                                         [['w%d' % slB, 'u_%d' % kc] for kc in range(16)])
                                K.op('act', lambda e, bB=bB, n=n, a=a, z=z, ts_=ts_: e.copy(out=Bs[:, ts_, a:z],
                                                                                          in_=ps[bB][:, 0:n]),
                                     reads=[PS(bB)], writes=['Bs%d' % ts_])
                            K.op('dve', lambda e, ts_=ts_: e.tensor_scalar(out=tb_[:, ts_, 2:2 + HALO],
                                                                           in0=tb_[:, ts_, 2:2 + HALO],
                                                                           scalar1=halom[:, 0:1], scalar2=None,
                                                                           op0=ALU.mult),
                                 reads=['t%d' % ts_, 'halom'], writes=['t%d' % ts_])
                            K.op('dve', lambda e, ts_=ts_, c=c: e.tensor_scalar(out=yc[:, ts_, :],
                                                                                in0=tb_[:, ts_, 2:NT + 2],
                                                                                scalar1=cva[:, c, 2:3], scalar2=None,
                                                                                op0=ALU.mult),
                                 reads=['t%d' % ts_, 'cva'], writes=['yc%d' % ts_])
                            for jj in (1, 0):
                                K.op('dve', lambda e, ts_=ts_, c=c, jj=jj: e.scalar_tensor_tensor(
                                    out=yc[:, ts_, :], in0=tb_[:, ts_, jj:NT + jj], scalar=cva[:, c, jj:jj + 1],
                                    in1=yc[:, ts_, :], op0=ALU.mult, op1=ALU.add),
                                    reads=['t%d' % ts_, 'cva', 'yc%d' % ts_], writes=['yc%d' % ts_])
                            K.op('pool', lambda e, ts_=ts_, c=c: e.tensor_tensor(out=yaT[:, c, :], in0=Bs[:, ts_, :],
                                                                                 in1=yc[:, ts_, :], op=ALU.mult),
                                 reads=['Bs%d' % ts_, 'yc%d' % ts_], writes=['ya_%d' % c])
                        K.barrier()
                    for c in range(16):
                        bA = dense(yaT, lambda kc: 'ya_%d' % kc)
                        bG = dense(u1T, lambda kc: 'u_%d' % kc)
                        gated(c, bA, bG, True)
                    K.barrier()

            def layernorm(zT, ntile, cols, gi, bi, scope, tag, post):
                sq = sb("sq" + tag, [128, 2, 343], F32, scope)
                mean = sb("mean" + tag, [128, 343], F32, scope)
                rstd = sb("rstd" + tag, [128, 343], F32, scope)
                tmp = sb("tmp" + tag, [128, 2, 343], F32, scope)
                for ti, (a, z) in ntile:
                    n = z - a
                    b1 = K.bank()
                    mm_group(b1, 128, n, [(onesf[:, :], zT[:, c, a:z]) for c in range(16)],
                             [['onesf', 'z_%d' % c] for c in range(16)])
                    b2 = K.bank()
                    for c in range(16):
                        s_ = c % 2
                        K.op('act', lambda e, s_=s_, c=c, n=n, a=a, z=z: e.activation(out=sq[:, s_, 0:n],
                                                                                    in_=zT[:, c, a:z], func=AF.Square),
                             reads=['z_%d' % c], writes=['sq%d' % s_])
                        K.op('pe', lambda e, s_=s_, c=c, n=n: e.matmul(ps[b2][:, 0:n], lhsT=onesf[:, :],
                                                                       rhs=sq[:, s_, 0:n], start=(c == 0),
                                                                       stop=(c == 15)),
                             reads=['onesf', 'sq%d' % s_], writes=[PS(b2)], signal=True)
                    K.op('dve', lambda e, n=n: e.tensor_scalar(out=mean[:, 0:n], in0=ps[b1][:, 0:n], scalar1=1.0 / D,
                                                               scalar2=None, op0=ALU.mult),
                         reads=[PS(b1)], writes=['mean'])
                    K.op('dve', lambda e, n=n: e.tensor_tensor(out=rstd[:, 0:n], in0=mean[:, 0:n], in1=mean[:, 0:n],
                                                               op=ALU.mult), reads=['mean'], writes=['rstd'])
                    K.op('dve', lambda e, n=n: e.scalar_tensor_tensor(out=rstd[:, 0:n], in0=ps[b2][:, 0:n],
                                                                      scalar=1.0 / D, in1=rstd[:, 0:n],
                                                                      op0=ALU.mult, op1=ALU.subtract),
                         reads=[PS(b2), 'rstd'], writes=['rstd'])
                    K.op('act', lambda e, n=n: e.activation(out=rstd[:, 0:n], in_=rstd[:, 0:n], func=AF.Sqrt,
                                                            bias=epst[:, 0:1], scale=1.0),
                         reads=['rstd', 'epst'], writes=['rstd'])
                    K.op('dve', lambda e, n=n: e.reciprocal(out=rstd[:, 0:n], in_=rstd[:, 0:n]),
                         reads=['rstd'], writes=['rstd'])
                    for c in range(16):
                        s_ = c % 2
                        eng = 'dve' if c % 2 == 0 else 'pool'
                        K.op(eng, lambda e, s_=s_, c=c, n=n, a=a, z=z: e.tensor_tensor(
                            out=tmp[:, s_, 0:n], in0=zT[:, c, a:z], in1=mean[:, 0:n], op=ALU.subtract),
                            reads=['z_%d' % c, 'mean'], writes=['tmp%d' % s_])
                        K.op(eng, lambda e, s_=s_, n=n: e.tensor_tensor(
                            out=tmp[:, s_, 0:n], in0=tmp[:, s_, 0:n], in1=rstd[:, 0:n], op=ALU.mult),
                            reads=['tmp%d' % s_, 'rstd'], writes=['tmp%d' % s_])
                        K.op('dve', lambda e, s_=s_, c=c, n=n, a=a, z=z: e.tensor_scalar(
                            out=zT[:, c, a:z], in0=tmp[:, s_, 0:n], scalar1=lnp[:, gi, c:c + 1],
                            scalar2=lnp[:, bi, c:c + 1], op0=ALU.mult, op1=ALU.add),
                            reads=['tmp%d' % s_, 'lnp'], writes=['z_%d' % c])
                        post(c, ti, a, z)

            x1_toks = []
            u2_toks = []
            with ExitStack() as pl:
                zT = sb("zT", [128, 16, NT], F32, pl)
                xr2 = sb("xr2", [128, 2, NT], F32, pl)
                u2s = sb("u2s", [128, 2, 343], BF16, pl)
                for c in range(16):
                    sl = c % 2
                    K.dma('sp', xr2[:, sl, :], xo_d[c], 'xq%d' % sl, writes=['xq%d' % sl])
                    K.op('act', lambda e, sl=sl: e.mul(out=xr2[:, sl, :], in_=xr2[:, sl, :], mul=ALPHA),
                         reads=['xq%d' % sl], writes=['xq%d' % sl])
                    banks = dense(QT, lambda kc: 'm_%d' % kc)
                    for ti, (a, z) in enumerate(TT):
                        n = z - a
                        K.op('dve', lambda e, b=banks[ti], n=n, a=a, z=z, c=c, sl=sl: e.scalar_tensor_tensor(
                            out=zT[:, c, a:z], in0=ps[b][:, 0:n], scalar=modP[:, G_M + c:G_M + c + 1],
                            in1=xr2[:, sl, a:z], op0=ALU.mult, op1=ALU.add),
                            reads=[PS(banks[ti]), 'modP', 'xq%d' % sl], writes=['z_%d' % c])
                u2i = [0]

                def post1(c, ti, a, z):
                    n = z - a
                    s_ = u2i[0] % 2
                    u2i[0] += 1
                    K.op('act', lambda e: e.activation(out=u2s[:, s_, 0:n], in_=zT[:, c, a:z], func=AF.Identity,
                                                       bias=modT[:, SH_F + c:SH_F + c + 1],
                                                       scale=modP[:, SC_F + c:SC_F + c + 1]),
                         reads=['z_%d' % c, 'modT', 'modP'], writes=['u2s%d' % s_])
                    u2_toks.append(K.dma('sp', u2_d[c][:, a:z], u2s[:, s_, 0:n], 'u2s%d' % s_, reads=['u2s%d' % s_]))
                    x1_toks.append(K.dma('sp', x1_d[c][:, a:z], zT[:, c, a:z], 'x1st', reads=['z_%d' % c]))
                layernorm(zT, list(enumerate(TT)), NT, 0, 1, pl, "1", post1)
                K.barrier()
            K.barrier()

        with ExitStack() as pf:
            gT = sb("gT", [128, NFC, NT], BF16, pf)
            with ExitStack() as pu:
                u2T = sb("u2T", [128, 16, NT], BF16, pu)
                araw = sb("araw", [128, 2, NT + 2], F32, pu)
                ac = sb("ac", [128, 2, NT], F32, pu)
                K.wait('sp', u2_toks)
                for c in range(16):
                    K.dma('sp', u2T[:, c, :], u2_d[c], 'u2l', writes=['u2_%d' % c])
                for c in range(16):
                    K.res['u2_%d' % c][0] = ('u2l', K.cnt['u2l'])
                K.op('dve', lambda e: e.memset(araw[:, :, 0:2], 0.0), writes=['ar0', 'ar1'])
                for j in range(NFC):
                    as_ = j % 2
                    bA = dense(u2T, lambda kc: 'u2_%d' % kc)
                    for ti, (a, z) in enumerate(TT):
                        K.op('act', lambda e, b=bA[ti], a=a, z=z, as_=as_: e.copy(out=araw[:, as_, 2 + a:2 + z],
                                                                                in_=ps[b][:, 0:z - a]),
                             reads=[PS(bA[ti])], writes=['ar%d' % as_])
                    K.op('dve', lambda e, as_=as_: e.tensor_scalar(out=araw[:, as_, 2:2 + HALO],
                                                                   in0=araw[:, as_, 2:2 + HALO],
                                                                   scalar1=halom[:, 0:1], scalar2=None, op0=ALU.mult),
                         reads=['ar%d' % as_, 'halom'], writes=['ar%d' % as_])
                    K.op('pool', lambda e, as_=as_, j=j: e.tensor_scalar(out=ac[:, as_, :], in0=araw[:, as_, 2:NT + 2],
                                                                         scalar1=cvf[:, j, 2:3], scalar2=None,
                                                                         op0=ALU.mult),
                         reads=['ar%d' % as_, 'cvf'], writes=['ac%d' % as_])
                    for jj in (1, 0):
                        K.op('dve', lambda e, as_=as_, j=j, jj=jj: e.scalar_tensor_tensor(
                            out=ac[:, as_, :], in0=araw[:, as_, jj:NT + jj], scalar=cvf[:, j, jj:jj + 1],
                            in1=ac[:, as_, :], op0=ALU.mult, op1=ALU.add),
                            reads=['ar%d' % as_, 'cvf', 'ac%d' % as_], writes=['ac%d' % as_])
                    K.op('act', lambda e, as_=as_: e.activation(out=ac[:, as_, :], in_=ac[:, as_, :],
                                                                func=AF.Gelu_apprx_tanh),
                         reads=['ac%d' % as_], writes=['ac%d' % as_])
                    bB = dense(u2T, lambda kc: 'u2_%d' % kc)
                    for ti, (a, z) in enumerate(TT):
                        K.op('dve', lambda e, b=bB[ti], a=a, z=z, as_=as_, j=j: e.tensor_tensor(
                            out=gT[:, j, a:z], in0=ps[b][:, 0:z - a], in1=ac[:, as_, a:z], op=ALU.mult),
                            reads=[PS(bB[ti]), 'ac%d' % as_], writes=['g_%d' % j])
                K.barrier()
            with ExitStack() as pd:
                z2 = sb("z2", [128, 16, 343], F32, pd)
                x1r = sb("x1r", [128, 2, 343], F32, pd)
                K.wait('sp', x1_toks)
                out_toks = []
                for ti, (a, z) in enumerate(TT):
                    n = z - a
                    for c in range(16):
                        sl = c % 2
                        K.dma('sp', x1r[:, sl, 0:n], x1_d[c][:, a:z], 'x1r%d' % sl, writes=['x1r%d' % sl])
                        K.op('act', lambda e, sl=sl, n=n: e.mul(out=x1r[:, sl, 0:n], in_=x1r[:, sl, 0:n], mul=ALPHA),
                             reads=['x1r%d' % sl], writes=['x1r%d' % sl])
                        b = K.bank()
                        pairs, rl = [], []
                        for part in range(3):
                            slot = next_w()
                            for kc in range(min(16, NFC - part * 16)):
                                pairs.append((wring[:, slot, kc, :], gT[:, part * 16 + kc, a:z]))
                                rl.append(['w%d' % slot, 'g_%d' % (part * 16 + kc)])
                        mm_group(b, 128, n, pairs, rl)
                        K.op('dve', lambda e, b=b, n=n, c=c, sl=sl: e.scalar_tensor_tensor(
                            out=z2[:, c, 0:n], in0=ps[b][:, 0:n], scalar=modP[:, G_F + c:G_F + c + 1],
                            in1=x1r[:, sl, 0:n], op0=ALU.mult, op1=ALU.add),
                            reads=[PS(b), 'modP', 'x1r%d' % sl], writes=['z_%d' % c])

                    def post2(c, ti_, a_, z_):
                        if c == 15:
                            out_toks.append(K.dma('sp', out_d[:, :, a:z].rearrange("c p t -> p c t"), z2[:, :, 0:n],
                                                  'outst', reads=['z_%d' % cc for cc in range(16)]))
                    with ExitStack() as pln:
                        layernorm(z2, [(ti, (0, n))], n, 2, 3, pln, "2_%d" % ti, post2)
                        K.barrier()
                K.wait('sp', [('outst', K.cnt['outst'])])
                if debug:
                    K.wait('sp', [('dbg', K.cnt['dbg'])])
    return nc


_NC_CACHE = {}


def _prep(inputs):
    f32 = np.float32
    x = np.asarray(inputs['x'], f32)
    c = np.asarray(inputs['c'], f32)
    W = {k: np.asarray(inputs[k][0], f32) for k in ('w_cond', 'w_in', 'w_a', 'w_b', 'w_o', 'w_up', 'w_down')}
    items, seq = stream_plan()

    def fm(Wm, col0, ncols, kc0=0, nkc=16):
        blk = Wm[kc0 * 128:(kc0 + nkc) * 128, col0:col0 + ncols].reshape(nkc, 128, ncols).transpose(1, 0, 2)
        if nkc < 16:
            blk = np.concatenate([blk, np.zeros((128, 16 - nkc, ncols), f32)], axis=1)
        return blk
    ws = np.empty((len(items), 128, 16, 128), f32)
    for i, (nm, col0, kc0, nkc) in enumerate(items):
        ws[i] = fm(W[nm], col0, 128, kc0, nkc)
    wcond = np.ascontiguousarray(W['w_cond'].reshape(16, 128, 24, 512).transpose(2, 1, 0, 3))
    bcond = np.asarray(inputs['b_cond'], f32).reshape(1, 12288)
    wk = np.ascontiguousarray(fm(W['w_in'], O_K, 512))
    wv = np.ascontiguousarray(fm(W['w_in'], O_V, 512))
    wki = np.ascontiguousarray(fm(W['w_in'], O_KI, 64))
    wwi = np.ascontiguousarray(fm(W['w_in'], O_WI, 16))

    def pv(v):
        return np.ascontiguousarray(np.asarray(v, f32).reshape(-1, 128).T)
    cva = np.ascontiguousarray(np.stack([pv(inputs['conv_a'][0][j]) for j in range(3)], axis=2))
    cvf = np.ascontiguousarray(np.stack([pv(inputs['conv_f'][0][j]) for j in range(3)], axis=2))
    lnp = np.ascontiguousarray(np.stack([pv(inputs['ln1_g'][0]), pv(inputs['ln1_b'][0]),
                                         pv(inputs['ln2_g'][0]), pv(inputs['ln2_b'][0])], axis=1))
    kng = np.ascontiguousarray(np.broadcast_to(np.stack([np.asarray(inputs['idx_kn_g'][0], f32),
                                                         np.asarray(inputs['idx_kn_b'][0], f32)])[None], (128, 2, 64)))
    cidx = np.ascontiguousarray(np.broadcast_to((np.arange(64, dtype=f32) * 64.0)[None], (128, 64)))
    ident = np.eye(128, dtype=f32)
    maps = []
    for core in range(8):
        b, q = core // 4, core % 4
        s0 = q * 1024
        g = s0 - HALO + np.arange(NT)
        xo = np.zeros((NT, D), f32)
        valid = g >= 0
        xo[valid] = x[b, g[valid]]
        xo = np.ascontiguousarray(xo.T.reshape(16, 128, NT))
        xs = np.ascontiguousarray(x[b].T.reshape(16, 128, 16, 256).transpose(2, 1, 0, 3))
        cT = np.ascontiguousarray(c[b].reshape(16, 128).T)
        gq = g[2:2 + NQT * QTS].reshape(NQT, QTS)
        lim = np.clip((np.floor_divide(gq, 64) + 1) * 64, 64, S).astype(f32)
        qlim = np.full((128, NQT), float(S), f32)
        qlim[:QTS, :] = lim.T
        halom = np.full((128, 1), 0.0 if q == 0 else 1.0, f32)
        m = dict(xo=xo, xs=xs, cT=cT, wcond=wcond, bcond=bcond, ws=ws, wk=wk, wv=wv, wki=wki, wwi=wwi,
                 cva=cva, cvf=cvf, lnp=lnp, kng=kng, qlim=qlim, halom=halom, cidx=cidx, ident=ident)
        maps.append({"i_" + k_: v_ for k_, v_ in m.items()})
    return maps


def _assemble(results):
    out = np.empty((2, S, D), np.float32)
    for core in range(8):
        b, q = core // 4, core % 4
        o = np.asarray(results[core]["out"], np.float32)
        out[b, q * 1024:(q + 1) * 1024, :] = o.reshape(D, NT)[:, HALO:].T
    return out


def kernel(**inputs):
    if 'nc' not in _NC_CACHE:
        _NC_CACHE['nc'] = build_nc(False)
    maps = _prep(inputs)
    res = run_bass_kernel_spmd(_NC_CACHE['nc'], maps, core_ids=list(range(8)))
    return _assemble(res.results)
```

```python
import numpy as np
import ml_dtypes
from contextlib import ExitStack
import concourse.bass as bass
import concourse.mybir as mybir
from concourse.bass_utils import run_bass_kernel_spmd

F32 = mybir.dt.float32
BF16 = mybir.dt.bfloat16
AF = mybir.ActivationFunctionType
ALU = mybir.AluOpType
AX = mybir.AxisListType

D = 2048
S = 4096
NT = 1028
HALO = 4
TT = [(0, 343), (343, 686), (686, 1028)]
NQT = 9
QTS = 114
NQB = 3
QBS = 342
DFF = 5632
NFC = 44
ALPHA = 2.0 ** 0.25
LN_EPS = 1e-5
IDX_W_SCALE = 1024.0 ** -0.5
TOPK = 256.0
BIG = 1.0e6
NBIS = 16
WR = 8
O_B, O_C, O_H, O_Q, O_K, O_V, O_QI, O_KI, O_WI, O_GA, O_GB = 0, 2048, 4096, 6144, 8192, 8704, 9216, 10240, 10304, 10320, 12368


def stream_plan():
    items, seq = [], []

    def add(*d):
        items.append(d)
        seq.append(len(items) - 1)
    for h in range(16):
        add('w_in', O_Q + h * 128, 0, 16)
    for c in range(8):
        add('w_in', O_QI + c * 128, 0, 16)
    for c in range(16):
        add('w_b', c * 128, 0, 16)
        add('w_in', O_GB + c * 128, 0, 16)
    for c in range(16):
        add('w_in', O_C + c * 128, 0, 16)
        add('w_in', O_H + c * 128, 0, 16)
        add('w_in', O_B + c * 128, 0, 16)
    for c in range(16):
        add('w_a', c * 128, 0, 16)
        add('w_in', O_GA + c * 128, 0, 16)
    for c in range(16):
        add('w_o', c * 128, 0, 16)
    for j in range(NFC):
        add('w_up', j * 128, 0, 16)
        add('w_up', DFF + j * 128, 0, 16)
    base = len(items)
    for c in range(16):
        for part in range(3):
            items.append(('w_down', c * 128, part * 16, min(16, NFC - part * 16)))
    for tt in range(3):
        for c in range(16):
            for part in range(3):
                seq.append(base + c * 3 + part)
    return items, seq


class KB:
    def __init__(self, nc, es):
        self.nc, self.es = nc, es
        self.engs = {'pe': nc.tensor, 'act': nc.scalar, 'dve': nc.vector, 'pool': nc.gpsimd, 'sp': nc.sync}
        self.sems = {k: es.enter_context(nc.semaphore('s_' + k)) for k in self.engs}
        self.cnt = {k: 0 for k in self.engs}
        self.waited = {}
        self.res = {}
        self.pbank = 0

    def dsem(self, name):
        if name not in self.sems:
            self.sems[name] = self.es.enter_context(self.nc.semaphore('d_' + name))
            self.cnt[name] = 0
        return self.sems[name]

    def _deps(self, reads, writes):
        deps = []
        for r in reads:
            st = self.res.get(r)
            if st and st[0]:
                deps.append(st[0])
        for w in writes:
            st = self.res.get(w)
            if st:
                if st[0]:
                    deps.append(st[0])
                deps.extend(st[1])
        return deps

    def wait(self, eng, deps):
        for (key, val) in deps:
            if key == eng and eng == 'pe':
                continue
            if self.waited.get((eng, key), 0) >= val:
                continue
            self.engs[eng].wait_ge(self.sems[key], val)
            self.waited[(eng, key)] = val

    def _upd(self, tok, reads, writes):
        for r in reads:
            st = self.res.setdefault(r, [None, []])
            st[1].append(tok)
        for w in writes:
            self.res[w] = [tok, []]

    def op(self, eng, fn, reads=(), writes=(), signal=True):
        self.wait(eng, self._deps(reads, writes))
        ins = fn(self.engs[eng])
        if signal:
            self.cnt[eng] += 1
            ins.then_inc(self.sems[eng], 1)
            tok = (eng, self.cnt[eng])
        else:
            tok = (eng, self.cnt[eng] + 1)
        self._upd(tok, reads, writes)
        return tok

    def dma(self, queue, out, in_, sem, reads=(), writes=()):
        self.dsem(sem)
        self.wait(queue, self._deps(reads, writes))
        ins = self.engs[queue].dma_start(out=out, in_=in_)
        self.cnt[sem] += 16
        ins.then_inc(self.sems[sem], 16)
        tok = (sem, self.cnt[sem])
        self._upd(tok, reads, writes)
        return tok

    def barrier(self):
        toks = [(k, v) for k, v in self.cnt.items() if v > 0]
        for e in ('pe', 'act', 'dve', 'pool', 'sp'):
            self.wait(e, toks)

    def bank(self):
        b = self.pbank
        self.pbank = (self.pbank + 1) % 8
        return b


def build_nc(debug=False):
    nc = bass.Bass("TRN2", target_bir_lowering=False)
    items, seq = stream_plan()
    NI = len(items)

    def din(name, shape, dt=F32):
        return nc.dram_tensor("i_" + name, shape, dt, kind="ExternalInput").ap()
    xo_d = din("xo", [16, 128, NT])
    xs_d = din("xs", [16, 128, 16, 256])
    cT_d = din("cT", [128, 16])
    wc_d = din("wcond", [24, 128, 16, 512])
    bc_d = din("bcond", [1, 12288])
    ws_d = din("ws", [NI, 128, 16, 128])
    wk_d = din("wk", [128, 16, 512])
    wv_d = din("wv", [128, 16, 512])
    wki_d = din("wki", [128, 16, 64])
    wwi_d = din("wwi", [128, 16, 16])
    cva_d = din("cva", [128, 16, 3])
    cvf_d = din("cvf", [128, NFC, 3])
    lnp_d = din("lnp", [128, 4, 16])
    kng_d = din("kng", [128, 2, 64])
    qlim_d = din("qlim", [128, NQT])
    halo_d = din("halom", [128, 1])
    cidx_d = din("cidx", [128, 64])
    ident_d = din("ident", [128, 128])
    okind = "ExternalOutput" if debug else "Internal"
    out_d = nc.dram_tensor("out", [16, 128, NT], F32, kind="ExternalOutput").ap()
    kT_d = nc.dram_tensor("kT_s", [4, 128, S], BF16, kind=okind).ap()
    v_d = nc.dram_tensor("v_s", [4, 128, 32, 128], BF16, kind=okind).ap()
    yb_d = nc.dram_tensor("yb_s", [16, 128, NT], BF16, kind=okind).ap()
    x1_d = nc.dram_tensor("x1_s", [16, 128, NT], F32, kind=okind).ap()
    mod_o = nc.dram_tensor("mod_o", [128, 96], F32, kind=okind).ap()

    with ExitStack() as es:
        K = KB(nc, es)

        def sb(name, shape, dt=F32, scope=es):
            return scope.enter_context(nc.sbuf_tensor(name, shape, dt))
        ps = [es.enter_context(nc.psum_tensor("ps%d" % i, [128, 512], F32)) for i in range(8)]

        def PS(b):
            return "ps%d" % b

        ident = sb("ident", [128, 128], BF16)
        identf = sb("identf", [128, 128], F32)
        onesb = sb("onesb", [128, 128], BF16)
        onesf = sb("onesf", [128, 128], F32)
        one11 = sb("one11", [1, 2], F32)
        epst = sb("epst", [128, 1], F32)
        cva = sb("cva", [128, 16, 3])
        cvf = sb("cvf", [128, NFC, 3])
        lnp = sb("lnp", [128, 4, 16])
        kng = sb("kng", [128, 2, 64])
        qlim = sb("qlim", [128, NQT])
        halom = sb("halom", [128, 1])
        cidx = sb("cidx", [128, 64])
        modT = sb("modT", [128, 96])
        modP = sb("modP", [128, 96])
        wring = sb("wring", [128, WR, 16, 128], BF16)
        kiTE = sb("kiTE", [128, S], BF16)
        kiTO = sb("kiTO", [128, S], BF16)
        for (t, d_) in ((identf, ident_d), (cva, cva_d), (cvf, cvf_d), (lnp, lnp_d), (kng, kng_d), (qlim, qlim_d),
                        (halom, halo_d), (cidx, cidx_d)):
            K.dma('sp', t[:], d_, 'c_' + t.name, writes=[t.name])
        K.op('dve', lambda e: e.tensor_copy(out=ident[:], in_=identf[:]), reads=['identf'], writes=['ident'])
        K.op('dve', lambda e: e.memset(onesb[:], 1.0), writes=['onesb'])
        K.op('dve', lambda e: e.memset(onesf[:], 1.0), writes=['onesf'])
        K.op('dve', lambda e: e.memset(one11[:], 1.0), writes=['one11'])
        K.op('dve', lambda e: e.memset(epst[:], LN_EPS), writes=['epst'])

        wst = {'pos': 0, 'loaded': 0}

        def w_prefetch(upto):
            while wst['loaded'] < min(upto, len(seq)):
                i = wst['loaded']
                slot = i % WR
                K.dma('pool', wring[:, slot, :, :], ws_d[seq[i]], 'w%d' % slot, writes=['w%d' % slot])
                wst['loaded'] += 1

        def next_w():
            i = wst['pos']
            w_prefetch(i + 1)
            wst['pos'] += 1
            w_prefetch(i + WR - 2)
            return i % WR

        def mm_group(bank, M, n, pairs, reads_list, moff=0):
            last = len(pairs) - 1
            for i, (l, r) in enumerate(pairs):
                K.op('pe', lambda e, l=l, r=r, i=i: e.matmul(ps[bank][moff:moff + M, 0:n], lhsT=l, rhs=r,
                                                              start=(i == 0), stop=(i == last)),
                     reads=reads_list[i], writes=[PS(bank)], signal=(i == last))

        with ExitStack() as p0:
            cTs = sb("cTs", [128, 16], F32, p0)
            cact = sb("cact", [128, 16], BF16, p0)
            wcb = sb("wcb", [128, 2, 16, 512], BF16, p0)
            modrow = sb("modrow", [1, 12288], F32, p0)
            brow = sb("brow", [1, 12288], F32, p0)
            K.dma('sp', cTs[:], cT_d, 'c_cTs', writes=['cTs'])
            K.dma('sp', brow[:], bc_d, 'c_brow', writes=['brow'])
            K.op('act', lambda e: e.activation(out=cact[:], in_=cTs[:], func=AF.Silu), reads=['cTs'], writes=['cact'])
            for nb in range(24):
                sl = nb % 2
                K.dma('pool', wcb[:, sl, :, :], wc_d[nb], 'wc%d' % sl, writes=['wc%d' % sl])
                b = K.bank()
                mm_group(b, 1, 512, [(cact[:, kc:kc + 1], wcb[:, sl, kc, :]) for kc in range(16)],
                         [['cact', 'wc%d' % sl]] * 16)
                K.op('dve', lambda e, b=b, nb=nb: e.tensor_tensor(out=modrow[0:1, nb * 512:(nb + 1) * 512],
                                                                   in0=ps[b][0:1, 0:512],
                                                                   in1=brow[0:1, nb * 512:(nb + 1) * 512], op=ALU.add),
                     reads=[PS(b), 'brow'], writes=['modrow'])
            w_prefetch(WR - 2)
            b = K.bank()
            for c in range(96):
                K.op('pe', lambda e, c=c: e.matmul(ps[b][:, c:c + 1], lhsT=modrow[0:1, c * 128:(c + 1) * 128],
                                                   rhs=one11[0:1, 0:1], start=True, stop=True),
                     reads=['modrow', 'one11'], writes=[PS(b)], signal=(c == 95))
            K.op('dve', lambda e: e.tensor_copy(out=modT[:], in_=ps[b][:, 0:96]), reads=[PS(b)], writes=['modT'])
            K.op('dve', lambda e: e.tensor_scalar(out=modP[:], in0=modT[:], scalar1=1.0, scalar2=None, op0=ALU.add),
                 reads=['modT'], writes=['modP'])
            K.barrier()
        if debug:
            K.dma('sp', mod_o, modT[:], 'dbg', reads=['modT'])
        SH_M, SC_M, G_M, SH_F, SC_F, G_F = 0, 16, 32, 48, 64, 80

        def modulate(eng_i, out_ap, in_ap, kc, scb, shb, reads, writes):
            if eng_i % 2 == 0:
                K.op('dve', lambda e: e.tensor_scalar(out=out_ap, in0=in_ap, scalar1=modP[:, scb + kc:scb + kc + 1],
                                                      scalar2=modT[:, shb + kc:shb + kc + 1], op0=ALU.mult, op1=ALU.add),
                     reads=reads + ['modT', 'modP'], writes=writes)
            else:
                K.op('act', lambda e: e.activation(out=out_ap, in_=in_ap, func=AF.Identity,
                                                   bias=modT[:, shb + kc:shb + kc + 1],
                                                   scale=modP[:, scb + kc:scb + kc + 1]),
                     reads=reads + ['modT', 'modP'], writes=writes)

        store_toks = []
        with ExitStack() as p1:
            wk = sb("wk", [128, 16, 512], BF16, p1)
            wv = sb("wv", [128, 16, 512], BF16, p1)
            wki = sb("wki", [128, 16, 64], BF16, p1)
            xsb = sb("xsb", [128, 2, 16, 256], F32, p1)
            usb = sb("usb", [128, 2, 16, 256], BF16, p1)
            kst = sb("kst", [128, 2, 4, 256], BF16, p1)
            vst = sb("vst", [128, 2, 2, 512], BF16, p1)
            kraw = sb("kraw", [128, 2, 64], F32, p1)
            kn2 = sb("kn2", [128, 2, 2, 128], BF16, p1)
            K.op('dve', lambda e: e.memset(kn2[:], 0.0), writes=['ki0n', 'ki1n'])
            bst = sb("bst", [128, 2, 6], F32, p1)
            bmv = sb("bmv", [128, 2, 2], F32, p1)
            K.dma('pool', wk[:], wk_d, 'c_wk', writes=['wk'])
            K.dma('pool', wv[:], wv_d, 'c_wv', writes=['wv'])
            K.dma('pool', wki[:], wki_d, 'c_wki', writes=['wki'])
            K.dma('sp', xsb[:, 0, :, :], xs_d[0], 'xs0', writes=['xs0'])
            for tb in range(16):
                sl = tb % 2
                if tb + 1 < 16:
                    K.dma('sp', xsb[:, 1 - sl, :, :], xs_d[tb + 1], 'xs%d' % (1 - sl), writes=['xs%d' % (1 - sl)])
                for kc in range(16):
                    modulate(kc, usb[:, sl, kc, :], xsb[:, sl, kc, :], kc, SC_M, SH_M,
                             ['xs%d' % sl], ['us%d_%d' % (sl, kc)])
                for n in range(4):
                    b = K.bank()
                    mm_group(b, 128, 256, [(wk[:, kc, n * 128:(n + 1) * 128], usb[:, sl, kc, :]) for kc in range(16)],
                             [['wk', 'us%d_%d' % (sl, kc)] for kc in range(16)])
                    K.op('act', lambda e, b=b, n=n: e.copy(out=kst[:, sl, n, :], in_=ps[b][:, 0:256]),
                         reads=[PS(b)], writes=['kst%d' % sl])
                store_toks.append(K.dma('sp', kT_d[:, :, tb * 256:(tb + 1) * 256].rearrange("n p t -> p n t"),
                                        kst[:, sl, :, :], 'kst%d' % sl, reads=['kst%d' % sl]))
                for sub in range(2):
                    tsl = slice(sub * 128, (sub + 1) * 128)
                    b = K.bank()
                    mm_group(b, 128, 512, [(usb[:, sl, kc, tsl], wv[:, kc, :]) for kc in range(16)],
                             [['wv', 'us%d_%d' % (sl, kc)] for kc in range(16)])
                    K.op('dve', lambda e, b=b, sub=sub: e.tensor_copy(out=vst[:, sl, sub, :], in_=ps[b][:, 0:512]),
                         reads=[PS(b)], writes=['vst%d' % sl])
                    b = K.bank()
                    mm_group(b, 128, 64, [(usb[:, sl, kc, tsl], wki[:, kc, :]) for kc in range(16)],
                             [['wki', 'us%d_%d' % (sl, kc)] for kc in range(16)])
                    r_ = 'ki%d' % sub
                    K.op('act', lambda e, b=b, sub=sub: e.copy(out=kraw[:, sub, :], in_=ps[b][:, 0:64]),
                         reads=[PS(b)], writes=[r_])
                    K.op('dve', lambda e, sub=sub: e.bn_stats(out=bst[:, sub, :], in_=kraw[:, sub, :]),
                         reads=[r_], writes=[r_ + 's'])
                    K.op('dve', lambda e, sub=sub: e.bn_aggr(out=bmv[:, sub, :], in_=bst[:, sub, :]),
                         reads=[r_ + 's'], writes=[r_ + 'm'])
                    K.op('act', lambda e, sub=sub: e.activation(out=bmv[:, sub, 1:2], in_=bmv[:, sub, 1:2], func=AF.Sqrt,
                                                                bias=epst[:, 0:1], scale=1.0),
                         reads=[r_ + 'm', 'epst'], writes=[r_ + 'm'])
                    K.op('dve', lambda e, sub=sub: e.reciprocal(out=bmv[:, sub, 1:2], in_=bmv[:, sub, 1:2]),
                         reads=[r_ + 'm'], writes=[r_ + 'm'])
                    K.op('dve', lambda e, sub=sub: e.tensor_scalar(out=kraw[:, sub, :], in0=kraw[:, sub, :],
                                                                   scalar1=bmv[:, sub, 0:1], scalar2=bmv[:, sub, 1:2],
                                                                   op0=ALU.subtract, op1=ALU.mult),
                         reads=[r_, r_ + 'm'], writes=[r_])
                    K.op('dve', lambda e, sub=sub: e.tensor_tensor(out=kraw[:, sub, :], in0=kraw[:, sub, :],
                                                                   in1=kng[:, 0, :], op=ALU.mult),
                         reads=[r_, 'kng'], writes=[r_])
                    K.op('dve', lambda e, sub=sub: e.tensor_tensor(out=kn2[:, sub, 0, 0:64], in0=kraw[:, sub, :],
                                                                   in1=kng[:, 1, :], op=ALU.add),
                         reads=[r_, 'kng'], writes=[r_ + 'n'])
                    K.op('dve', lambda e, sub=sub: e.tensor_copy(out=kn2[:, sub, 1, 64:128], in_=kn2[:, sub, 0, 0:64]),
                         reads=[r_ + 'n'], writes=[r_ + 'n'])
                    b = K.bank()
                    pT = ps[b][:].bitcast(BF16)
                    for eo in range(2):
                        K.op('pe', lambda e, pT=pT, sub=sub, eo=eo: e.transpose(out=pT[:, eo * 128:(eo + 1) * 128],
                                                                                in_=kn2[:, sub, eo, :],
                                                                                identity=ident[:, :]),
                             reads=[r_ + 'n', 'ident'], writes=[PS(b)], signal=(eo == 1))
                    t0 = tb * 256 + sub * 128
                    K.op('act', lambda e, pT=pT, t0=t0: e.copy(out=kiTE[:, t0:t0 + 128], in_=pT[:, 0:128]),
                         reads=[PS(b)], writes=['kiT'])
                    K.op('act', lambda e, pT=pT, t0=t0: e.copy(out=kiTO[:, t0:t0 + 128], in_=pT[:, 128:256]),
                         reads=[PS(b)], writes=['kiT'])
                for n in range(4):
                    store_toks.append(K.dma('sp', v_d[n][:, tb * 2:tb * 2 + 2, :], vst[:, sl, :, n * 128:(n + 1) * 128],
                                            'vst%d' % sl, reads=['vst%d' % sl]))
            K.barrier()

        def build_u(uT, scope, scb, shb, src_d, tag):
            xr = sb("xr_" + tag, [128, 2, NT], F32, scope)
            for kc in range(16):
                sl = kc % 2
                K.dma('sp', xr[:, sl, :], src_d[kc], 'xr%d' % sl, writes=['xr%d' % sl])
                modulate(kc, uT[:, kc, :], xr[:, sl, :], kc, scb, shb, ['xr%d' % sl], ['u_%d' % kc])

        def dense(rhsT, rhs_res, nkc=16):
            slot = next_w()
            banks = []
            for (a, z) in TT:
                b = K.bank()
                mm_group(b, 128, z - a, [(wring[:, slot, kc, :], rhsT[:, kc, a:z]) for kc in range(nkc)],
                         [['w%d' % slot, rhs_res(kc)] for kc in range(nkc)])
                banks.append(b)
            return banks

        with ExitStack() as pA:
            qT = sb("qT", [128, 16, NT], BF16, pA)
            qiT = sb("qiT", [128, 8, NT], BF16, pA)
            wtm = sb("wtm", [128, NQT, 16], F32, pA)
            with ExitStack() as p2a:
                u1T = sb("u1T", [128, 16, NT], BF16, p2a)
                wwi = sb("wwi", [128, 16, 16], BF16, p2a)
                K.dma('pool', wwi[:], wwi_d, 'c_wwi', writes=['wwi'])
                build_u(u1T, p2a, SC_M, SH_M, xo_d, "a")
                for h in range(24):
                    banks = dense(u1T, lambda kc: 'u_%d' % kc)
                    for ti, (a, z) in enumerate(TT):
                        b = banks[ti]
                        dst = qT[:, h, a:z] if h < 16 else qiT[:, h - 16, a:z]
                        dres = 'q_%d' % h
                        if (h + ti) % 2 == 0:
                            K.op('act', lambda e, b=b, dst=dst, n=z - a: e.copy(out=dst, in_=ps[b][:, 0:n]),
                                 reads=[PS(b)], writes=[dres])
                        else:
                            K.op('dve', lambda e, b=b, dst=dst, n=z - a: e.tensor_copy(out=dst, in_=ps[b][:, 0:n]),
                                 reads=[PS(b)], writes=[dres])
                for qt in range(NQT):
                    j0 = 2 + qt * QTS
                    b = K.bank()
                    mm_group(b, QTS, 16, [(u1T[:, kc, j0:j0 + QTS], wwi[:, kc, :]) for kc in range(16)],
                             [['wwi', 'u_%d' % kc] for kc in range(16)])
                    K.op('act', lambda e, b=b, qt=qt: e.mul(out=wtm[0:QTS, qt, :], in_=ps[b][0:QTS, 0:16],
                                                            mul=IDX_W_SCALE),
                         reads=[PS(b)], writes=['wtm'])
                K.barrier()

            with ExitStack() as pt:
                score = sb("score", [128, 2, S], F32, pt)
                dg = sb("dg", [128, 16, QTS], BF16, pt)
                rb = sb("rb", [128, 4, 512], BF16, pt)
                mask = sb("mask", [128, S], BF16, pt)
                maskT = sb("maskT", [128, 32, QBS], BF16, pt)
                pen = sb("pen", [128, 2, 64], F32, pt)
                bis = sb("bis", [128, 8], F32, pt)
                nst = sb("nst", [128, NBIS], F32, pt)
                pwn = sb("pwn", [128, NBIS], F32, pt)
                c35 = sb("c35", [128, 1], F32, pt)
                knb = sb("knb", [128, 2, S], BF16, pt)
                vnb = sb("vnb", [128, 32, 128], BF16, pt)
                Eb = sb("Eb", [128, 4, QBS], BF16, pt)
                Em = sb("Em", [128, 4, QBS], BF16, pt)
                rz = sb("rz", [128, QBS], F32, pt)
                ybs = sb("ybs", [128, 2, QBS], BF16, pt)
                for k_ in range(NBIS):
                    K.op('dve', lambda e, k_=k_: e.memset(pwn[:, k_:k_ + 1], -(2.0 ** -(k_ + 1))), writes=['pwn'])
                K.op('dve', lambda e: e.memset(c35[:], float(S) - 2.0 * TOPK + 0.5), writes=['c35'])
                K.wait('sp', store_toks)
                kvload = [0]
                kvslot = {}
                yb_toks = []
                LOOK = 3
                SBK = [4, 5, 6, 7]
                P = slice(0, QTS)
                uctr = [0]

                def emit_acc(qt):
                    sb_ = qt % 2
                    j0 = 2 + qt * QTS
                    K.op('dve', lambda e: e.tensor_scalar(out=pen[P, sb_, :], in0=cidx[P, :],
                                                          scalar1=qlim[P, qt:qt + 1], scalar2=-BIG,
                                                          op0=ALU.is_ge, op1=ALU.mult),
                         reads=['cidx', 'qlim'], writes=['pen%d' % sb_])
                    for h in range(16):
                        K.op('dve', lambda e, h=h: e.tensor_scalar(out=dg[P, h, :], in0=ident[P, 0:QTS],
                                                                   scalar1=wtm[P, qt, h:h + 1], scalar2=None,
                                                                   op0=ALU.mult),
                             reads=['ident', 'wtm'], writes=['dg%d' % h])
                    for kb in range(8):
                        ks = slice(kb * 512, (kb + 1) * 512)
                        sbank = kb % 2
                        rings = {}

                        def emit_D(h):
                            u = uctr[0]
                            uctr[0] += 1
                            bank = 2 + (u % 6)
                            ring = u % 4
                            kz = kiTE if h % 2 == 0 else kiTO
                            K.op('pe', lambda e: e.matmul(ps[bank][P, 0:512], lhsT=qiT[:, h // 2, j0:j0 + QTS],
                                                          rhs=kz[:, ks], start=True, stop=True),
                                 reads=['q_%d' % (16 + h // 2), 'kiT'], writes=[PS(bank)])
                            if h % 8 == 0:
                                K.op('act', lambda e: e.activation(out=rb[P, ring, :], in_=ps[bank][P, 0:512],
                                                                   func=AF.Relu),
                                     reads=[PS(bank)], writes=['rb%d' % ring])
                            else:
                                K.op('dve', lambda e: e.tensor_scalar(out=rb[P, ring, :], in0=ps[bank][P, 0:512],
                                                                      scalar1=0.0, scalar2=None, op0=ALU.max),
                                     reads=[PS(bank)], writes=['rb%d' % ring])
                            rings[h] = ring
                        for h in range(2):
                            emit_D(h)
                        for h in range(16):
                            if h + 2 < 16:
                                emit_D(h + 2)
                            K.op('pe', lambda e, h=h, r_=rings[h]: e.matmul(
                                ps[sbank][P, 0:512], lhsT=dg[P, h, :], rhs=rb[P, r_, :],
                                start=(h == 0), stop=(h == 15)),
                                reads=['dg%d' % h, 'rb%d' % rings[h]], writes=[PS(sbank)], signal=(h == 15))
                        K.op('dve', lambda e: e.tensor_tensor(
                            out=score[P, sb_, ks].rearrange("p (c j) -> p c j", j=64),
                            in0=ps[sbank][P, 0:512].rearrange("p (c j) -> p c j", j=64),
                            in1=pen[P, sb_, kb * 8:(kb + 1) * 8].unsqueeze(2).to_broadcast([QTS, 8, 64]), op=ALU.add),
                            reads=[PS(sbank), 'pen%d' % sb_], writes=['sc%d_%d' % (sb_, kb)])

                def emit_bis(qt):
                    sb_ = qt % 2
                    qi3 = qt % 3
                    allsc = ['sc%d_%d' % (sb_, kb) for kb in range(8)]
                    K.op('dve', lambda e: e.tensor_reduce(out=bis[P, 5:6], in_=score[P, sb_, :], axis=AX.X, op=ALU.max),
                         reads=allsc, writes=['bisA'])
                    K.op('dve', lambda e: e.tensor_reduce(out=bis[P, 6:7], in_=score[P, sb_, 0:256], axis=AX.X,
                                                          op=ALU.min), reads=allsc, writes=['bisB'])
                    K.op('dve', lambda e: e.tensor_scalar(out=bis[P, 6:7], in0=bis[P, 6:7], scalar1=-1000.0,
                                                          scalar2=None, op0=ALU.max), reads=['bisB'], writes=['bisB'])
                    K.op('dve', lambda e: e.scalar_tensor_tensor(out=bis[P, 0:1], in0=bis[P, 5:6], scalar=-0.5,
                                                                 in1=bis[P, 6:7], op0=ALU.mult, op1=ALU.subtract),
                         reads=['bisA', 'bisB'], writes=['bisN'])
                    K.op('dve', lambda e: e.scalar_tensor_tensor(out=bis[P, 0:1], in0=bis[P, 6:7], scalar=0.5,
                                                                 in1=bis[P, 0:1], op0=ALU.mult, op1=ALU.add),
                         reads=['bisB', 'bisN'], writes=['bisN'])
                    K.op('dve', lambda e: e.tensor_tensor(out=bis[P, 1:2], in0=bis[P, 5:6], in1=bis[P, 6:7],
                                                          op=ALU.subtract), reads=['bisA', 'bisB'], writes=['bisH'])
                    K.op('dve', lambda e: e.tensor_scalar(out=bis[P, 1:2], in0=bis[P, 1:2], scalar1=0.5,
                                                          scalar2=1e-3, op0=ALU.mult, op1=ALU.add),
                         reads=['bisH'], writes=['bisH'])
                    K.op('dve', lambda e: e.tensor_scalar(out=nst[P, :], in0=pwn[P, :], scalar1=bis[P, 1:2],
                                                          scalar2=None, op0=ALU.mult),
                         reads=['bisH', 'pwn'], writes=['nst'])
                    K.op('dve', lambda e: e.tensor_scalar(out=bis[P, 2:3], in0=bis[P, 1:2],
                                                          scalar1=2.0 ** -NBIS, scalar2=None, op0=ALU.mult),
                         reads=['bisH'], writes=['bisL'])
                    for it in range(NBIS):
                        K.op('act', lambda e: e.activation(out=mask[P, :], in_=score[P, sb_, :], func=AF.Sign,
                                                           bias=bis[P, 0:1], scale=1.0, accum_out=bis[P, 3:4]),
                             reads=allsc + ['bisN'], writes=['mask', 'bisC'])
                        K.op('act', lambda e: e.activation(out=bis[P, 4:5], in_=bis[P, 3:4], func=AF.Sign,
                                                           bias=c35[P, 0:1], scale=1.0),
                             reads=['bisC', 'c35'], writes=['bisS'])
                        K.op('act', lambda e, it=it: e.activation(out=bis[P, 0:1], in_=bis[P, 4:5],
                                                                  func=AF.Identity, bias=bis[P, 0:1],
                                                                  scale=nst[P, it:it + 1]),
                             reads=['bisS', 'nst', 'bisN'], writes=['bisN'])

                def emit_fin(qt):
                    sb_ = qt % 2
                    qi3 = qt % 3
                    allsc = ['sc%d_%d' % (sb_, kb) for kb in range(8)]
                    K.op('dve', lambda e: e.scalar_tensor_tensor(out=bis[P, 7:8], in0=bis[P, 0:1], scalar=-1.0,
                                                                 in1=bis[P, 2:3], op0=ALU.mult, op1=ALU.subtract),
                         reads=['bisN', 'bisL'], writes=['bisT'])
                    K.op('dve', lambda e: e.tensor_scalar(out=mask[P, :], in0=score[P, sb_, :], scalar1=bis[P, 7:8],
                                                          scalar2=None, op0=ALU.is_ge),
                         reads=allsc + ['bisT'], writes=['mask'])
                    for g in range(8):
                        b = K.bank()
                        pT = ps[b][:].bitcast(BF16)
                        for k4 in range(4):
                            kt = g * 4 + k4
                            K.op('pe', lambda e, pT=pT, k4=k4, kt=kt: e.transpose(
                                out=pT[:, k4 * QTS:(k4 + 1) * QTS], in_=mask[P, kt * 128:(kt + 1) * 128],
                                identity=ident[P, 0:QTS]),
                                reads=['mask', 'ident'], writes=[PS(b)], signal=(k4 == 3))
                        src = pT[:, 0:4 * QTS].rearrange("p (k q) -> p k q", q=QTS)
                        dst = maskT[:, g * 4:(g + 1) * 4, qi3 * QTS:(qi3 + 1) * QTS]
                        if g % 2 == 0:
                            K.op('act', lambda e, src=src, dst=dst: e.copy(out=dst, in_=src),
                                 reads=[PS(b)], writes=['mT%d' % qi3])
                        else:
                            K.op('dve', lambda e, src=src, dst=dst: e.tensor_copy(out=dst, in_=src),
                                 reads=[PS(b)], writes=['mT%d' % qi3])

                emit_acc(0)
                for qb in range(NQB):
                    q0 = 2 + qb * QBS
                    for qi3 in range(3):
                        qt = qb * 3 + qi3
                        emit_bis(qt)
                        if qt + 1 < NQT:
                            emit_acc(qt + 1)
                        emit_fin(qt)
                    its = [(h, kt) for h in range(16) for kt in range(32)]

                    def emit_S(i, q0=q0):
                        h, kt = its[i]
                        n = h // 4
                        if kt == 0 and h % 4 == 0:
                            ksl = kvload[0] % 2
                            kvload[0] += 1
                            kvslot[n] = ksl
                            K.dma('sp', knb[:, ksl, :], kT_d[n], 'kn%d' % ksl, writes=['kn%d' % ksl])
                        ksl = kvslot[n]
                        bs_ = SBK[i % 4]
                        es_ = i % 4
                        K.op('pe', lambda e: e.matmul(ps[bs_][:, 0:QBS], lhsT=knb[:, ksl, kt * 128:(kt + 1) * 128],
                                                      rhs=qT[:, h, q0:q0 + QBS], start=True, stop=True),
                             reads=['kn%d' % ksl, 'q_%d' % h], writes=[PS(bs_)])
                        K.op('act', lambda e: e.activation(out=Eb[:, es_, :], in_=ps[bs_][:, 0:QBS], func=AF.Exp,
                                                           scale=128.0 ** -0.5),
                             reads=[PS(bs_)], writes=['E%d' % es_])
                        K.op('dve', lambda e: e.tensor_tensor(out=Em[:, es_, :], in0=Eb[:, es_, :],
                                                              in1=maskT[:, kt, :], op=ALU.mult),
                             reads=['E%d' % es_, 'mT0', 'mT1', 'mT2'], writes=['Em%d' % es_])
                    for i in range(LOOK):
                        emit_S(i)
                    for i in range(len(its)):
                        if i + LOOK < len(its):
                            emit_S(i + LOOK)
                        h, kt = its[i]
                        n = h // 4
                        ksl = kvslot[n]
                        es_ = i % 4
                        bo, bz = (0, 1) if h % 2 == 0 else (2, 3)
                        if kt == 0 and h % 4 == 0:
                            K.dma('sp', vnb[:, :, :], v_d[n], 'vn0', writes=['vn0'])
                        K.op('pe', lambda e, bo=bo, kt=kt, ksl=ksl, es_=es_: e.matmul(
                            ps[bo][:, 0:QBS], lhsT=vnb[:, kt, :], rhs=Em[:, es_, :],
                            start=(kt == 0), stop=(kt == 31)),
                            reads=['vn0', 'Em%d' % es_], writes=[PS(bo)], signal=False)
                        K.op('pe', lambda e, bz=bz, kt=kt, es_=es_: e.matmul(
                            ps[bz][:, 0:QBS], lhsT=onesb[:, :], rhs=Em[:, es_, :],
                            start=(kt == 0), stop=(kt == 31)),
                            reads=['onesb', 'Em%d' % es_], writes=[PS(bz)], signal=True)
                        if kt == 31:
                            K.op('dve', lambda e, bz=bz: e.reciprocal(out=rz[:, :], in_=ps[bz][:, 0:QBS]),
                                 reads=[PS(bz)], writes=['rz'])
                            ys = h % 2
                            K.op('dve', lambda e, bo=bo, ys=ys: e.tensor_tensor(out=ybs[:, ys, :], in0=ps[bo][:, 0:QBS],
                                                                                in1=rz[:, :], op=ALU.mult),
                                 reads=[PS(bo), 'rz'], writes=['ybs%d' % ys])
                            yb_toks.append(K.dma('sp', yb_d[h][:, q0:q0 + QBS], ybs[:, ys, :], 'ybs%d' % ys,
                                                 reads=['ybs%d' % ys]))
                K.barrier()

        K.barrier()
        u2_d = nc.dram_tensor("u2_s", [16, 128, NT], BF16, kind="Internal").ap()
        with ExitStack() as pQ:
            QT = sb("QT", [128, 16, NT], BF16, pQ)
            with ExitStack() as pm:
                u1T = sb("u1Tb", [128, 16, NT], BF16, pm)
                build_u(u1T, pm, SC_M, SH_M, xo_d, "b")
                sg = sb("sg", [128, 2, 343], F32, pm)
                sgi = [0]

                def gated(c, banksA, banksG, accumulate):
                    for ti, (a, z) in enumerate(TT):
                        n = z - a
                        s_ = sgi[0] % 2
                        sgi[0] += 1
                        K.op('act', lambda e, b=banksG[ti], s_=s_, n=n: e.activation(out=sg[:, s_, 0:n],
                                                                                   in_=ps[b][:, 0:n], func=AF.Sigmoid),
                             reads=[PS(banksG[ti])], writes=['sg%d' % s_])
                        if not accumulate:
                            K.op('dve', lambda e, b=banksA[ti], s_=s_, n=n, a=a, z=z: e.tensor_tensor(
                                out=QT[:, c, a:z], in0=ps[b][:, 0:n], in1=sg[:, s_, 0:n], op=ALU.mult),
                                reads=[PS(banksA[ti]), 'sg%d' % s_], writes=['m_%d' % c])
                        else:
                            K.op('dve', lambda e, b=banksA[ti], s_=s_, n=n: e.tensor_tensor(
                                out=sg[:, s_, 0:n], in0=ps[b][:, 0:n], in1=sg[:, s_, 0:n], op=ALU.mult),
                                reads=[PS(banksA[ti]), 'sg%d' % s_], writes=['sg%d' % s_])
                            K.op('dve', lambda e, s_=s_, n=n, a=a, z=z: e.tensor_tensor(
                                out=QT[:, c, a:z], in0=sg[:, s_, 0:n], in1=QT[:, c, a:z], op=ALU.add),
                                reads=['sg%d' % s_, 'm_%d' % c], writes=['m_%d' % c])

                with ExitStack() as pm2:
                    ybT = sb("ybT", [128, 16, NT], BF16, pm2)
                    K.wait('sp', yb_toks)
                    for h in range(16):
                        K.op('pool', lambda e, h=h: e.memset(ybT[:, h, 0:2], 0.0), writes=['yb_%d' % h])
                    for h in range(16):
                        K.dma('sp', ybT[:, h, 2:NT], yb_d[h][:, 2:NT], 'ybl', writes=['yb_%d' % h])
                    for h in range(16):
                        K.res['yb_%d' % h][0] = ('ybl', K.cnt['ybl'])
                    for c in range(16):
                        bA = dense(ybT, lambda kc: 'yb_%d' % kc)
                        bG = dense(u1T, lambda kc: 'u_%d' % kc)
                        gated(c, bA, bG, False)
                    K.barrier()
                with ExitStack() as pm1:
                    yaT = sb("yaT", [128, 16, NT], BF16, pm1)
                    with ExitStack() as p2b:
                        tb_ = sb("tbuf", [128, 2, NT + 2], F32, p2b)
                        hs = sb("hs", [128, 2, 343], F32, p2b)
                        Bs = sb("Bs", [128, 2, NT], F32, p2b)
                        yc = sb("yc", [128, 2, NT], F32, p2b)
                        K.op('dve', lambda e: e.memset(tb_[:, :, 0:2], 0.0), writes=['t0', 't1'])
                        hi = 0
                        for c in range(16):
                            ts_ = c % 2
                            slC = next_w()
                            slH = next_w()
                            slB = next_w()
                            for ti, (a, z) in enumerate(TT):
                                n = z - a
                                bC = K.bank()
                                mm_group(bC, 128, n, [(wring[:, slC, kc, :], u1T[:, kc, a:z]) for kc in range(16)],
                                         [['w%d' % slC, 'u_%d' % kc] for kc in range(16)])
                                bH = K.bank()
                                mm_group(bH, 128, n, [(wring[:, slH, kc, :], u1T[:, kc, a:z]) for kc in range(16)],
                                         [['w%d' % slH, 'u_%d' % kc] for kc in range(16)])
                                h_ = hi % 2
                                hi += 1
                                K.op('act', lambda e, bH=bH, h_=h_, n=n: e.copy(out=hs[:, h_, 0:n], in_=ps[bH][:, 0:n]),
                                     reads=[PS(bH)], writes=['hs%d' % h_])
                                K.op('dve', lambda e, bC=bC, h_=h_, n=n, a=a, z=z, ts_=ts_: e.tensor_tensor(
                                    out=tb_[:, ts_, 2 + a:2 + z], in0=ps[bC][:, 0:n], in1=hs[:, h_, 0:n], op=ALU.mult),
                                    reads=[PS(bC), 'hs%d' % h_], writes=['t%d' % ts_])
                            for ti, (a, z) in enumerate(TT):
                                n = z - a
                                bB = K.bank()
                                mm_group(bB, 128, n, [(wring[:, slB, kc, :], u1T[:, kc, a:z]) for kc in range(16)],
                                         [['w%d' % slB, 'u_%d' % kc] for kc in range(16)])
                                K.op('act', lambda e, bB=bB, n=n, a=a, z=z, ts_=ts_: e.copy(out=Bs[:, ts_, a:z],
                                                                                          in_=ps[bB][:, 0:n]),
                                     reads=[PS(bB)], writes=['Bs%d' % ts_])
                            K.op('dve', lambda e, ts_=ts_: e.tensor_scalar(out=tb_[:, ts_, 2:2 + HALO],
                                                                           in0=tb_[:, ts_, 2:2 + HALO],
                                                                           scalar1=halom[:, 0:1], scalar2=None,
                                                                           op0=ALU.mult),
                                 reads=['t%d' % ts_, 'halom'], writes=['t%d' % ts_])
                            K.op('dve', lambda e, ts_=ts_, c=c: e.tensor_scalar(out=yc[:, ts_, :],
                                                                                in0=tb_[:, ts_, 2:NT + 2],
                                                                                scalar1=cva[:, c, 2:3], scalar2=None,
                                                                                op0=ALU.mult),
                                 reads=['t%d' % ts_, 'cva'], writes=['yc%d' % ts_])
                            for jj in (1, 0):
                                K.op('dve', lambda e, ts_=ts_, c=c, jj=jj: e.scalar_tensor_tensor(
                                    out=yc[:, ts_, :], in0=tb_[:, ts_, jj:NT + jj], scalar=cva[:, c, jj:jj + 1],
                                    in1=yc[:, ts_, :], op0=ALU.mult, op1=ALU.add),
                                    reads=['t%d' % ts_, 'cva', 'yc%d' % ts_], writes=['yc%d' % ts_])
                            K.op('pool', lambda e, ts_=ts_, c=c: e.tensor_tensor(out=yaT[:, c, :], in0=Bs[:, ts_, :],
                                                                                 in1=yc[:, ts_, :], op=ALU.mult),
                                 reads=['Bs%d' % ts_, 'yc%d' % ts_], writes=['ya_%d' % c])
                        K.barrier()
                    for c in range(16):
                        bA = dense(yaT, lambda kc: 'ya_%d' % kc)
                        bG = dense(u1T, lambda kc: 'u_%d' % kc)
                        gated(c, bA, bG, True)
                    K.barrier()

            def layernorm(zT, ntile, cols, gi, bi, scope, tag, post):
                sq = sb("sq" + tag, [128, 2, 343], F32, scope)
                mean = sb("mean" + tag, [128, 343], F32, scope)
                rstd = sb("rstd" + tag, [128, 343], F32, scope)
                tmp = sb("tmp" + tag, [128, 2, 343], F32, scope)
                for ti, (a, z) in ntile:
                    n = z - a
                    b1 = K.bank()
                    mm_group(b1, 128, n, [(onesf[:, :], zT[:, c, a:z]) for c in range(16)],
                             [['onesf', 'z_%d' % c] for c in range(16)])
                    b2 = K.bank()
                    for c in range(16):
                        s_ = c % 2
                        K.op('act', lambda e, s_=s_, c=c, n=n, a=a, z=z: e.activation(out=sq[:, s_, 0:n],
                                                                                    in_=zT[:, c, a:z], func=AF.Square),
                             reads=['z_%d' % c], writes=['sq%d' % s_])
                        K.op('pe', lambda e, s_=s_, c=c, n=n: e.matmul(ps[b2][:, 0:n], lhsT=onesf[:, :],
                                                                       rhs=sq[:, s_, 0:n], start=(c == 0),
                                                                       stop=(c == 15)),
                             reads=['onesf', 'sq%d' % s_], writes=[PS(b2)], signal=True)
                    K.op('dve', lambda e, n=n: e.tensor_scalar(out=mean[:, 0:n], in0=ps[b1][:, 0:n], scalar1=1.0 / D,
                                                               scalar2=None, op0=ALU.mult),
                         reads=[PS(b1)], writes=['mean'])
                    K.op('dve', lambda e, n=n: e.tensor_tensor(out=rstd[:, 0:n], in0=mean[:, 0:n], in1=mean[:, 0:n],
                                                               op=ALU.mult), reads=['mean'], writes=['rstd'])
                    K.op('dve', lambda e, n=n: e.scalar_tensor_tensor(out=rstd[:, 0:n], in0=ps[b2][:, 0:n],
                                                                      scalar=1.0 / D, in1=rstd[:, 0:n],
                                                                      op0=ALU.mult, op1=ALU.subtract),
                         reads=[PS(b2), 'rstd'], writes=['rstd'])
                    K.op('act', lambda e, n=n: e.activation(out=rstd[:, 0:n], in_=rstd[:, 0:n], func=AF.Sqrt,
                                                            bias=epst[:, 0:1], scale=1.0),
                         reads=['rstd', 'epst'], writes=['rstd'])
                    K.op('dve', lambda e, n=n: e.reciprocal(out=rstd[:, 0:n], in_=rstd[:, 0:n]),
                         reads=['rstd'], writes=['rstd'])
                    for c in range(16):
                        s_ = c % 2
                        eng = 'dve' if c % 2 == 0 else 'pool'
                        K.op(eng, lambda e, s_=s_, c=c, n=n, a=a, z=z: e.tensor_tensor(
                            out=tmp[:, s_, 0:n], in0=zT[:, c, a:z], in1=mean[:, 0:n], op=ALU.subtract),
                            reads=['z_%d' % c, 'mean'], writes=['tmp%d' % s_])
                        K.op(eng, lambda e, s_=s_, n=n: e.tensor_tensor(
                            out=tmp[:, s_, 0:n], in0=tmp[:, s_, 0:n], in1=rstd[:, 0:n], op=ALU.mult),
                            reads=['tmp%d' % s_, 'rstd'], writes=['tmp%d' % s_])
                        K.op('dve', lambda e, s_=s_, c=c, n=n, a=a, z=z: e.tensor_scalar(
                            out=zT[:, c, a:z], in0=tmp[:, s_, 0:n], scalar1=lnp[:, gi, c:c + 1],
                            scalar2=lnp[:, bi, c:c + 1], op0=ALU.mult, op1=ALU.add),
                            reads=['tmp%d' % s_, 'lnp'], writes=['z_%d' % c])
                        post(c, ti, a, z)

            x1_toks = []
            u2_toks = []
            with ExitStack() as pl:
                zT = sb("zT", [128, 16, NT], F32, pl)
                xr2 = sb("xr2", [128, 2, NT], F32, pl)
                u2s = sb("u2s", [128, 2, 343], BF16, pl)
                for c in range(16):
                    sl = c % 2
                    K.dma('sp', xr2[:, sl, :], xo_d[c], 'xq%d' % sl, writes=['xq%d' % sl])
                    K.op('act', lambda e, sl=sl: e.mul(out=xr2[:, sl, :], in_=xr2[:, sl, :], mul=ALPHA),
                         reads=['xq%d' % sl], writes=['xq%d' % sl])
                    banks = dense(QT, lambda kc: 'm_%d' % kc)
                    for ti, (a, z) in enumerate(TT):
                        n = z - a
                        K.op('dve', lambda e, b=banks[ti], n=n, a=a, z=z, c=c, sl=sl: e.scalar_tensor_tensor(
                            out=zT[:, c, a:z], in0=ps[b][:, 0:n], scalar=modP[:, G_M + c:G_M + c + 1],
                            in1=xr2[:, sl, a:z], op0=ALU.mult, op1=ALU.add),
                            reads=[PS(banks[ti]), 'modP', 'xq%d' % sl], writes=['z_%d' % c])
                u2i = [0]

                def post1(c, ti, a, z):
                    n = z - a
                    s_ = u2i[0] % 2
                    u2i[0] += 1
                    K.op('act', lambda e: e.activation(out=u2s[:, s_, 0:n], in_=zT[:, c, a:z], func=AF.Identity,
                                                       bias=modT[:, SH_F + c:SH_F + c + 1],
                                                       scale=modP[:, SC_F + c:SC_F + c + 1]),
                         reads=['z_%d' % c, 'modT', 'modP'], writes=['u2s%d' % s_])
                    u2_toks.append(K.dma('sp', u2_d[c][:, a:z], u2s[:, s_, 0:n], 'u2s%d' % s_, reads=['u2s%d' % s_]))
                    x1_toks.append(K.dma('sp', x1_d[c][:, a:z], zT[:, c, a:z], 'x1st', reads=['z_%d' % c]))
                layernorm(zT, list(enumerate(TT)), NT, 0, 1, pl, "1", post1)
                K.barrier()
            K.barrier()

        with ExitStack() as pf:
            gT = sb("gT", [128, NFC, NT], BF16, pf)
            with ExitStack() as pu:
                u2T = sb("u2T", [128, 16, NT], BF16, pu)
                araw = sb("araw", [128, 2, NT + 2], F32, pu)
                ac = sb("ac", [128, 2, NT], F32, pu)
                K.wait('sp', u2_toks)
                for c in range(16):
                    K.dma('sp', u2T[:, c, :], u2_d[c], 'u2l', writes=['u2_%d' % c])
                for c in range(16):
                    K.res['u2_%d' % c][0] = ('u2l', K.cnt['u2l'])
                K.op('dve', lambda e: e.memset(araw[:, :, 0:2], 0.0), writes=['ar0', 'ar1'])
                for j in range(NFC):
                    as_ = j % 2
                    bA = dense(u2T, lambda kc: 'u2_%d' % kc)
                    for ti, (a, z) in enumerate(TT):
                        K.op('act', lambda e, b=bA[ti], a=a, z=z, as_=as_: e.copy(out=araw[:, as_, 2 + a:2 + z],
                                                                                in_=ps[b][:, 0:z - a]),
                             reads=[PS(bA[ti])], writes=['ar%d' % as_])
                    K.op('dve', lambda e, as_=as_: e.tensor_scalar(out=araw[:, as_, 2:2 + HALO],
                                                                   in0=araw[:, as_, 2:2 + HALO],
                                                                   scalar1=halom[:, 0:1], scalar2=None, op0=ALU.mult),
                         reads=['ar%d' % as_, 'halom'], writes=['ar%d' % as_])
                    K.op('pool', lambda e, as_=as_, j=j: e.tensor_scalar(out=ac[:, as_, :], in0=araw[:, as_, 2:NT + 2],
                                                                         scalar1=cvf[:, j, 2:3], scalar2=None,
                                                                         op0=ALU.mult),
                         reads=['ar%d' % as_, 'cvf'], writes=['ac%d' % as_])
                    for jj in (1, 0):
                        K.op('dve', lambda e, as_=as_, j=j, jj=jj: e.scalar_tensor_tensor(
                            out=ac[:, as_, :], in0=araw[:, as_, jj:NT + jj], scalar=cvf[:, j, jj:jj + 1],
                            in1=ac[:, as_, :], op0=ALU.mult, op1=ALU.add),
                            reads=['ar%d' % as_, 'cvf', 'ac%d' % as_], writes=['ac%d' % as_])
                    K.op('act', lambda e, as_=as_: e.activation(out=ac[:, as_, :], in_=ac[:, as_, :],
                                                                func=AF.Gelu_apprx_tanh),
                         reads=['ac%d' % as_], writes=['ac%d' % as_])
                    bB = dense(u2T, lambda kc: 'u2_%d' % kc)
                    for ti, (a, z) in enumerate(TT):
                        K.op('dve', lambda e, b=bB[ti], a=a, z=z, as_=as_, j=j: e.tensor_tensor(
                            out=gT[:, j, a:z], in0=ps[b][:, 0:z - a], in1=ac[:, as_, a:z], op=ALU.mult),
                            reads=[PS(bB[ti]), 'ac%d' % as_], writes=['g_%d' % j])
                K.barrier()
            with ExitStack() as pd:
                z2 = sb("z2", [128, 16, 343], F32, pd)
                x1r = sb("x1r", [128, 2, 343], F32, pd)
                K.wait('sp', x1_toks)
                out_toks = []
                for ti, (a, z) in enumerate(TT):
                    n = z - a
                    for c in range(16):
                        sl = c % 2
                        K.dma('sp', x1r[:, sl, 0:n], x1_d[c][:, a:z], 'x1r%d' % sl, writes=['x1r%d' % sl])
                        K.op('act', lambda e, sl=sl, n=n: e.mul(out=x1r[:, sl, 0:n], in_=x1r[:, sl, 0:n], mul=ALPHA),
                             reads=['x1r%d' % sl], writes=['x1r%d' % sl])
                        b = K.bank()
                        pairs, rl = [], []
                        for part in range(3):
                            slot = next_w()
                            for kc in range(min(16, NFC - part * 16)):
                                pairs.append((wring[:, slot, kc, :], gT[:, part * 16 + kc, a:z]))
                                rl.append(['w%d' % slot, 'g_%d' % (part * 16 + kc)])
                        mm_group(b, 128, n, pairs, rl)
                        K.op('dve', lambda e, b=b, n=n, c=c, sl=sl: e.scalar_tensor_tensor(
                            out=z2[:, c, 0:n], in0=ps[b][:, 0:n], scalar=modP[:, G_F + c:G_F + c + 1],
                            in1=x1r[:, sl, 0:n], op0=ALU.mult, op1=ALU.add),
                            reads=[PS(b), 'modP', 'x1r%d' % sl], writes=['z_%d' % c])

                    def post2(c, ti_, a_, z_):
                        if c == 15:
                            out_toks.append(K.dma('sp', out_d[:, :, a:z].rearrange("c p t -> p c t"), z2[:, :, 0:n],
                                                  'outst', reads=['z_%d' % cc for cc in range(16)]))
                    with ExitStack() as pln:
                        layernorm(z2, [(ti, (0, n))], n, 2, 3, pln, "2_%d" % ti, post2)
                        K.barrier()
                K.wait('sp', [('outst', K.cnt['outst'])])
                if debug:
                    K.wait('sp', [('dbg', K.cnt['dbg'])])
    return nc


_NC_CACHE = {}


def _prep(inputs):
    f32 = np.float32
    x = np.asarray(inputs['x'], f32)
    c = np.asarray(inputs['c'], f32)
    W = {k: np.asarray(inputs[k][0], f32) for k in ('w_cond', 'w_in', 'w_a', 'w_b', 'w_o', 'w_up', 'w_down')}
    items, seq = stream_plan()

    def fm(Wm, col0, ncols, kc0=0, nkc=16):
        blk = Wm[kc0 * 128:(kc0 + nkc) * 128, col0:col0 + ncols].reshape(nkc, 128, ncols).transpose(1, 0, 2)
        if nkc < 16:
            blk = np.concatenate([blk, np.zeros((128, 16 - nkc, ncols), f32)], axis=1)
        return blk
    ws = np.empty((len(items), 128, 16, 128), f32)
    for i, (nm, col0, kc0, nkc) in enumerate(items):
        ws[i] = fm(W[nm], col0, 128, kc0, nkc)
    wcond = np.ascontiguousarray(W['w_cond'].reshape(16, 128, 24, 512).transpose(2, 1, 0, 3))
    bcond = np.asarray(inputs['b_cond'], f32).reshape(1, 12288)
    wk = np.ascontiguousarray(fm(W['w_in'], O_K, 512))
    wv = np.ascontiguousarray(fm(W['w_in'], O_V, 512))
    wki = np.ascontiguousarray(fm(W['w_in'], O_KI, 64))
    wwi = np.ascontiguousarray(fm(W['w_in'], O_WI, 16))

    def pv(v):
        return np.ascontiguousarray(np.asarray(v, f32).reshape(-1, 128).T)
    cva = np.ascontiguousarray(np.stack([pv(inputs['conv_a'][0][j]) for j in range(3)], axis=2))
    cvf = np.ascontiguousarray(np.stack([pv(inputs['conv_f'][0][j]) for j in range(3)], axis=2))
    lnp = np.ascontiguousarray(np.stack([pv(inputs['ln1_g'][0]), pv(inputs['ln1_b'][0]),
                                         pv(inputs['ln2_g'][0]), pv(inputs['ln2_b'][0])], axis=1))
    kng = np.ascontiguousarray(np.broadcast_to(np.stack([np.asarray(inputs['idx_kn_g'][0], f32),
                                                         np.asarray(inputs['idx_kn_b'][0], f32)])[None], (128, 2, 64)))
    cidx = np.ascontiguousarray(np.broadcast_to((np.arange(64, dtype=f32) * 64.0)[None], (128, 64)))
    ident = np.eye(128, dtype=f32)
    maps = []
    for core in range(8):
        b, q = core // 4, core % 4
        s0 = q * 1024
        g = s0 - HALO + np.arange(NT)
        xo = np.zeros((NT, D), f32)
        valid = g >= 0
        xo[valid] = x[b, g[valid]]
        xo = np.ascontiguousarray(xo.T.reshape(16, 128, NT))
        xs = np.ascontiguousarray(x[b].T.reshape(16, 128, 16, 256).transpose(2, 1, 0, 3))
        cT = np.ascontiguousarray(c[b].reshape(16, 128).T)
        gq = g[2:2 + NQT * QTS].reshape(NQT, QTS)
        lim = np.clip((np.floor_divide(gq, 64) + 1) * 64, 64, S).astype(f32)
        qlim = np.full((128, NQT), float(S), f32)
        qlim[:QTS, :] = lim.T
        halom = np.full((128, 1), 0.0 if q == 0 else 1.0, f32)
        m = dict(xo=xo, xs=xs, cT=cT, wcond=wcond, bcond=bcond, ws=ws, wk=wk, wv=wv, wki=wki, wwi=wwi,
                 cva=cva, cvf=cvf, lnp=lnp, kng=kng, qlim=qlim, halom=halom, cidx=cidx, ident=ident)
        maps.append({"i_" + k_: v_ for k_, v_ in m.items()})
    return maps


def _assemble(results):
    out = np.empty((2, S, D), np.float32)
    for core in range(8):
        b, q = core // 4, core % 4
        o = np.asarray(results[core]["out"], np.float32)
        out[b, q * 1024:(q + 1) * 1024, :] = o.reshape(D, NT)[:, HALO:].T
    return out


def kernel(**inputs):
    if 'nc' not in _NC_CACHE:
        _NC_CACHE['nc'] = build_nc(False)
    maps = _prep(inputs)
    res = run_bass_kernel_spmd(_NC_CACHE['nc'], maps, core_ids=list(range(8)))
    return _assemble(res.results)
```

```python
import numpy as np
import ml_dtypes
from contextlib import ExitStack
import concourse.bass as bass
import concourse.mybir as mybir
from concourse.bass_utils import run_bass_kernel_spmd

F32 = mybir.dt.float32
BF16 = mybir.dt.bfloat16
AF = mybir.ActivationFunctionType
ALU = mybir.AluOpType
AX = mybir.AxisListType

D = 2048
S = 4096
NT = 1028
HALO = 4
TT = [(0, 343), (343, 686), (686, 1028)]
NQT = 9
QTS = 114
NQB = 3
QBS = 342
DFF = 5632
NFC = 44
ALPHA = 2.0 ** 0.25
LN_EPS = 1e-5
IDX_W_SCALE = 1024.0 ** -0.5
TOPK = 256.0
BIG = 1.0e6
NBIS = 16
WR = 8
O_B, O_C, O_H, O_Q, O_K, O_V, O_QI, O_KI, O_WI, O_GA, O_GB = 0, 2048, 4096, 6144, 8192, 8704, 9216, 10240, 10304, 10320, 12368


def stream_plan():
    items, seq = [], []

    def add(*d):
        items.append(d)
        seq.append(len(items) - 1)
    for h in range(16):
        add('w_in', O_Q + h * 128, 0, 16)
    for c in range(8):
        add('w_in', O_QI + c * 128, 0, 16)
    for c in range(16):
        add('w_b', c * 128, 0, 16)
        add('w_in', O_GB + c * 128, 0, 16)
    for c in range(16):
        add('w_in', O_C + c * 128, 0, 16)
        add('w_in', O_H + c * 128, 0, 16)
        add('w_in', O_B + c * 128, 0, 16)
    for c in range(16):
        add('w_a', c * 128, 0, 16)
        add('w_in', O_GA + c * 128, 0, 16)
    for c in range(16):
        add('w_o', c * 128, 0, 16)
    for j in range(NFC):
        add('w_up', j * 128, 0, 16)
        add('w_up', DFF + j * 128, 0, 16)
    base = len(items)
    for c in range(16):
        for part in range(3):
            items.append(('w_down', c * 128, part * 16, min(16, NFC - part * 16)))
    for tt in range(3):
        for c in range(16):
            for part in range(3):
                seq.append(base + c * 3 + part)
    return items, seq


class KB:
    def __init__(self, nc, es):
        self.nc, self.es = nc, es
        self.engs = {'pe': nc.tensor, 'act': nc.scalar, 'dve': nc.vector, 'pool': nc.gpsimd, 'sp': nc.sync}
        self.sems = {k: es.enter_context(nc.semaphore('s_' + k)) for k in self.engs}
        self.cnt = {k: 0 for k in self.engs}
        self.waited = {}
        self.res = {}
        self.pbank = 0

    def dsem(self, name):
        if name not in self.sems:
            self.sems[name] = self.es.enter_context(self.nc.semaphore('d_' + name))
            self.cnt[name] = 0
        return self.sems[name]

    def _deps(self, reads, writes):
        deps = []
        for r in reads:
            st = self.res.get(r)
            if st and st[0]:
                deps.append(st[0])
        for w in writes:
            st = self.res.get(w)
            if st:
                if st[0]:
                    deps.append(st[0])
                deps.extend(st[1])
        return deps

    def wait(self, eng, deps):
        for (key, val) in deps:
            if key == eng and eng == 'pe':
                continue
            if self.waited.get((eng, key), 0) >= val:
                continue
            self.engs[eng].wait_ge(self.sems[key], val)
            self.waited[(eng, key)] = val

    def _upd(self, tok, reads, writes):
        for r in reads:
            st = self.res.setdefault(r, [None, []])
            st[1].append(tok)
        for w in writes:
            self.res[w] = [tok, []]

    def op(self, eng, fn, reads=(), writes=(), signal=True):
        self.wait(eng, self._deps(reads, writes))
        ins = fn(self.engs[eng])
        if signal:
            self.cnt[eng] += 1
            ins.then_inc(self.sems[eng], 1)
            tok = (eng, self.cnt[eng])
        else:
            tok = (eng, self.cnt[eng] + 1)
        self._upd(tok, reads, writes)
        return tok

    def dma(self, queue, out, in_, sem, reads=(), writes=()):
        self.dsem(sem)
        self.wait(queue, self._deps(reads, writes))
        ins = self.engs[queue].dma_start(out=out, in_=in_)
        self.cnt[sem] += 16
        ins.then_inc(self.sems[sem], 16)
        tok = (sem, self.cnt[sem])
        self._upd(tok, reads, writes)
        return tok

    def barrier(self):
        toks = [(k, v) for k, v in self.cnt.items() if v > 0]
        for e in ('pe', 'act', 'dve', 'pool', 'sp'):
            self.wait(e, toks)

    def bank(self):
        b = self.pbank
        self.pbank = (self.pbank + 1) % 8
        return b


def build_nc(debug=False):
    nc = bass.Bass("TRN2", target_bir_lowering=False)
    items, seq = stream_plan()
    NI = len(items)

    def din(name, shape, dt=F32):
        return nc.dram_tensor("i_" + name, shape, dt, kind="ExternalInput").ap()
    xo_d = din("xo", [16, 128, NT])
    xs_d = din("xs", [16, 128, 16, 256])
    cT_d = din("cT", [128, 16])
    wc_d = din("wcond", [24, 128, 16, 512])
    bc_d = din("bcond", [1, 12288])
    ws_d = din("ws", [NI, 128, 16, 128])
    wk_d = din("wk", [128, 16, 512])
    wv_d = din("wv", [128, 16, 512])
    wki_d = din("wki", [128, 16, 64])
    wwi_d = din("wwi", [128, 16, 16])
    cva_d = din("cva", [128, 16, 3])
    cvf_d = din("cvf", [128, NFC, 3])
    lnp_d = din("lnp", [128, 4, 16])
    kng_d = din("kng", [128, 2, 64])
    qlim_d = din("qlim", [128, NQT])
    halo_d = din("halom", [128, 1])
    cidx_d = din("cidx", [128, 64])
    ident_d = din("ident", [128, 128])
    okind = "ExternalOutput" if debug else "Internal"
    out_d = nc.dram_tensor("out", [16, 128, NT], F32, kind="ExternalOutput").ap()
    kT_d = nc.dram_tensor("kT_s", [4, 128, S], BF16, kind=okind).ap()
    v_d = nc.dram_tensor("v_s", [4, 128, 32, 128], BF16, kind=okind).ap()
    yb_d = nc.dram_tensor("yb_s", [16, 128, NT], BF16, kind=okind).ap()
    x1_d = nc.dram_tensor("x1_s", [16, 128, NT], F32, kind=okind).ap()
    mod_o = nc.dram_tensor("mod_o", [128, 96], F32, kind=okind).ap()

    with ExitStack() as es:
        K = KB(nc, es)

        def sb(name, shape, dt=F32, scope=es):
            return scope.enter_context(nc.sbuf_tensor(name, shape, dt))
        ps = [es.enter_context(nc.psum_tensor("ps%d" % i, [128, 512], F32)) for i in range(8)]

        def PS(b):
            return "ps%d" % b

        ident = sb("ident", [128, 128], BF16)
        identf = sb("identf", [128, 128], F32)
        onesb = sb("onesb", [128, 128], BF16)
        onesf = sb("onesf", [128, 128], F32)
        one11 = sb("one11", [1, 2], F32)
        epst = sb("epst", [128, 1], F32)
        cva = sb("cva", [128, 16, 3])
        cvf = sb("cvf", [128, NFC, 3])
        lnp = sb("lnp", [128, 4, 16])
        kng = sb("kng", [128, 2, 64])
        qlim = sb("qlim", [128, NQT])
        halom = sb("halom", [128, 1])
        cidx = sb("cidx", [128, 64])
        modT = sb("modT", [128, 96])
        modP = sb("modP", [128, 96])
        wring = sb("wring", [128, WR, 16, 128], BF16)
        kiTE = sb("kiTE", [128, S], BF16)
        kiTO = sb("kiTO", [128, S], BF16)
        for (t, d_) in ((identf, ident_d), (cva, cva_d), (cvf, cvf_d), (lnp, lnp_d), (kng, kng_d), (qlim, qlim_d),
                        (halom, halo_d), (cidx, cidx_d)):
            K.dma('sp', t[:], d_, 'c_' + t.name, writes=[t.name])
        K.op('dve', lambda e: e.tensor_copy(out=ident[:], in_=identf[:]), reads=['identf'], writes=['ident'])
        K.op('dve', lambda e: e.memset(onesb[:], 1.0), writes=['onesb'])
        K.op('dve', lambda e: e.memset(onesf[:], 1.0), writes=['onesf'])
        K.op('dve', lambda e: e.memset(one11[:], 1.0), writes=['one11'])
        K.op('dve', lambda e: e.memset(epst[:], LN_EPS), writes=['epst'])

        wst = {'pos': 0, 'loaded': 0}

        def w_prefetch(upto):
            while wst['loaded'] < min(upto, len(seq)):
                i = wst['loaded']
                slot = i % WR
                K.dma('pool', wring[:, slot, :, :], ws_d[seq[i]], 'w%d' % slot, writes=['w%d' % slot])
                wst['loaded'] += 1

        def next_w():
            i = wst['pos']
            w_prefetch(i + 1)
            wst['pos'] += 1
            w_prefetch(i + WR - 2)
            return i % WR

        def mm_group(bank, M, n, pairs, reads_list, moff=0):
            last = len(pairs) - 1
            for i, (l, r) in enumerate(pairs):
                K.op('pe', lambda e, l=l, r=r, i=i: e.matmul(ps[bank][moff:moff + M, 0:n], lhsT=l, rhs=r,
                                                              start=(i == 0), stop=(i == last)),
                     reads=reads_list[i], writes=[PS(bank)], signal=(i == last))

        with ExitStack() as p0:
            cTs = sb("cTs", [128, 16], F32, p0)
            cact = sb("cact", [128, 16], BF16, p0)
            wcb = sb("wcb", [128, 2, 16, 512], BF16, p0)
            modrow = sb("modrow", [1, 12288], F32, p0)
            brow = sb("brow", [1, 12288], F32, p0)
            K.dma('sp', cTs[:], cT_d, 'c_cTs', writes=['cTs'])
            K.dma('sp', brow[:], bc_d, 'c_brow', writes=['brow'])
            K.op('act', lambda e: e.activation(out=cact[:], in_=cTs[:], func=AF.Silu), reads=['cTs'], writes=['cact'])
            for nb in range(24):
                sl = nb % 2
                K.dma('pool', wcb[:, sl, :, :], wc_d[nb], 'wc%d' % sl, writes=['wc%d' % sl])
                b = K.bank()
                mm_group(b, 1, 512, [(cact[:, kc:kc + 1], wcb[:, sl, kc, :]) for kc in range(16)],
                         [['cact', 'wc%d' % sl]] * 16)
                K.op('dve', lambda e, b=b, nb=nb: e.tensor_tensor(out=modrow[0:1, nb * 512:(nb + 1) * 512],
                                                                   in0=ps[b][0:1, 0:512],
                                                                   in1=brow[0:1, nb * 512:(nb + 1) * 512], op=ALU.add),
                     reads=[PS(b), 'brow'], writes=['modrow'])
            w_prefetch(WR - 2)
            b = K.bank()
            for c in range(96):
                K.op('pe', lambda e, c=c: e.matmul(ps[b][:, c:c + 1], lhsT=modrow[0:1, c * 128:(c + 1) * 128],
                                                   rhs=one11[0:1, 0:1], start=True, stop=True),
                     reads=['modrow', 'one11'], writes=[PS(b)], signal=(c == 95))
            K.op('dve', lambda e: e.tensor_copy(out=modT[:], in_=ps[b][:, 0:96]), reads=[PS(b)], writes=['modT'])
            K.op('dve', lambda e: e.tensor_scalar(out=modP[:], in0=modT[:], scalar1=1.0, scalar2=None, op0=ALU.add),
                 reads=['modT'], writes=['modP'])
            K.barrier()
        if debug:
            K.dma('sp', mod_o, modT[:], 'dbg', reads=['modT'])
        SH_M, SC_M, G_M, SH_F, SC_F, G_F = 0, 16, 32, 48, 64, 80

        def modulate(eng_i, out_ap, in_ap, kc, scb, shb, reads, writes):
            if eng_i % 2 == 0:
                K.op('dve', lambda e: e.tensor_scalar(out=out_ap, in0=in_ap, scalar1=modP[:, scb + kc:scb + kc + 1],
                                                      scalar2=modT[:, shb + kc:shb + kc + 1], op0=ALU.mult, op1=ALU.add),
                     reads=reads + ['modT', 'modP'], writes=writes)
            else:
                K.op('act', lambda e: e.activation(out=out_ap, in_=in_ap, func=AF.Identity,
                                                   bias=modT[:, shb + kc:shb + kc + 1],
                                                   scale=modP[:, scb + kc:scb + kc + 1]),
                     reads=reads + ['modT', 'modP'], writes=writes)

        store_toks = []
        with ExitStack() as p1:
            wk = sb("wk", [128, 16, 512], BF16, p1)
            wv = sb("wv", [128, 16, 512], BF16, p1)
            wki = sb("wki", [128, 16, 64], BF16, p1)
            xsb = sb("xsb", [128, 2, 16, 256], F32, p1)
            usb = sb("usb", [128, 2, 16, 256], BF16, p1)
            kst = sb("kst", [128, 2, 4, 256], BF16, p1)
            vst = sb("vst", [128, 2, 2, 512], BF16, p1)
            kraw = sb("kraw", [128, 2, 64], F32, p1)
            kn2 = sb("kn2", [128, 2, 2, 128], BF16, p1)
            K.op('dve', lambda e: e.memset(kn2[:], 0.0), writes=['ki0n', 'ki1n'])
            bst = sb("bst", [128, 2, 6], F32, p1)
            bmv = sb("bmv", [128, 2, 2], F32, p1)
            K.dma('pool', wk[:], wk_d, 'c_wk', writes=['wk'])
            K.dma('pool', wv[:], wv_d, 'c_wv', writes=['wv'])
            K.dma('pool', wki[:], wki_d, 'c_wki', writes=['wki'])
            K.dma('sp', xsb[:, 0, :, :], xs_d[0], 'xs0', writes=['xs0'])
            for tb in range(16):
                sl = tb % 2
                if tb + 1 < 16:
                    K.dma('sp', xsb[:, 1 - sl, :, :], xs_d[tb + 1], 'xs%d' % (1 - sl), writes=['xs%d' % (1 - sl)])
                for kc in range(16):
                    modulate(kc, usb[:, sl, kc, :], xsb[:, sl, kc, :], kc, SC_M, SH_M,
                             ['xs%d' % sl], ['us%d_%d' % (sl, kc)])
                for n in range(4):
                    b = K.bank()
                    mm_group(b, 128, 256, [(wk[:, kc, n * 128:(n + 1) * 128], usb[:, sl, kc, :]) for kc in range(16)],
                             [['wk', 'us%d_%d' % (sl, kc)] for kc in range(16)])
                    K.op('act', lambda e, b=b, n=n: e.copy(out=kst[:, sl, n, :], in_=ps[b][:, 0:256]),
                         reads=[PS(b)], writes=['kst%d' % sl])
                store_toks.append(K.dma('sp', kT_d[:, :, tb * 256:(tb + 1) * 256].rearrange("n p t -> p n t"),
                                        kst[:, sl, :, :], 'kst%d' % sl, reads=['kst%d' % sl]))
                for sub in range(2):
                    tsl = slice(sub * 128, (sub + 1) * 128)
                    b = K.bank()
                    mm_group(b, 128, 512, [(usb[:, sl, kc, tsl], wv[:, kc, :]) for kc in range(16)],
                             [['wv', 'us%d_%d' % (sl, kc)] for kc in range(16)])
                    K.op('dve', lambda e, b=b, sub=sub: e.tensor_copy(out=vst[:, sl, sub, :], in_=ps[b][:, 0:512]),
                         reads=[PS(b)], writes=['vst%d' % sl])
                    b = K.bank()
                    mm_group(b, 128, 64, [(usb[:, sl, kc, tsl], wki[:, kc, :]) for kc in range(16)],
                             [['wki', 'us%d_%d' % (sl, kc)] for kc in range(16)])
                    r_ = 'ki%d' % sub
                    K.op('act', lambda e, b=b, sub=sub: e.copy(out=kraw[:, sub, :], in_=ps[b][:, 0:64]),
                         reads=[PS(b)], writes=[r_])
                    K.op('dve', lambda e, sub=sub: e.bn_stats(out=bst[:, sub, :], in_=kraw[:, sub, :]),
                         reads=[r_], writes=[r_ + 's'])
                    K.op('dve', lambda e, sub=sub: e.bn_aggr(out=bmv[:, sub, :], in_=bst[:, sub, :]),
                         reads=[r_ + 's'], writes=[r_ + 'm'])
                    K.op('act', lambda e, sub=sub: e.activation(out=bmv[:, sub, 1:2], in_=bmv[:, sub, 1:2], func=AF.Sqrt,
                                                                bias=epst[:, 0:1], scale=1.0),
                         reads=[r_ + 'm', 'epst'], writes=[r_ + 'm'])
                    K.op('dve', lambda e, sub=sub: e.reciprocal(out=bmv[:, sub, 1:2], in_=bmv[:, sub, 1:2]),
                         reads=[r_ + 'm'], writes=[r_ + 'm'])
                    K.op('dve', lambda e, sub=sub: e.tensor_scalar(out=kraw[:, sub, :], in0=kraw[:, sub, :],
                                                                   scalar1=bmv[:, sub, 0:1], scalar2=bmv[:, sub, 1:2],
                                                                   op0=ALU.subtract, op1=ALU.mult),
                         reads=[r_, r_ + 'm'], writes=[r_])
                    K.op('dve', lambda e, sub=sub: e.tensor_tensor(out=kraw[:, sub, :], in0=kraw[:, sub, :],
                                                                   in1=kng[:, 0, :], op=ALU.mult),
                         reads=[r_, 'kng'], writes=[r_])
                    K.op('dve', lambda e, sub=sub: e.tensor_tensor(out=kn2[:, sub, 0, 0:64], in0=kraw[:, sub, :],
                                                                   in1=kng[:, 1, :], op=ALU.add),
                         reads=[r_, 'kng'], writes=[r_ + 'n'])
                    K.op('dve', lambda e, sub=sub: e.tensor_copy(out=kn2[:, sub, 1, 64:128], in_=kn2[:, sub, 0, 0:64]),
                         reads=[r_ + 'n'], writes=[r_ + 'n'])
                    b = K.bank()
                    pT = ps[b][:].bitcast(BF16)
                    for eo in range(2):
                        K.op('pe', lambda e, pT=pT, sub=sub, eo=eo: e.transpose(out=pT[:, eo * 128:(eo + 1) * 128],
                                                                                in_=kn2[:, sub, eo, :],
                                                                                identity=ident[:, :]),
                             reads=[r_ + 'n', 'ident'], writes=[PS(b)], signal=(eo == 1))
                    t0 = tb * 256 + sub * 128
                    K.op('act', lambda e, pT=pT, t0=t0: e.copy(out=kiTE[:, t0:t0 + 128], in_=pT[:, 0:128]),
                         reads=[PS(b)], writes=['kiT'])
                    K.op('act', lambda e, pT=pT, t0=t0: e.copy(out=kiTO[:, t0:t0 + 128], in_=pT[:, 128:256]),
                         reads=[PS(b)], writes=['kiT'])
                for n in range(4):
                    store_toks.append(K.dma('sp', v_d[n][:, tb * 2:tb * 2 + 2, :], vst[:, sl, :, n * 128:(n + 1) * 128],
                                            'vst%d' % sl, reads=['vst%d' % sl]))
            K.barrier()

        def build_u(uT, scope, scb, shb, src_d, tag):
            xr = sb("xr_" + tag, [128, 2, NT], F32, scope)
            for kc in range(16):
                sl = kc % 2
                K.dma('sp', xr[:, sl, :], src_d[kc], 'xr%d' % sl, writes=['xr%d' % sl])
                modulate(kc, uT[:, kc, :], xr[:, sl, :], kc, scb, shb, ['xr%d' % sl], ['u_%d' % kc])

        def dense(rhsT, rhs_res, nkc=16):
            slot = next_w()
            banks = []
            for (a, z) in TT:
                b = K.bank()
                mm_group(b, 128, z - a, [(wring[:, slot, kc, :], rhsT[:, kc, a:z]) for kc in range(nkc)],
                         [['w%d' % slot, rhs_res(kc)] for kc in range(nkc)])
                banks.append(b)
            return banks

        with ExitStack() as pA:
            qT = sb("qT", [128, 16, NT], BF16, pA)
            qiT = sb("qiT", [128, 8, NT], BF16, pA)
            wtm = sb("wtm", [128, NQT, 16], F32, pA)
            with ExitStack() as p2a:
                u1T = sb("u1T", [128, 16, NT], BF16, p2a)
                wwi = sb("wwi", [128, 16, 16], BF16, p2a)
                K.dma('pool', wwi[:], wwi_d, 'c_wwi', writes=['wwi'])
                build_u(u1T, p2a, SC_M, SH_M, xo_d, "a")
                for h in range(24):
                    banks = dense(u1T, lambda kc: 'u_%d' % kc)
                    for ti, (a, z) in enumerate(TT):
                        b = banks[ti]
                        dst = qT[:, h, a:z] if h < 16 else qiT[:, h - 16, a:z]
                        dres = 'q_%d' % h
                        if (h + ti) % 2 == 0:
                            K.op('act', lambda e, b=b, dst=dst, n=z - a: e.copy(out=dst, in_=ps[b][:, 0:n]),
                                 reads=[PS(b)], writes=[dres])
                        else:
                            K.op('dve', lambda e, b=b, dst=dst, n=z - a: e.tensor_copy(out=dst, in_=ps[b][:, 0:n]),
                                 reads=[PS(b)], writes=[dres])
                for qt in range(NQT):
                    j0 = 2 + qt * QTS
                    b = K.bank()
                    mm_group(b, QTS, 16, [(u1T[:, kc, j0:j0 + QTS], wwi[:, kc, :]) for kc in range(16)],
                             [['wwi', 'u_%d' % kc] for kc in range(16)])
                    K.op('act', lambda e, b=b, qt=qt: e.mul(out=wtm[0:QTS, qt, :], in_=ps[b][0:QTS, 0:16],
                                                            mul=IDX_W_SCALE),
                         reads=[PS(b)], writes=['wtm'])
                K.barrier()

            with ExitStack() as pt:
                score = sb("score", [128, 2, S], F32, pt)
                dg = sb("dg", [128, 16, QTS], BF16, pt)
                rb = sb("rb", [128, 4, 512], BF16, pt)
                mask = sb("mask", [128, S], BF16, pt)
                maskT = sb("maskT", [128, 32, QBS], BF16, pt)
                pen = sb("pen", [128, 2, 64], F32, pt)
                bis = sb("bis", [128, 8], F32, pt)
                nst = sb("nst", [128, NBIS], F32, pt)
                pwn = sb("pwn", [128, NBIS], F32, pt)
                c35 = sb("c35", [128, 1], F32, pt)
                knb = sb("knb", [128, 2, S], BF16, pt)
                vnb = sb("vnb", [128, 32, 128], BF16, pt)
                Eb = sb("Eb", [128, 4, QBS], BF16, pt)
                Em = sb("Em", [128, 4, QBS], BF16, pt)
                rz = sb("rz", [128, QBS], F32, pt)
                ybs = sb("ybs", [128, 2, QBS], BF16, pt)
                for k_ in range(NBIS):
                    K.op('dve', lambda e, k_=k_: e.memset(pwn[:, k_:k_ + 1], -(2.0 ** -(k_ + 1))), writes=['pwn'])
                K.op('dve', lambda e: e.memset(c35[:], float(S) - 2.0 * TOPK + 0.5), writes=['c35'])
                K.wait('sp', store_toks)
                kvload = [0]
                kvslot = {}
                yb_toks = []
                LOOK = 3
                SBK = [4, 5, 6, 7]
                P = slice(0, QTS)
                uctr = [0]

                def emit_acc(qt):
                    sb_ = qt % 2
                    j0 = 2 + qt * QTS
                    K.op('dve', lambda e: e.tensor_scalar(out=pen[P, sb_, :], in0=cidx[P, :],
                                                          scalar1=qlim[P, qt:qt + 1], scalar2=-BIG,
                                                          op0=ALU.is_ge, op1=ALU.mult),
                         reads=['cidx', 'qlim'], writes=['pen%d' % sb_])
                    for h in range(16):
                        K.op('dve', lambda e, h=h: e.tensor_scalar(out=dg[P, h, :], in0=ident[P, 0:QTS],
                                                                   scalar1=wtm[P, qt, h:h + 1], scalar2=None,
                                                                   op0=ALU.mult),
                             reads=['ident', 'wtm'], writes=['dg%d' % h])
                    for kb in range(8):
                        ks = slice(kb * 512, (kb + 1) * 512)
                        sbank = kb % 2
                        rings = {}

                        def emit_D(h):
                            u = uctr[0]
                            uctr[0] += 1
                            bank = 2 + (u % 6)
                            ring = u % 4
                            kz = kiTE if h % 2 == 0 else kiTO
                            K.op('pe', lambda e: e.matmul(ps[bank][P, 0:512], lhsT=qiT[:, h // 2, j0:j0 + QTS],
                                                          rhs=kz[:, ks], start=True, stop=True),
                                 reads=['q_%d' % (16 + h // 2), 'kiT'], writes=[PS(bank)])
                            if False:
                                K.op('act', lambda e: e.activation(out=rb[P, ring, :], in_=ps[bank][P, 0:512],
                                                                   func=AF.Relu),
                                     reads=[PS(bank)], writes=['rb%d' % ring])
                            else:
                                K.op('dve', lambda e: e.tensor_scalar(out=rb[P, ring, :], in0=ps[bank][P, 0:512],
                                                                      scalar1=0.0, scalar2=None, op0=ALU.max),
                                     reads=[PS(bank)], writes=['rb%d' % ring])
                            rings[h] = ring
                        for h in range(2):
                            emit_D(h)
                        for h in range(16):
                            if h + 2 < 16:
                                emit_D(h + 2)
                            K.op('pe', lambda e, h=h, r_=rings[h]: e.matmul(
                                ps[sbank][P, 0:512], lhsT=dg[P, h, :], rhs=rb[P, r_, :],
                                start=(h == 0), stop=(h == 15)),
                                reads=['dg%d' % h, 'rb%d' % rings[h]], writes=[PS(sbank)], signal=(h == 15))
                        K.op('dve', lambda e: e.tensor_tensor(
                            out=score[P, sb_, ks].rearrange("p (c j) -> p c j", j=64),
                            in0=ps[sbank][P, 0:512].rearrange("p (c j) -> p c j", j=64),
                            in1=pen[P, sb_, kb * 8:(kb + 1) * 8].unsqueeze(2).to_broadcast([QTS, 8, 64]), op=ALU.add),
                            reads=[PS(sbank), 'pen%d' % sb_], writes=['sc%d_%d' % (sb_, kb)])

                def emit_bis(qt):
                    sb_ = qt % 2
                    qi3 = qt % 3
                    allsc = ['sc%d_%d' % (sb_, kb) for kb in range(8)]
                    K.op('dve', lambda e: e.tensor_reduce(out=bis[P, 5:6], in_=score[P, sb_, :], axis=AX.X, op=ALU.max),
                         reads=allsc, writes=['bisA'])
                    K.op('dve', lambda e: e.tensor_reduce(out=bis[P, 6:7], in_=score[P, sb_, 0:256], axis=AX.X,
                                                          op=ALU.min), reads=allsc, writes=['bisB'])
                    K.op('dve', lambda e: e.tensor_scalar(out=bis[P, 6:7], in0=bis[P, 6:7], scalar1=-1000.0,
                                                          scalar2=None, op0=ALU.max), reads=['bisB'], writes=['bisB'])
                    K.op('dve', lambda e: e.scalar_tensor_tensor(out=bis[P, 0:1], in0=bis[P, 5:6], scalar=-0.5,
                                                                 in1=bis[P, 6:7], op0=ALU.mult, op1=ALU.subtract),
                         reads=['bisA', 'bisB'], writes=['bisN'])
                    K.op('dve', lambda e: e.scalar_tensor_tensor(out=bis[P, 0:1], in0=bis[P, 6:7], scalar=0.5,
                                                                 in1=bis[P, 0:1], op0=ALU.mult, op1=ALU.add),
                         reads=['bisB', 'bisN'], writes=['bisN'])
                    K.op('dve', lambda e: e.tensor_tensor(out=bis[P, 1:2], in0=bis[P, 5:6], in1=bis[P, 6:7],
                                                          op=ALU.subtract), reads=['bisA', 'bisB'], writes=['bisH'])
                    K.op('dve', lambda e: e.tensor_scalar(out=bis[P, 1:2], in0=bis[P, 1:2], scalar1=0.5,
                                                          scalar2=1e-3, op0=ALU.mult, op1=ALU.add),
                         reads=['bisH'], writes=['bisH'])
                    K.op('dve', lambda e: e.tensor_scalar(out=nst[P, :], in0=pwn[P, :], scalar1=bis[P, 1:2],
                                                          scalar2=None, op0=ALU.mult),
                         reads=['bisH', 'pwn'], writes=['nst'])
                    K.op('dve', lambda e: e.tensor_scalar(out=bis[P, 2:3], in0=bis[P, 1:2],
                                                          scalar1=2.0 ** -NBIS, scalar2=None, op0=ALU.mult),
                         reads=['bisH'], writes=['bisL'])
                    for it in range(NBIS):
                        K.op('act', lambda e: e.activation(out=mask[P, :], in_=score[P, sb_, :], func=AF.Sign,
                                                           bias=bis[P, 0:1], scale=1.0, accum_out=bis[P, 3:4]),
                             reads=allsc + ['bisN'], writes=['mask', 'bisC'])
                        K.op('act', lambda e: e.activation(out=bis[P, 4:5], in_=bis[P, 3:4], func=AF.Sign,
                                                           bias=c35[P, 0:1], scale=1.0),
                             reads=['bisC', 'c35'], writes=['bisS'])
                        K.op('act', lambda e, it=it: e.activation(out=bis[P, 0:1], in_=bis[P, 4:5],
                                                                  func=AF.Identity, bias=bis[P, 0:1],
                                                                  scale=nst[P, it:it + 1]),
                             reads=['bisS', 'nst', 'bisN'], writes=['bisN'])

                def emit_fin(qt):
                    sb_ = qt % 2
                    qi3 = qt % 3
                    allsc = ['sc%d_%d' % (sb_, kb) for kb in range(8)]
                    K.op('dve', lambda e: e.scalar_tensor_tensor(out=bis[P, 7:8], in0=bis[P, 0:1], scalar=-1.0,
                                                                 in1=bis[P, 2:3], op0=ALU.mult, op1=ALU.subtract),
                         reads=['bisN', 'bisL'], writes=['bisT'])
                    K.op('dve', lambda e: e.tensor_scalar(out=mask[P, :], in0=score[P, sb_, :], scalar1=bis[P, 7:8],
                                                          scalar2=None, op0=ALU.is_ge),
                         reads=allsc + ['bisT'], writes=['mask'])
                    for g in range(8):
                        b = K.bank()
                        pT = ps[b][:].bitcast(BF16)
                        for k4 in range(4):
                            kt = g * 4 + k4
                            K.op('pe', lambda e, pT=pT, k4=k4, kt=kt: e.transpose(
                                out=pT[:, k4 * QTS:(k4 + 1) * QTS], in_=mask[P, kt * 128:(kt + 1) * 128],
                                identity=ident[P, 0:QTS]),
                                reads=['mask', 'ident'], writes=[PS(b)], signal=(k4 == 3))
                        src = pT[:, 0:4 * QTS].rearrange("p (k q) -> p k q", q=QTS)
                        dst = maskT[:, g * 4:(g + 1) * 4, qi3 * QTS:(qi3 + 1) * QTS]
                        if g % 2 == 0:
                            K.op('act', lambda e, src=src, dst=dst: e.copy(out=dst, in_=src),
                                 reads=[PS(b)], writes=['mT%d' % qi3])
                        else:
                            K.op('dve', lambda e, src=src, dst=dst: e.tensor_copy(out=dst, in_=src),
                                 reads=[PS(b)], writes=['mT%d' % qi3])

                emit_acc(0)
                for qb in range(NQB):
                    q0 = 2 + qb * QBS
                    for qi3 in range(3):
                        qt = qb * 3 + qi3
                        emit_bis(qt)
                        if qt + 1 < NQT:
                            emit_acc(qt + 1)
                        emit_fin(qt)
                    its = [(h, kt) for h in range(16) for kt in range(32)]

                    def emit_S(i, q0=q0):
                        h, kt = its[i]
                        n = h // 4
                        if kt == 0 and h % 4 == 0:
                            ksl = kvload[0] % 2
                            kvload[0] += 1
                            kvslot[n] = ksl
                            K.dma('sp', knb[:, ksl, :], kT_d[n], 'kn%d' % ksl, writes=['kn%d' % ksl])
                        ksl = kvslot[n]
                        bs_ = SBK[i % 4]
                        es_ = i % 4
                        K.op('pe', lambda e: e.matmul(ps[bs_][:, 0:QBS], lhsT=knb[:, ksl, kt * 128:(kt + 1) * 128],
                                                      rhs=qT[:, h, q0:q0 + QBS], start=True, stop=True),
                             reads=['kn%d' % ksl, 'q_%d' % h], writes=[PS(bs_)])
                        K.op('act', lambda e: e.activation(out=Eb[:, es_, :], in_=ps[bs_][:, 0:QBS], func=AF.Exp,
                                                           scale=128.0 ** -0.5),
                             reads=[PS(bs_)], writes=['E%d' % es_])
                        K.op('dve', lambda e: e.tensor_tensor(out=Em[:, es_, :], in0=Eb[:, es_, :],
                                                              in1=maskT[:, kt, :], op=ALU.mult),
                             reads=['E%d' % es_, 'mT0', 'mT1', 'mT2'], writes=['Em%d' % es_])
                    for i in range(LOOK):
                        emit_S(i)
                    for i in range(len(its)):
                        if i + LOOK < len(its):
                            emit_S(i + LOOK)
                        h, kt = its[i]
                        n = h // 4
                        ksl = kvslot[n]
                        es_ = i % 4
                        bo, bz = (0, 1) if h % 2 == 0 else (2, 3)
                        if kt == 0 and h % 4 == 0:
                            K.dma('sp', vnb[:, :, :], v_d[n], 'vn0', writes=['vn0'])
                        K.op('pe', lambda e, bo=bo, kt=kt, ksl=ksl, es_=es_: e.matmul(
                            ps[bo][:, 0:QBS], lhsT=vnb[:, kt, :], rhs=Em[:, es_, :],
                            start=(kt == 0), stop=(kt == 31)),
                            reads=['vn0', 'Em%d' % es_], writes=[PS(bo)], signal=False)
                        K.op('pe', lambda e, bz=bz, kt=kt, es_=es_: e.matmul(
                            ps[bz][:, 0:QBS], lhsT=onesb[:, :], rhs=Em[:, es_, :],
                            start=(kt == 0), stop=(kt == 31)),
                            reads=['onesb', 'Em%d' % es_], writes=[PS(bz)], signal=True)
                        if kt == 31:
                            K.op('dve', lambda e, bz=bz: e.reciprocal(out=rz[:, :], in_=ps[bz][:, 0:QBS]),
                                 reads=[PS(bz)], writes=['rz'])
                            ys = h % 2
                            K.op('dve', lambda e, bo=bo, ys=ys: e.tensor_tensor(out=ybs[:, ys, :], in0=ps[bo][:, 0:QBS],
                                                                                in1=rz[:, :], op=ALU.mult),
                                 reads=[PS(bo), 'rz'], writes=['ybs%d' % ys])
                            yb_toks.append(K.dma('sp', yb_d[h][:, q0:q0 + QBS], ybs[:, ys, :], 'ybs%d' % ys,
                                                 reads=['ybs%d' % ys]))
                K.barrier()

        K.barrier()
        u2_d = nc.dram_tensor("u2_s", [16, 128, NT], BF16, kind="Internal").ap()
        with ExitStack() as pQ:
            QT = sb("QT", [128, 16, NT], BF16, pQ)
            with ExitStack() as pm:
                u1T = sb("u1Tb", [128, 16, NT], BF16, pm)
                build_u(u1T, pm, SC_M, SH_M, xo_d, "b")
                sg = sb("sg", [128, 2, 343], F32, pm)
                sgi = [0]

                def gated(c, banksA, banksG, accumulate):
                    for ti, (a, z) in enumerate(TT):
                        n = z - a
                        s_ = sgi[0] % 2
                        sgi[0] += 1
                        K.op('act', lambda e, b=banksG[ti], s_=s_, n=n: e.activation(out=sg[:, s_, 0:n],
                                                                                   in_=ps[b][:, 0:n], func=AF.Sigmoid),
                             reads=[PS(banksG[ti])], writes=['sg%d' % s_])
                        if not accumulate:
                            K.op('dve', lambda e, b=banksA[ti], s_=s_, n=n, a=a, z=z: e.tensor_tensor(
                                out=QT[:, c, a:z], in0=ps[b][:, 0:n], in1=sg[:, s_, 0:n], op=ALU.mult),
                                reads=[PS(banksA[ti]), 'sg%d' % s_], writes=['m_%d' % c])
                        else:
                            K.op('dve', lambda e, b=banksA[ti], s_=s_, n=n: e.tensor_tensor(
                                out=sg[:, s_, 0:n], in0=ps[b][:, 0:n], in1=sg[:, s_, 0:n], op=ALU.mult),
                                reads=[PS(banksA[ti]), 'sg%d' % s_], writes=['sg%d' % s_])
                            K.op('dve', lambda e, s_=s_, n=n, a=a, z=z: e.tensor_tensor(
                                out=QT[:, c, a:z], in0=sg[:, s_, 0:n], in1=QT[:, c, a:z], op=ALU.add),
                                reads=['sg%d' % s_, 'm_%d' % c], writes=['m_%d' % c])

                with ExitStack() as pm2:
                    ybT = sb("ybT", [128, 16, NT], BF16, pm2)
                    K.wait('sp', yb_toks)
                    for h in range(16):
                        K.op('pool', lambda e, h=h: e.memset(ybT[:, h, 0:2], 0.0), writes=['yb_%d' % h])
                    for h in range(16):
                        K.dma('sp', ybT[:, h, 2:NT], yb_d[h][:, 2:NT], 'ybl', writes=['yb_%d' % h])
                    for h in range(16):
                        K.res['yb_%d' % h][0] = ('ybl', K.cnt['ybl'])
                    for c in range(16):
                        bA = dense(ybT, lambda kc: 'yb_%d' % kc)
                        bG = dense(u1T, lambda kc: 'u_%d' % kc)
                        gated(c, bA, bG, False)
                    K.barrier()
                with ExitStack() as pm1:
                    yaT = sb("yaT", [128, 16, NT], BF16, pm1)
                    with ExitStack() as p2b:
                        tb_ = sb("tbuf", [128, 2, NT + 2], F32, p2b)
                        hs = sb("hs", [128, 2, 343], F32, p2b)
                        Bs = sb("Bs", [128, 2, NT], F32, p2b)
                        yc = sb("yc", [128, 2, NT], F32, p2b)
                        K.op('dve', lambda e: e.memset(tb_[:, :, 0:2], 0.0), writes=['t0', 't1'])
                        hi = 0
                        for c in range(16):
                            ts_ = c % 2
                            slC = next_w()
                            slH = next_w()
                            slB = next_w()
                            for ti, (a, z) in enumerate(TT):
                                n = z - a
                                bC = K.bank()
                                mm_group(bC, 128, n, [(wring[:, slC, kc, :], u1T[:, kc, a:z]) for kc in range(16)],
                                         [['w%d' % slC, 'u_%d' % kc] for kc in range(16)])
                                bH = K.bank()
                                mm_group(bH, 128, n, [(wring[:, slH, kc, :], u1T[:, kc, a:z]) for kc in range(16)],
                                         [['w%d' % slH, 'u_%d' % kc] for kc in range(16)])
                                h_ = hi % 2
                                hi += 1
                                K.op('act', lambda e, bH=bH, h_=h_, n=n: e.copy(out=hs[:, h_, 0:n], in_=ps[bH][:, 0:n]),
                                     reads=[PS(bH)], writes=['hs%d' % h_])
                                K.op('dve', lambda e, bC=bC, h_=h_, n=n, a=a, z=z, ts_=ts_: e.tensor_tensor(
                                    out=tb_[:, ts_, 2 + a:2 + z], in0=ps[bC][:, 0:n], in1=hs[:, h_, 0:n], op=ALU.mult),
                                    reads=[PS(bC), 'hs%d' % h_], writes=['t%d' % ts_])
                            for ti, (a, z) in enumerate(TT):
                                n = z - a
                                bB = K.bank()
                                mm_group(bB, 128, n, [(wring[:, slB, kc, :], u1T[:, kc, a:z]) for kc in range(16)],
                                         [['w%d' % slB, 'u_%d' % kc] for kc in range(16)])
                                K.op('act', lambda e, bB=bB, n=n, a=a, z=z, ts_=ts_: e.copy(out=Bs[:, ts_, a:z],
                                                                                          in_=ps[bB][:, 0:n]),
                                     reads=[PS(bB)], writes=['Bs%d' % ts_])
                            K.op('dve', lambda e, ts_=ts_: e.tensor_scalar(out=tb_[:, ts_, 2:2 + HALO],
                                                                           in0=tb_[:, ts_, 2:2 + HALO],
                                                                           scalar1=halom[:, 0:1], scalar2=None,
                                                                           op0=ALU.mult),
                                 reads=['t%d' % ts_, 'halom'], writes=['t%d' % ts_])
                            K.op('dve', lambda e, ts_=ts_, c=c: e.tensor_scalar(out=yc[:, ts_, :],
                                                                                in0=tb_[:, ts_, 2:NT + 2],
                                                                                scalar1=cva[:, c, 2:3], scalar2=None,
                                                                                op0=ALU.mult),
                                 reads=['t%d' % ts_, 'cva'], writes=['yc%d' % ts_])
                            for jj in (1, 0):
                                K.op('dve', lambda e, ts_=ts_, c=c, jj=jj: e.scalar_tensor_tensor(
                                    out=yc[:, ts_, :], in0=tb_[:, ts_, jj:NT + jj], scalar=cva[:, c, jj:jj + 1],
                                    in1=yc[:, ts_, :], op0=ALU.mult, op1=ALU.add),
                                    reads=['t%d' % ts_, 'cva', 'yc%d' % ts_], writes=['yc%d' % ts_])
                            K.op('pool', lambda e, ts_=ts_, c=c: e.tensor_tensor(out=yaT[:, c, :], in0=Bs[:, ts_, :],
                                                                                 in1=yc[:, ts_, :], op=ALU.mult),
                                 reads=['Bs%d' % ts_, 'yc%d' % ts_], writes=['ya_%d' % c])
                        K.barrier()
                    for c in range(16):
                        bA = dense(yaT, lambda kc: 'ya_%d' % kc)
                        bG = dense(u1T, lambda kc: 'u_%d' % kc)
                        gated(c, bA, bG, True)
                    K.barrier()

            def layernorm(zT, ntile, cols, gi, bi, scope, tag, post):
                sq = sb("sq" + tag, [128, 2, 343], F32, scope)
                mean = sb("mean" + tag, [128, 343], F32, scope)
                rstd = sb("rstd" + tag, [128, 343], F32, scope)
                tmp = sb("tmp" + tag, [128, 2, 343], F32, scope)
                for ti, (a, z) in ntile:
                    n = z - a
                    b1 = K.bank()
                    mm_group(b1, 128, n, [(onesf[:, :], zT[:, c, a:z]) for c in range(16)],
                             [['onesf', 'z_%d' % c] for c in range(16)])
                    b2 = K.bank()
                    for c in range(16):
                        s_ = c % 2
                        K.op('act', lambda e, s_=s_, c=c, n=n, a=a, z=z: e.activation(out=sq[:, s_, 0:n],
                                                                                    in_=zT[:, c, a:z], func=AF.Square),
                             reads=['z_%d' % c], writes=['sq%d' % s_])
                        K.op('pe', lambda e, s_=s_, c=c, n=n: e.matmul(ps[b2][:, 0:n], lhsT=onesf[:, :],
                                                                       rhs=sq[:, s_, 0:n], start=(c == 0),
                                                                       stop=(c == 15)),
                             reads=['onesf', 'sq%d' % s_], writes=[PS(b2)], signal=True)
                    K.op('dve', lambda e, n=n: e.tensor_scalar(out=mean[:, 0:n], in0=ps[b1][:, 0:n], scalar1=1.0 / D,
                                                               scalar2=None, op0=ALU.mult),
                         reads=[PS(b1)], writes=['mean'])
                    K.op('dve', lambda e, n=n: e.tensor_tensor(out=rstd[:, 0:n], in0=mean[:, 0:n], in1=mean[:, 0:n],
                                                               op=ALU.mult), reads=['mean'], writes=['rstd'])
                    K.op('dve', lambda e, n=n: e.scalar_tensor_tensor(out=rstd[:, 0:n], in0=ps[b2][:, 0:n],
                                                                      scalar=1.0 / D, in1=rstd[:, 0:n],
                                                                      op0=ALU.mult, op1=ALU.subtract),
                         reads=[PS(b2), 'rstd'], writes=['rstd'])
                    K.op('act', lambda e, n=n: e.activation(out=rstd[:, 0:n], in_=rstd[:, 0:n], func=AF.Sqrt,
                                                            bias=epst[:, 0:1], scale=1.0),
                         reads=['rstd', 'epst'], writes=['rstd'])
                    K.op('dve', lambda e, n=n: e.reciprocal(out=rstd[:, 0:n], in_=rstd[:, 0:n]),
                         reads=['rstd'], writes=['rstd'])
                    for c in range(16):
                        s_ = c % 2
                        eng = 'dve' if c % 2 == 0 else 'pool'
                        K.op(eng, lambda e, s_=s_, c=c, n=n, a=a, z=z: e.tensor_tensor(
                            out=tmp[:, s_, 0:n], in0=zT[:, c, a:z], in1=mean[:, 0:n], op=ALU.subtract),
                            reads=['z_%d' % c, 'mean'], writes=['tmp%d' % s_])
                        K.op(eng, lambda e, s_=s_, n=n: e.tensor_tensor(
                            out=tmp[:, s_, 0:n], in0=tmp[:, s_, 0:n], in1=rstd[:, 0:n], op=ALU.mult),
                            reads=['tmp%d' % s_, 'rstd'], writes=['tmp%d' % s_])
                        K.op('dve', lambda e, s_=s_, c=c, n=n, a=a, z=z: e.tensor_scalar(
                            out=zT[:, c, a:z], in0=tmp[:, s_, 0:n], scalar1=lnp[:, gi, c:c + 1],
                            scalar2=lnp[:, bi, c:c + 1], op0=ALU.mult, op1=ALU.add),
                            reads=['tmp%d' % s_, 'lnp'], writes=['z_%d' % c])
                        post(c, ti, a, z)

            x1_toks = []
            u2_toks = []
            with ExitStack() as pl:
                zT = sb("zT", [128, 16, NT], F32, pl)
                xr2 = sb("xr2", [128, 2, NT], F32, pl)
                u2s = sb("u2s", [128, 2, 343], BF16, pl)
                for c in range(16):
                    sl = c % 2
                    K.dma('sp', xr2[:, sl, :], xo_d[c], 'xq%d' % sl, writes=['xq%d' % sl])
                    K.op('act', lambda e, sl=sl: e.mul(out=xr2[:, sl, :], in_=xr2[:, sl, :], mul=ALPHA),
                         reads=['xq%d' % sl], writes=['xq%d' % sl])
                    banks = dense(QT, lambda kc: 'm_%d' % kc)
                    for ti, (a, z) in enumerate(TT):
                        n = z - a
                        K.op('dve', lambda e, b=banks[ti], n=n, a=a, z=z, c=c, sl=sl: e.scalar_tensor_tensor(
                            out=zT[:, c, a:z], in0=ps[b][:, 0:n], scalar=modP[:, G_M + c:G_M + c + 1],
                            in1=xr2[:, sl, a:z], op0=ALU.mult, op1=ALU.add),
                            reads=[PS(banks[ti]), 'modP', 'xq%d' % sl], writes=['z_%d' % c])
                u2i = [0]

                def post1(c, ti, a, z):
                    n = z - a
                    s_ = u2i[0] % 2
                    u2i[0] += 1
                    K.op('act', lambda e: e.activation(out=u2s[:, s_, 0:n], in_=zT[:, c, a:z], func=AF.Identity,
                                                       bias=modT[:, SH_F + c:SH_F + c + 1],
                                                       scale=modP[:, SC_F + c:SC_F + c + 1]),
                         reads=['z_%d' % c, 'modT', 'modP'], writes=['u2s%d' % s_])
                    u2_toks.append(K.dma('sp', u2_d[c][:, a:z], u2s[:, s_, 0:n], 'u2s%d' % s_, reads=['u2s%d' % s_]))
                    x1_toks.append(K.dma('sp', x1_d[c][:, a:z], zT[:, c, a:z], 'x1st', reads=['z_%d' % c]))
                layernorm(zT, list(enumerate(TT)), NT, 0, 1, pl, "1", post1)
                K.barrier()
            K.barrier()

        with ExitStack() as pf:
            gT = sb("gT", [128, NFC, NT], BF16, pf)
            with ExitStack() as pu:
                u2T = sb("u2T", [128, 16, NT], BF16, pu)
                araw = sb("araw", [128, 2, NT + 2], F32, pu)
                ac = sb("ac", [128, 2, NT], F32, pu)
                K.wait('sp', u2_toks)
                for c in range(16):
                    K.dma('sp', u2T[:, c, :], u2_d[c], 'u2l', writes=['u2_%d' % c])
                for c in range(16):
                    K.res['u2_%d' % c][0] = ('u2l', K.cnt['u2l'])
                K.op('dve', lambda e: e.memset(araw[:, :, 0:2], 0.0), writes=['ar0', 'ar1'])
                for j in range(NFC):
                    as_ = j % 2
                    bA = dense(u2T, lambda kc: 'u2_%d' % kc)
                    for ti, (a, z) in enumerate(TT):
                        K.op('act', lambda e, b=bA[ti], a=a, z=z, as_=as_: e.copy(out=araw[:, as_, 2 + a:2 + z],
                                                                                in_=ps[b][:, 0:z - a]),
                             reads=[PS(bA[ti])], writes=['ar%d' % as_])
                    K.op('dve', lambda e, as_=as_: e.tensor_scalar(out=araw[:, as_, 2:2 + HALO],
                                                                   in0=araw[:, as_, 2:2 + HALO],
                                                                   scalar1=halom[:, 0:1], scalar2=None, op0=ALU.mult),
                         reads=['ar%d' % as_, 'halom'], writes=['ar%d' % as_])
                    K.op('pool', lambda e, as_=as_, j=j: e.tensor_scalar(out=ac[:, as_, :], in0=araw[:, as_, 2:NT + 2],
                                                                         scalar1=cvf[:, j, 2:3], scalar2=None,
                                                                         op0=ALU.mult),
                         reads=['ar%d' % as_, 'cvf'], writes=['ac%d' % as_])
                    for jj in (1, 0):
                        K.op('dve', lambda e, as_=as_, j=j, jj=jj: e.scalar_tensor_tensor(
                            out=ac[:, as_, :], in0=araw[:, as_, jj:NT + jj], scalar=cvf[:, j, jj:jj + 1],
                            in1=ac[:, as_, :], op0=ALU.mult, op1=ALU.add),
                            reads=['ar%d' % as_, 'cvf', 'ac%d' % as_], writes=['ac%d' % as_])
                    K.op('act', lambda e, as_=as_: e.activation(out=ac[:, as_, :], in_=ac[:, as_, :],
                                                                func=AF.Gelu_apprx_tanh),
                         reads=['ac%d' % as_], writes=['ac%d' % as_])
                    bB = dense(u2T, lambda kc: 'u2_%d' % kc)
                    for ti, (a, z) in enumerate(TT):
                        K.op('dve', lambda e, b=bB[ti], a=a, z=z, as_=as_, j=j: e.tensor_tensor(
                            out=gT[:, j, a:z], in0=ps[b][:, 0:z - a], in1=ac[:, as_, a:z], op=ALU.mult),
                            reads=[PS(bB[ti]), 'ac%d' % as_], writes=['g_%d' % j])
                K.barrier()
            with ExitStack() as pd:
                z2 = sb("z2", [128, 16, 343], F32, pd)
                x1r = sb("x1r", [128, 2, 343], F32, pd)
                K.wait('sp', x1_toks)
                out_toks = []
                for ti, (a, z) in enumerate(TT):
                    n = z - a
                    for c in range(16):
                        sl = c % 2
                        K.dma('sp', x1r[:, sl, 0:n], x1_d[c][:, a:z], 'x1r%d' % sl, writes=['x1r%d' % sl])
                        K.op('act', lambda e, sl=sl, n=n: e.mul(out=x1r[:, sl, 0:n], in_=x1r[:, sl, 0:n], mul=ALPHA),
                             reads=['x1r%d' % sl], writes=['x1r%d' % sl])
                        b = K.bank()
                        pairs, rl = [], []
                        for part in range(3):
                            slot = next_w()
                            for kc in range(min(16, NFC - part * 16)):
                                pairs.append((wring[:, slot, kc, :], gT[:, part * 16 + kc, a:z]))
                                rl.append(['w%d' % slot, 'g_%d' % (part * 16 + kc)])
                        mm_group(b, 128, n, pairs, rl)
                        K.op('dve', lambda e, b=b, n=n, c=c, sl=sl: e.scalar_tensor_tensor(
                            out=z2[:, c, 0:n], in0=ps[b][:, 0:n], scalar=modP[:, G_F + c:G_F + c + 1],
                            in1=x1r[:, sl, 0:n], op0=ALU.mult, op1=ALU.add),
                            reads=[PS(b), 'modP', 'x1r%d' % sl], writes=['z_%d' % c])

                    def post2(c, ti_, a_, z_):
                        if c == 15:
                            out_toks.append(K.dma('sp', out_d[:, :, a:z].rearrange("c p t -> p c t"), z2[:, :, 0:n],
                                                  'outst', reads=['z_%d' % cc for cc in range(16)]))
                    with ExitStack() as pln:
                        layernorm(z2, [(ti, (0, n))], n, 2, 3, pln, "2_%d" % ti, post2)
                        K.barrier()
                K.wait('sp', [('outst', K.cnt['outst'])])
                if debug:
                    K.wait('sp', [('dbg', K.cnt['dbg'])])
    return nc


_NC_CACHE = {}


def _prep(inputs):
    f32 = np.float32
    x = np.asarray(inputs['x'], f32)
    c = np.asarray(inputs['c'], f32)
    W = {k: np.asarray(inputs[k][0], f32) for k in ('w_cond', 'w_in', 'w_a', 'w_b', 'w_o', 'w_up', 'w_down')}
    items, seq = stream_plan()

    def fm(Wm, col0, ncols, kc0=0, nkc=16):
        blk = Wm[kc0 * 128:(kc0 + nkc) * 128, col0:col0 + ncols].reshape(nkc, 128, ncols).transpose(1, 0, 2)
        if nkc < 16:
            blk = np.concatenate([blk, np.zeros((128, 16 - nkc, ncols), f32)], axis=1)
        return blk
    ws = np.empty((len(items), 128, 16, 128), f32)
    for i, (nm, col0, kc0, nkc) in enumerate(items):
        ws[i] = fm(W[nm], col0, 128, kc0, nkc)
    wcond = np.ascontiguousarray(W['w_cond'].reshape(16, 128, 24, 512).transpose(2, 1, 0, 3))
    bcond = np.asarray(inputs['b_cond'], f32).reshape(1, 12288)
    wk = np.ascontiguousarray(fm(W['w_in'], O_K, 512))
    wv = np.ascontiguousarray(fm(W['w_in'], O_V, 512))
    wki = np.ascontiguousarray(fm(W['w_in'], O_KI, 64))
    wwi = np.ascontiguousarray(fm(W['w_in'], O_WI, 16))

    def pv(v):
        return np.ascontiguousarray(np.asarray(v, f32).reshape(-1, 128).T)
    cva = np.ascontiguousarray(np.stack([pv(inputs['conv_a'][0][j]) for j in range(3)], axis=2))
    cvf = np.ascontiguousarray(np.stack([pv(inputs['conv_f'][0][j]) for j in range(3)], axis=2))
    lnp = np.ascontiguousarray(np.stack([pv(inputs['ln1_g'][0]), pv(inputs['ln1_b'][0]),
                                         pv(inputs['ln2_g'][0]), pv(inputs['ln2_b'][0])], axis=1))
    kng = np.ascontiguousarray(np.broadcast_to(np.stack([np.asarray(inputs['idx_kn_g'][0], f32),
                                                         np.asarray(inputs['idx_kn_b'][0], f32)])[None], (128, 2, 64)))
    cidx = np.ascontiguousarray(np.broadcast_to((np.arange(64, dtype=f32) * 64.0)[None], (128, 64)))
    ident = np.eye(128, dtype=f32)
    maps = []
    for core in range(8):
        b, q = core // 4, core % 4
        s0 = q * 1024
        g = s0 - HALO + np.arange(NT)
        xo = np.zeros((NT, D), f32)
        valid = g >= 0
        xo[valid] = x[b, g[valid]]
        xo = np.ascontiguousarray(xo.T.reshape(16, 128, NT))
        xs = np.ascontiguousarray(x[b].T.reshape(16, 128, 16, 256).transpose(2, 1, 0, 3))
        cT = np.ascontiguousarray(c[b].reshape(16, 128).T)
        gq = g[2:2 + NQT * QTS].reshape(NQT, QTS)
        lim = np.clip((np.floor_divide(gq, 64) + 1) * 64, 64, S).astype(f32)
        qlim = np.full((128, NQT), float(S), f32)
        qlim[:QTS, :] = lim.T
        halom = np.full((128, 1), 0.0 if q == 0 else 1.0, f32)
        m = dict(xo=xo, xs=xs, cT=cT, wcond=wcond, bcond=bcond, ws=ws, wk=wk, wv=wv, wki=wki, wwi=wwi,
                 cva=cva, cvf=cvf, lnp=lnp, kng=kng, qlim=qlim, halom=halom, cidx=cidx, ident=ident)
        maps.append({"i_" + k_: v_ for k_, v_ in m.items()})
    return maps


def _assemble(results):
    out = np.empty((2, S, D), np.float32)
    for core in range(8):
        b, q = core // 4, core % 4
        o = np.asarray(results[core]["out"], np.float32)
        out[b, q * 1024:(q + 1) * 1024, :] = o.reshape(D, NT)[:, HALO:].T
    return out


def kernel(**inputs):
    if 'nc' not in _NC_CACHE:
        _NC_CACHE['nc'] = build_nc(False)
    maps = _prep(inputs)
    res = run_bass_kernel_spmd(_NC_CACHE['nc'], maps, core_ids=list(range(8)))
    return _assemble(res.results)
```

```python
import numpy as np
import ml_dtypes
from contextlib import ExitStack
import concourse.bass as bass
import concourse.mybir as mybir
from concourse.bass_utils import run_bass_kernel_spmd

F32 = mybir.dt.float32
BF16 = mybir.dt.bfloat16
AF = mybir.ActivationFunctionType
ALU = mybir.AluOpType
AX = mybir.AxisListType

D = 2048
S = 4096
NT = 1028
HALO = 4
TT = [(0, 343), (343, 686), (686, 1028)]
NQT = 9
QTS = 114
NQB = 3
QBS = 342
DFF = 5632
NFC = 44
ALPHA = 2.0 ** 0.25
LN_EPS = 1e-5
IDX_W_SCALE = 1024.0 ** -0.5
TOPK = 256.0
BIG = 1.0e6
NBIS = 16
WR = 8
O_B, O_C, O_H, O_Q, O_K, O_V, O_QI, O_KI, O_WI, O_GA, O_GB = 0, 2048, 4096, 6144, 8192, 8704, 9216, 10240, 10304, 10320, 12368


def stream_plan():
    items, seq = [], []

    def add(*d):
        items.append(d)
        seq.append(len(items) - 1)
    for h in range(16):
        add('w_in', O_Q + h * 128, 0, 16)
    for c in range(8):
        add('w_in', O_QI + c * 128, 0, 16)
    for c in range(16):
        add('w_b', c * 128, 0, 16)
        add('w_in', O_GB + c * 128, 0, 16)
    for c in range(16):
        add('w_in', O_C + c * 128, 0, 16)
        add('w_in', O_H + c * 128, 0, 16)
        add('w_in', O_B + c * 128, 0, 16)
    for c in range(16):
        add('w_a', c * 128, 0, 16)
        add('w_in', O_GA + c * 128, 0, 16)
    for c in range(16):
        add('w_o', c * 128, 0, 16)
    for j in range(NFC):
        add('w_up', j * 128, 0, 16)
        add('w_up', DFF + j * 128, 0, 16)
    base = len(items)
    for c in range(16):
        for part in range(3):
            items.append(('w_down', c * 128, part * 16, min(16, NFC - part * 16)))
    for c in range(16):
        for part in range(3):
            seq.append(base + c * 3 + part)
    return items, seq


class KB:
    def __init__(self, nc, es):
        self.nc, self.es = nc, es
        self.engs = {'pe': nc.tensor, 'act': nc.scalar, 'dve': nc.vector, 'pool': nc.gpsimd, 'sp': nc.sync}
        self.sems = {k: es.enter_context(nc.semaphore('s_' + k)) for k in self.engs}
        self.cnt = {k: 0 for k in self.engs}
        self.waited = {}
        self.res = {}
        self.pbank = 0

    def dsem(self, name):
        if name not in self.sems:
            self.sems[name] = self.es.enter_context(self.nc.semaphore('d_' + name))
            self.cnt[name] = 0
        return self.sems[name]

    def _deps(self, reads, writes):
        deps = []
        for r in reads:
            st = self.res.get(r)
            if st and st[0]:
                deps.append(st[0])
        for w in writes:
            st = self.res.get(w)
            if st:
                if st[0]:
                    deps.append(st[0])
                deps.extend(st[1])
        return deps

    def wait(self, eng, deps):
        for (key, val) in deps:
            if key == eng and eng == 'pe':
                continue
            if self.waited.get((eng, key), 0) >= val:
                continue
            self.engs[eng].wait_ge(self.sems[key], val)
            self.waited[(eng, key)] = val

    def _upd(self, tok, reads, writes):
        for r in reads:
            st = self.res.setdefault(r, [None, []])
            st[1].append(tok)
        for w in writes:
            self.res[w] = [tok, []]

    def op(self, eng, fn, reads=(), writes=(), signal=True):
        self.wait(eng, self._deps(reads, writes))
        ins = fn(self.engs[eng])
        if signal:
            self.cnt[eng] += 1
            ins.then_inc(self.sems[eng], 1)
            tok = (eng, self.cnt[eng])
        else:
            tok = (eng, self.cnt[eng] + 1)
        self._upd(tok, reads, writes)
        return tok

    def dma(self, queue, out, in_, sem, reads=(), writes=()):
        self.dsem(sem)
        self.wait(queue, self._deps(reads, writes))
        ins = self.engs[queue].dma_start(out=out, in_=in_)
        self.cnt[sem] += 16
        ins.then_inc(self.sems[sem], 16)
        tok = (sem, self.cnt[sem])
        self._upd(tok, reads, writes)
        return tok

    def barrier(self):
        toks = [(k, v) for k, v in self.cnt.items() if v > 0]
        for e in ('pe', 'act', 'dve', 'pool', 'sp'):
            self.wait(e, toks)

    def bank(self):
        b = self.pbank
        self.pbank = (self.pbank + 1) % 8
        return b


def build_nc(debug=False):
    nc = bass.Bass("TRN2", target_bir_lowering=False)
    items, seq = stream_plan()
    NI = len(items)

    def din(name, shape, dt=F32):
        return nc.dram_tensor("i_" + name, shape, dt, kind="ExternalInput").ap()
    xo_d = din("xo", [16, 128, NT])
    xs_d = din("xs", [16, 128, 16, 256])
    cT_d = din("cT", [128, 16])
    wc_d = din("wcond", [24, 128, 16, 512])
    bc_d = din("bcond", [1, 12288])
    ws_d = din("ws", [NI, 128, 16, 128])
    wk_d = din("wk", [128, 16, 512])
    wv_d = din("wv", [128, 16, 512])
    wki_d = din("wki", [128, 16, 64])
    wwi_d = din("wwi", [128, 16, 16])
    cva_d = din("cva", [128, 16, 3])
    cvf_d = din("cvf", [128, NFC, 3])
    lnp_d = din("lnp", [128, 4, 16])
    kng_d = din("kng", [128, 2, 64])
    qlim_d = din("qlim", [128, NQT])
    halo_d = din("halom", [128, 1])
    cidx_d = din("cidx", [128, 64])
    ident_d = din("ident", [128, 128])
    okind = "ExternalOutput" if debug else "Internal"
    out_d = nc.dram_tensor("out", [16, 128, NT], F32, kind="ExternalOutput").ap()
    kT_d = nc.dram_tensor("kT_s", [4, 128, S], BF16, kind=okind).ap()
    v_d = nc.dram_tensor("v_s", [4, 128, 32, 128], BF16, kind=okind).ap()
    yb_d = nc.dram_tensor("yb_s", [16, 128, NT], BF16, kind=okind).ap()
    x1_d = nc.dram_tensor("x1_s", [16, 128, NT], F32, kind=okind).ap()
    mod_o = nc.dram_tensor("mod_o", [128, 96], F32, kind=okind).ap()

    with ExitStack() as es:
        K = KB(nc, es)

        def sb(name, shape, dt=F32, scope=es):
            return scope.enter_context(nc.sbuf_tensor(name, shape, dt))
        ps = [es.enter_context(nc.psum_tensor("ps%d" % i, [128, 512], F32)) for i in range(8)]

        def PS(b):
            return "ps%d" % b

        ident = sb("ident", [128, 128], BF16)
        identf = sb("identf", [128, 128], F32)
        onesb = sb("onesb", [128, 128], BF16)
        onesf = sb("onesf", [128, 128], F32)
        one11 = sb("one11", [1, 2], F32)
        epst = sb("epst", [128, 1], F32)
        cva = sb("cva", [128, 16, 3])
        cvf = sb("cvf", [128, NFC, 3])
        lnp = sb("lnp", [128, 4, 16])
        kng = sb("kng", [128, 2, 64])
        qlim = sb("qlim", [128, NQT])
        halom = sb("halom", [128, 1])
        cidx = sb("cidx", [128, 64])
        modT = sb("modT", [128, 96])
        modP = sb("modP", [128, 96])
        wring = sb("wring", [128, WR, 16, 128], BF16)
        pk = ExitStack()
        kiTE = sb("kiTE", [128, S], BF16, pk)
        kiTO = sb("kiTO", [128, S], BF16, pk)
        for (t, d_) in ((identf, ident_d), (cva, cva_d), (cvf, cvf_d), (lnp, lnp_d), (kng, kng_d), (qlim, qlim_d),
                        (halom, halo_d), (cidx, cidx_d)):
            K.dma('sp', t[:], d_, 'c_' + t.name, writes=[t.name])
        K.op('dve', lambda e: e.tensor_copy(out=ident[:], in_=identf[:]), reads=['identf'], writes=['ident'])
        K.op('dve', lambda e: e.memset(onesb[:], 1.0), writes=['onesb'])
        K.op('dve', lambda e: e.memset(onesf[:], 1.0), writes=['onesf'])
        K.op('dve', lambda e: e.memset(one11[:], 1.0), writes=['one11'])
        K.op('dve', lambda e: e.memset(epst[:], LN_EPS), writes=['epst'])

        wst = {'pos': 0, 'loaded': 0}

        def w_prefetch(upto):
            while wst['loaded'] < min(upto, len(seq)):
                i = wst['loaded']
                slot = i % WR
                K.dma('pool', wring[:, slot, :, :], ws_d[seq[i]], 'w%d' % slot, writes=['w%d' % slot])
                wst['loaded'] += 1

        def next_w():
            i = wst['pos']
            w_prefetch(i + 1)
            wst['pos'] += 1
            w_prefetch(i + WR - 2)
            return i % WR

        def mm_group(bank, M, n, pairs, reads_list, moff=0):
            last = len(pairs) - 1
            for i, (l, r) in enumerate(pairs):
                K.op('pe', lambda e, l=l, r=r, i=i: e.matmul(ps[bank][moff:moff + M, 0:n], lhsT=l, rhs=r,
                                                              start=(i == 0), stop=(i == last)),
                     reads=reads_list[i], writes=[PS(bank)], signal=(i == last))

        with ExitStack() as p0:
            cTs = sb("cTs", [128, 16], F32, p0)
            cact = sb("cact", [128, 16], BF16, p0)
            wcb = sb("wcb", [128, 2, 16, 512], BF16, p0)
            modrow = sb("modrow", [1, 12288], F32, p0)
            brow = sb("brow", [1, 12288], F32, p0)
            K.dma('sp', cTs[:], cT_d, 'c_cTs', writes=['cTs'])
            K.dma('sp', brow[:], bc_d, 'c_brow', writes=['brow'])
            K.op('act', lambda e: e.activation(out=cact[:], in_=cTs[:], func=AF.Silu), reads=['cTs'], writes=['cact'])
            for nb in range(24):
                sl = nb % 2
                K.dma('pool', wcb[:, sl, :, :], wc_d[nb], 'wc%d' % sl, writes=['wc%d' % sl])
                b = K.bank()
                mm_group(b, 1, 512, [(cact[:, kc:kc + 1], wcb[:, sl, kc, :]) for kc in range(16)],
                         [['cact', 'wc%d' % sl]] * 16)
                K.op('dve', lambda e, b=b, nb=nb: e.tensor_tensor(out=modrow[0:1, nb * 512:(nb + 1) * 512],
                                                                   in0=ps[b][0:1, 0:512],
                                                                   in1=brow[0:1, nb * 512:(nb + 1) * 512], op=ALU.add),
                     reads=[PS(b), 'brow'], writes=['modrow'])
            w_prefetch(WR - 2)
            b = K.bank()
            for c in range(96):
                K.op('pe', lambda e, c=c: e.matmul(ps[b][:, c:c + 1], lhsT=modrow[0:1, c * 128:(c + 1) * 128],
                                                   rhs=one11[0:1, 0:1], start=True, stop=True),
                     reads=['modrow', 'one11'], writes=[PS(b)], signal=(c == 95))
            K.op('dve', lambda e: e.tensor_copy(out=modT[:], in_=ps[b][:, 0:96]), reads=[PS(b)], writes=['modT'])
            K.op('dve', lambda e: e.tensor_scalar(out=modP[:], in0=modT[:], scalar1=1.0, scalar2=None, op0=ALU.add),
                 reads=['modT'], writes=['modP'])
            K.barrier()
        if debug:
            K.dma('sp', mod_o, modT[:], 'dbg', reads=['modT'])
        SH_M, SC_M, G_M, SH_F, SC_F, G_F = 0, 16, 32, 48, 64, 80

        def modulate(eng_i, out_ap, in_ap, kc, scb, shb, reads, writes):
            if eng_i % 2 == 0:
                K.op('dve', lambda e: e.tensor_scalar(out=out_ap, in0=in_ap, scalar1=modP[:, scb + kc:scb + kc + 1],
                                                      scalar2=modT[:, shb + kc:shb + kc + 1], op0=ALU.mult, op1=ALU.add),
                     reads=reads + ['modT', 'modP'], writes=writes)
            else:
                K.op('act', lambda e: e.activation(out=out_ap, in_=in_ap, func=AF.Identity,
                                                   bias=modT[:, shb + kc:shb + kc + 1],
                                                   scale=modP[:, scb + kc:scb + kc + 1]),
                     reads=reads + ['modT', 'modP'], writes=writes)

        store_toks = []
        with ExitStack() as p1:
            wk = sb("wk", [128, 16, 512], BF16, p1)
            wv = sb("wv", [128, 16, 512], BF16, p1)
            wki = sb("wki", [128, 16, 64], BF16, p1)
            xsb = sb("xsb", [128, 2, 16, 256], F32, p1)
            usb = sb("usb", [128, 2, 16, 256], BF16, p1)
            kst = sb("kst", [128, 2, 4, 256], BF16, p1)
            vst = sb("vst", [128, 2, 2, 512], BF16, p1)
            kraw = sb("kraw", [128, 2, 64], F32, p1)
            kn2 = sb("kn2", [128, 2, 2, 128], BF16, p1)
            K.op('dve', lambda e: e.memset(kn2[:], 0.0), writes=['ki0n', 'ki1n'])
            bst = sb("bst", [128, 2, 6], F32, p1)
            bmv = sb("bmv", [128, 2, 2], F32, p1)
            K.dma('pool', wk[:], wk_d, 'c_wk', writes=['wk'])
            K.dma('pool', wv[:], wv_d, 'c_wv', writes=['wv'])
            K.dma('pool', wki[:], wki_d, 'c_wki', writes=['wki'])
            K.dma('sp', xsb[:, 0, :, :], xs_d[0], 'xs0', writes=['xs0'])
            for tb in range(16):
                sl = tb % 2
                if tb + 1 < 16:
                    K.dma('sp', xsb[:, 1 - sl, :, :], xs_d[tb + 1], 'xs%d' % (1 - sl), writes=['xs%d' % (1 - sl)])
                for kc in range(16):
                    modulate(kc, usb[:, sl, kc, :], xsb[:, sl, kc, :], kc, SC_M, SH_M,
                             ['xs%d' % sl], ['us%d_%d' % (sl, kc)])
                for n in range(4):
                    b = K.bank()
                    mm_group(b, 128, 256, [(wk[:, kc, n * 128:(n + 1) * 128], usb[:, sl, kc, :]) for kc in range(16)],
                             [['wk', 'us%d_%d' % (sl, kc)] for kc in range(16)])
                    K.op('act', lambda e, b=b, n=n: e.copy(out=kst[:, sl, n, :], in_=ps[b][:, 0:256]),
                         reads=[PS(b)], writes=['kst%d' % sl])
                store_toks.append(K.dma('sp', kT_d[:, :, tb * 256:(tb + 1) * 256].rearrange("n p t -> p n t"),
                                        kst[:, sl, :, :], 'kst%d' % sl, reads=['kst%d' % sl]))
                for sub in range(2):
                    tsl = slice(sub * 128, (sub + 1) * 128)
                    b = K.bank()
                    mm_group(b, 128, 512, [(usb[:, sl, kc, tsl], wv[:, kc, :]) for kc in range(16)],
                             [['wv', 'us%d_%d' % (sl, kc)] for kc in range(16)])
                    K.op('dve', lambda e, b=b, sub=sub: e.tensor_copy(out=vst[:, sl, sub, :], in_=ps[b][:, 0:512]),
                         reads=[PS(b)], writes=['vst%d' % sl])
                    b = K.bank()
                    mm_group(b, 128, 64, [(usb[:, sl, kc, tsl], wki[:, kc, :]) for kc in range(16)],
                             [['wki', 'us%d_%d' % (sl, kc)] for kc in range(16)])
                    r_ = 'ki%d' % sub
                    K.op('act', lambda e, b=b, sub=sub: e.copy(out=kraw[:, sub, :], in_=ps[b][:, 0:64]),
                         reads=[PS(b)], writes=[r_])
                    K.op('dve', lambda e, sub=sub: e.bn_stats(out=bst[:, sub, :], in_=kraw[:, sub, :]),
                         reads=[r_], writes=[r_ + 's'])
                    K.op('dve', lambda e, sub=sub: e.bn_aggr(out=bmv[:, sub, :], in_=bst[:, sub, :]),
                         reads=[r_ + 's'], writes=[r_ + 'm'])
                    K.op('act', lambda e, sub=sub: e.activation(out=bmv[:, sub, 1:2], in_=bmv[:, sub, 1:2], func=AF.Sqrt,
                                                                bias=epst[:, 0:1], scale=1.0),
                         reads=[r_ + 'm', 'epst'], writes=[r_ + 'm'])
                    K.op('dve', lambda e, sub=sub: e.reciprocal(out=bmv[:, sub, 1:2], in_=bmv[:, sub, 1:2]),
                         reads=[r_ + 'm'], writes=[r_ + 'm'])
                    K.op('dve', lambda e, sub=sub: e.tensor_scalar(out=kraw[:, sub, :], in0=kraw[:, sub, :],
                                                                   scalar1=bmv[:, sub, 0:1], scalar2=bmv[:, sub, 1:2],
                                                                   op0=ALU.subtract, op1=ALU.mult),
                         reads=[r_, r_ + 'm'], writes=[r_])
                    K.op('dve', lambda e, sub=sub: e.tensor_tensor(out=kraw[:, sub, :], in0=kraw[:, sub, :],
                                                                   in1=kng[:, 0, :], op=ALU.mult),
                         reads=[r_, 'kng'], writes=[r_])
                    K.op('dve', lambda e, sub=sub: e.tensor_tensor(out=kn2[:, sub, 0, 0:64], in0=kraw[:, sub, :],
                                                                   in1=kng[:, 1, :], op=ALU.add),
                         reads=[r_, 'kng'], writes=[r_ + 'n'])
                    K.op('dve', lambda e, sub=sub: e.tensor_copy(out=kn2[:, sub, 1, 64:128], in_=kn2[:, sub, 0, 0:64]),
                         reads=[r_ + 'n'], writes=[r_ + 'n'])
                    b = K.bank()
                    pT = ps[b][:].bitcast(BF16)
                    for eo in range(2):
                        K.op('pe', lambda e, pT=pT, sub=sub, eo=eo: e.transpose(out=pT[:, eo * 128:(eo + 1) * 128],
                                                                                in_=kn2[:, sub, eo, :],
                                                                                identity=ident[:, :]),
                             reads=[r_ + 'n', 'ident'], writes=[PS(b)], signal=(eo == 1))
                    t0 = tb * 256 + sub * 128
                    K.op('act', lambda e, pT=pT, t0=t0: e.copy(out=kiTE[:, t0:t0 + 128], in_=pT[:, 0:128]),
                         reads=[PS(b)], writes=['kiT'])
                    K.op('act', lambda e, pT=pT, t0=t0: e.copy(out=kiTO[:, t0:t0 + 128], in_=pT[:, 128:256]),
                         reads=[PS(b)], writes=['kiT'])
                for n in range(4):
                    store_toks.append(K.dma('sp', v_d[n][:, tb * 2:tb * 2 + 2, :], vst[:, sl, :, n * 128:(n + 1) * 128],
                                            'vst%d' % sl, reads=['vst%d' % sl]))
            K.barrier()

        def build_u(uT, scope, scb, shb, src_d, tag):
            xr = sb("xr_" + tag, [128, 2, NT], F32, scope)
            for kc in range(16):
                sl = kc % 2
                K.dma('sp', xr[:, sl, :], src_d[kc], 'xr%d' % sl, writes=['xr%d' % sl])
                modulate(kc, uT[:, kc, :], xr[:, sl, :], kc, scb, shb, ['xr%d' % sl], ['u_%d' % kc])

        def dense(rhsT, rhs_res, nkc=16):
            slot = next_w()
            banks = []
            for (a, z) in TT:
                b = K.bank()
                mm_group(b, 128, z - a, [(wring[:, slot, kc, :], rhsT[:, kc, a:z]) for kc in range(nkc)],
                         [['w%d' % slot, rhs_res(kc)] for kc in range(nkc)])
                banks.append(b)
            return banks

        with ExitStack() as pA:
            qT = sb("qT", [128, 16, NT], BF16, pA)
            qiT = sb("qiT", [128, 8, NT], BF16, pA)
            wtm = sb("wtm", [128, NQT, 16], F32, pA)
            with ExitStack() as p2a:
                u1T = sb("u1T", [128, 16, NT], BF16, p2a)
                wwi = sb("wwi", [128, 16, 16], BF16, p2a)
                K.dma('pool', wwi[:], wwi_d, 'c_wwi', writes=['wwi'])
                build_u(u1T, p2a, SC_M, SH_M, xo_d, "a")
                for h in range(24):
                    banks = dense(u1T, lambda kc: 'u_%d' % kc)
                    for ti, (a, z) in enumerate(TT):
                        b = banks[ti]
                        dst = qT[:, h, a:z] if h < 16 else qiT[:, h - 16, a:z]
                        dres = 'q_%d' % h
                        if (h + ti) % 2 == 0:
                            K.op('act', lambda e, b=b, dst=dst, n=z - a: e.copy(out=dst, in_=ps[b][:, 0:n]),
                                 reads=[PS(b)], writes=[dres])
                        else:
                            K.op('dve', lambda e, b=b, dst=dst, n=z - a: e.tensor_copy(out=dst, in_=ps[b][:, 0:n]),
                                 reads=[PS(b)], writes=[dres])
                for qt in range(NQT):
                    j0 = 2 + qt * QTS
                    b = K.bank()
                    mm_group(b, QTS, 16, [(u1T[:, kc, j0:j0 + QTS], wwi[:, kc, :]) for kc in range(16)],
                             [['wwi', 'u_%d' % kc] for kc in range(16)])
                    K.op('act', lambda e, b=b, qt=qt: e.mul(out=wtm[0:QTS, qt, :], in_=ps[b][0:QTS, 0:16],
                                                            mul=IDX_W_SCALE),
                         reads=[PS(b)], writes=['wtm'])
                K.barrier()

            with ExitStack() as pt:
                score = sb("score", [128, 2, S], F32, pt)
                dg = sb("dg", [128, 16, QTS], BF16, pt)
                rb = sb("rb", [128, 4, 512], BF16, pt)
                mask = sb("mask", [128, S], BF16, pt)
                maskT = sb("maskT", [128, 32, QBS], BF16, pt)
                pen = sb("pen", [128, 2, 64], F32, pt)
                bis = sb("bis", [128, 8], F32, pt)
                nst = sb("nst", [128, NBIS], F32, pt)
                pwn = sb("pwn", [128, NBIS], F32, pt)
                c35 = sb("c35", [128, 1], F32, pt)
                knb = sb("knb", [128, 2, S], BF16, pt)
                vnb = sb("vnb", [128, 32, 128], BF16, pt)
                Eb = sb("Eb", [128, 4, QBS], BF16, pt)
                Em = sb("Em", [128, 4, QBS], BF16, pt)
                rz = sb("rz", [128, QBS], F32, pt)
                ybs = sb("ybs", [128, 2, QBS], BF16, pt)
                for k_ in range(NBIS):
                    K.op('dve', lambda e, k_=k_: e.memset(pwn[:, k_:k_ + 1], -(2.0 ** -(k_ + 1))), writes=['pwn'])
                K.op('dve', lambda e: e.memset(c35[:], float(S) - 2.0 * TOPK + 0.5), writes=['c35'])
                K.wait('sp', store_toks)
                kvload = [0]
                kvslot = {}
                yb_toks = []
                LOOK = 3
                SBK = [4, 5, 6, 7]
                P = slice(0, QTS)
                uctr = [0]

                def emit_acc(qt):
                    sb_ = qt % 2
                    j0 = 2 + qt * QTS
                    K.op('dve', lambda e: e.tensor_scalar(out=pen[P, sb_, :], in0=cidx[P, :],
                                                          scalar1=qlim[P, qt:qt + 1], scalar2=-BIG,
                                                          op0=ALU.is_ge, op1=ALU.mult),
                         reads=['cidx', 'qlim'], writes=['pen%d' % sb_])
                    for h in range(16):
                        K.op('dve', lambda e, h=h: e.tensor_scalar(out=dg[P, h, :], in0=ident[P, 0:QTS],
                                                                   scalar1=wtm[P, qt, h:h + 1], scalar2=None,
                                                                   op0=ALU.mult),
                             reads=['ident', 'wtm'], writes=['dg%d' % h])
                    for kb in range(8):
                        ks = slice(kb * 512, (kb + 1) * 512)
                        sbank = kb % 2
                        rings = {}

                        def emit_D(h):
                            u = uctr[0]
                            uctr[0] += 1
                            bank = 2 + (u % 6)
                            ring = u % 4
                            kz = kiTE if h % 2 == 0 else kiTO
                            K.op('pe', lambda e: e.matmul(ps[bank][P, 0:512], lhsT=qiT[:, h // 2, j0:j0 + QTS],
                                                          rhs=kz[:, ks], start=True, stop=True),
                                 reads=['q_%d' % (16 + h // 2), 'kiT'], writes=[PS(bank)])
                            if False:
                                K.op('act', lambda e: e.activation(out=rb[P, ring, :], in_=ps[bank][P, 0:512],
                                                                   func=AF.Relu),
                                     reads=[PS(bank)], writes=['rb%d' % ring])
                            else:
                                K.op('dve', lambda e: e.tensor_scalar(out=rb[P, ring, :], in0=ps[bank][P, 0:512],
                                                                      scalar1=0.0, scalar2=None, op0=ALU.max),
                                     reads=[PS(bank)], writes=['rb%d' % ring])
                            rings[h] = ring
                        for h in range(2):
                            emit_D(h)
                        for h in range(16):
                            if h + 2 < 16:
                                emit_D(h + 2)
                            K.op('pe', lambda e, h=h, r_=rings[h]: e.matmul(
                                ps[sbank][P, 0:512], lhsT=dg[P, h, :], rhs=rb[P, r_, :],
                                start=(h == 0), stop=(h == 15)),
                                reads=['dg%d' % h, 'rb%d' % rings[h]], writes=[PS(sbank)], signal=(h == 15))
                        K.op('dve', lambda e: e.tensor_tensor(
                            out=score[P, sb_, ks].rearrange("p (c j) -> p c j", j=64),
                            in0=ps[sbank][P, 0:512].rearrange("p (c j) -> p c j", j=64),
                            in1=pen[P, sb_, kb * 8:(kb + 1) * 8].unsqueeze(2).to_broadcast([QTS, 8, 64]), op=ALU.add),
                            reads=[PS(sbank), 'pen%d' % sb_], writes=['sc%d_%d' % (sb_, kb)])

                def emit_bis(qt):
                    sb_ = qt % 2
                    qi3 = qt % 3
                    allsc = ['sc%d_%d' % (sb_, kb) for kb in range(8)]
                    K.op('dve', lambda e: e.tensor_reduce(out=bis[P, 5:6], in_=score[P, sb_, :], axis=AX.X, op=ALU.max),
                         reads=allsc, writes=['bisA'])
                    K.op('dve', lambda e: e.tensor_reduce(out=bis[P, 6:7], in_=score[P, sb_, 0:256], axis=AX.X,
                                                          op=ALU.min), reads=allsc, writes=['bisB'])
                    K.op('dve', lambda e: e.tensor_scalar(out=bis[P, 6:7], in0=bis[P, 6:7], scalar1=-1000.0,
                                                          scalar2=None, op0=ALU.max), reads=['bisB'], writes=['bisB'])
                    K.op('dve', lambda e: e.scalar_tensor_tensor(out=bis[P, 0:1], in0=bis[P, 5:6], scalar=-0.5,
                                                                 in1=bis[P, 6:7], op0=ALU.mult, op1=ALU.subtract),
                         reads=['bisA', 'bisB'], writes=['bisN'])
                    K.op('dve', lambda e: e.scalar_tensor_tensor(out=bis[P, 0:1], in0=bis[P, 6:7], scalar=0.5,
                                                                 in1=bis[P, 0:1], op0=ALU.mult, op1=ALU.add),
                         reads=['bisB', 'bisN'], writes=['bisN'])
                    K.op('dve', lambda e: e.tensor_tensor(out=bis[P, 1:2], in0=bis[P, 5:6], in1=bis[P, 6:7],
                                                          op=ALU.subtract), reads=['bisA', 'bisB'], writes=['bisH'])
                    K.op('dve', lambda e: e.tensor_scalar(out=bis[P, 1:2], in0=bis[P, 1:2], scalar1=0.5,
                                                          scalar2=1e-3, op0=ALU.mult, op1=ALU.add),
                         reads=['bisH'], writes=['bisH'])
                    K.op('dve', lambda e: e.tensor_scalar(out=nst[P, :], in0=pwn[P, :], scalar1=bis[P, 1:2],
                                                          scalar2=None, op0=ALU.mult),
                         reads=['bisH', 'pwn'], writes=['nst'])
                    K.op('dve', lambda e: e.tensor_scalar(out=bis[P, 2:3], in0=bis[P, 1:2],
                                                          scalar1=2.0 ** -NBIS, scalar2=None, op0=ALU.mult),
                         reads=['bisH'], writes=['bisL'])
                    for it in range(NBIS):
                        K.op('act', lambda e: e.activation(out=mask[P, :], in_=score[P, sb_, :], func=AF.Sign,
                                                           bias=bis[P, 0:1], scale=1.0, accum_out=bis[P, 3:4]),
                             reads=allsc + ['bisN'], writes=['mask', 'bisC'])
                        K.op('act', lambda e: e.activation(out=bis[P, 4:5], in_=bis[P, 3:4], func=AF.Sign,
                                                           bias=c35[P, 0:1], scale=1.0),
                             reads=['bisC', 'c35'], writes=['bisS'])
                        K.op('act', lambda e, it=it: e.activation(out=bis[P, 0:1], in_=bis[P, 4:5],
                                                                  func=AF.Identity, bias=bis[P, 0:1],
                                                                  scale=nst[P, it:it + 1]),
                             reads=['bisS', 'nst', 'bisN'], writes=['bisN'])

                def emit_fin(qt):
                    sb_ = qt % 2
                    qi3 = qt % 3
                    allsc = ['sc%d_%d' % (sb_, kb) for kb in range(8)]
                    K.op('dve', lambda e: e.scalar_tensor_tensor(out=bis[P, 7:8], in0=bis[P, 0:1], scalar=-1.0,
                                                                 in1=bis[P, 2:3], op0=ALU.mult, op1=ALU.subtract),
                         reads=['bisN', 'bisL'], writes=['bisT'])
                    K.op('dve', lambda e: e.tensor_scalar(out=mask[P, :], in0=score[P, sb_, :], scalar1=bis[P, 7:8],
                                                          scalar2=None, op0=ALU.is_ge),
                         reads=allsc + ['bisT'], writes=['mask'])
                    for g in range(8):
                        b = K.bank()
                        pT = ps[b][:].bitcast(BF16)
                        for k4 in range(4):
                            kt = g * 4 + k4
                            K.op('pe', lambda e, pT=pT, k4=k4, kt=kt: e.transpose(
                                out=pT[:, k4 * QTS:(k4 + 1) * QTS], in_=mask[P, kt * 128:(kt + 1) * 128],
                                identity=ident[P, 0:QTS]),
                                reads=['mask', 'ident'], writes=[PS(b)], signal=(k4 == 3))
                        src = pT[:, 0:4 * QTS].rearrange("p (k q) -> p k q", q=QTS)
                        dst = maskT[:, g * 4:(g + 1) * 4, qi3 * QTS:(qi3 + 1) * QTS]
                        if g % 2 == 0:
                            K.op('act', lambda e, src=src, dst=dst: e.copy(out=dst, in_=src),
                                 reads=[PS(b)], writes=['mT%d' % qi3])
                        else:
                            K.op('dve', lambda e, src=src, dst=dst: e.tensor_copy(out=dst, in_=src),
                                 reads=[PS(b)], writes=['mT%d' % qi3])

                emit_acc(0)
                for qb in range(NQB):
                    q0 = 2 + qb * QBS
                    for qi3 in range(3):
                        qt = qb * 3 + qi3
                        emit_bis(qt)
                        if qt + 1 < NQT:
                            emit_acc(qt + 1)
                        emit_fin(qt)
                    its = [(h, kt) for h in range(16) for kt in range(32)]

                    def emit_S(i, q0=q0):
                        h, kt = its[i]
                        n = h // 4
                        if kt == 0 and h % 4 == 0:
                            ksl = kvload[0] % 2
                            kvload[0] += 1
                            kvslot[n] = ksl
                            K.dma('sp', knb[:, ksl, :], kT_d[n], 'kn%d' % ksl, writes=['kn%d' % ksl])
                        ksl = kvslot[n]
                        bs_ = SBK[i % 4]
                        es_ = i % 4
                        K.op('pe', lambda e: e.matmul(ps[bs_][:, 0:QBS], lhsT=knb[:, ksl, kt * 128:(kt + 1) * 128],
                                                      rhs=qT[:, h, q0:q0 + QBS], start=True, stop=True),
                             reads=['kn%d' % ksl, 'q_%d' % h], writes=[PS(bs_)])
                        K.op('act', lambda e: e.activation(out=Eb[:, es_, :], in_=ps[bs_][:, 0:QBS], func=AF.Exp,
                                                           scale=128.0 ** -0.5),
                             reads=[PS(bs_)], writes=['E%d' % es_])
                        K.op('dve', lambda e: e.tensor_tensor(out=Em[:, es_, :], in0=Eb[:, es_, :],
                                                              in1=maskT[:, kt, :], op=ALU.mult),
                             reads=['E%d' % es_, 'mT0', 'mT1', 'mT2'], writes=['Em%d' % es_])
                    for i in range(LOOK):
                        emit_S(i)
                    for i in range(len(its)):
                        if i + LOOK < len(its):
                            emit_S(i + LOOK)
                        h, kt = its[i]
                        n = h // 4
                        ksl = kvslot[n]
                        es_ = i % 4
                        bo, bz = (0, 1) if h % 2 == 0 else (2, 3)
                        if kt == 0 and h % 4 == 0:
                            K.dma('sp', vnb[:, :, :], v_d[n], 'vn0', writes=['vn0'])
                        K.op('pe', lambda e, bo=bo, kt=kt, ksl=ksl, es_=es_: e.matmul(
                            ps[bo][:, 0:QBS], lhsT=vnb[:, kt, :], rhs=Em[:, es_, :],
                            start=(kt == 0), stop=(kt == 31)),
                            reads=['vn0', 'Em%d' % es_], writes=[PS(bo)], signal=False)
                        K.op('pe', lambda e, bz=bz, kt=kt, es_=es_: e.matmul(
                            ps[bz][:, 0:QBS], lhsT=onesb[:, :], rhs=Em[:, es_, :],
                            start=(kt == 0), stop=(kt == 31)),
                            reads=['onesb', 'Em%d' % es_], writes=[PS(bz)], signal=True)
                        if kt == 31:
                            K.op('dve', lambda e, bz=bz: e.reciprocal(out=rz[:, :], in_=ps[bz][:, 0:QBS]),
                                 reads=[PS(bz)], writes=['rz'])
                            ys = h % 2
                            K.op('dve', lambda e, bo=bo, ys=ys: e.tensor_tensor(out=ybs[:, ys, :], in0=ps[bo][:, 0:QBS],
                                                                                in1=rz[:, :], op=ALU.mult),
                                 reads=[PS(bo), 'rz'], writes=['ybs%d' % ys])
                            yb_toks.append(K.dma('sp', yb_d[h][:, q0:q0 + QBS], ybs[:, ys, :], 'ybs%d' % ys,
                                                 reads=['ybs%d' % ys]))
                K.barrier()

        K.barrier()
        pk.close()
        u2_d = nc.dram_tensor("u2_s", [16, 128, NT], BF16, kind="Internal").ap()
        with ExitStack() as pQ:
            QT = sb("QT", [128, 16, NT], BF16, pQ)
            with ExitStack() as pm:
                u1T = sb("u1Tb", [128, 16, NT], BF16, pm)
                build_u(u1T, pm, SC_M, SH_M, xo_d, "b")
                sg = sb("sg", [128, 2, 343], F32, pm)
                sgi = [0]

                def gated(c, banksA, banksG, accumulate):
                    for ti, (a, z) in enumerate(TT):
                        n = z - a
                        s_ = sgi[0] % 2
                        sgi[0] += 1
                        K.op('act', lambda e, b=banksG[ti], s_=s_, n=n: e.activation(out=sg[:, s_, 0:n],
                                                                                   in_=ps[b][:, 0:n], func=AF.Sigmoid),
                             reads=[PS(banksG[ti])], writes=['sg%d' % s_])
                        if not accumulate:
                            K.op('dve', lambda e, b=banksA[ti], s_=s_, n=n, a=a, z=z: e.tensor_tensor(
                                out=QT[:, c, a:z], in0=ps[b][:, 0:n], in1=sg[:, s_, 0:n], op=ALU.mult),
                                reads=[PS(banksA[ti]), 'sg%d' % s_], writes=['m_%d' % c])
                        else:
                            K.op('dve', lambda e, b=banksA[ti], s_=s_, n=n: e.tensor_tensor(
                                out=sg[:, s_, 0:n], in0=ps[b][:, 0:n], in1=sg[:, s_, 0:n], op=ALU.mult),
                                reads=[PS(banksA[ti]), 'sg%d' % s_], writes=['sg%d' % s_])
                            K.op('dve', lambda e, s_=s_, n=n, a=a, z=z: e.tensor_tensor(
                                out=QT[:, c, a:z], in0=sg[:, s_, 0:n], in1=QT[:, c, a:z], op=ALU.add),
                                reads=['sg%d' % s_, 'm_%d' % c], writes=['m_%d' % c])

                with ExitStack() as pm2:
                    ybT = sb("ybT", [128, 16, NT], BF16, pm2)
                    K.wait('sp', yb_toks)
                    for h in range(16):
                        K.op('pool', lambda e, h=h: e.memset(ybT[:, h, 0:2], 0.0), writes=['yb_%d' % h])
                    for h in range(16):
                        K.dma('sp', ybT[:, h, 2:NT], yb_d[h][:, 2:NT], 'ybl', writes=['yb_%d' % h])
                    for h in range(16):
                        K.res['yb_%d' % h][0] = ('ybl', K.cnt['ybl'])
                    for c in range(16):
                        bA = dense(ybT, lambda kc: 'yb_%d' % kc)
                        bG = dense(u1T, lambda kc: 'u_%d' % kc)
                        gated(c, bA, bG, False)
                    K.barrier()
                with ExitStack() as pm1:
                    yaT = sb("yaT", [128, 16, NT], BF16, pm1)
                    with ExitStack() as p2b:
                        tb_ = sb("tbuf", [128, 2, NT + 2], F32, p2b)
                        hs = sb("hs", [128, 2, 343], F32, p2b)
                        Bs = sb("Bs", [128, 2, NT], F32, p2b)
                        yc = sb("yc", [128, 2, NT], F32, p2b)
                        K.op('dve', lambda e: e.memset(tb_[:, :, 0:2], 0.0), writes=['t0', 't1'])
                        hi = 0
                        for c in range(16):
                            ts_ = c % 2
                            slC = next_w()
                            slH = next_w()
                            slB = next_w()
                            for ti, (a, z) in enumerate(TT):
                                n = z - a
                                bC = K.bank()
                                mm_group(bC, 128, n, [(wring[:, slC, kc, :], u1T[:, kc, a:z]) for kc in range(16)],
                                         [['w%d' % slC, 'u_%d' % kc] for kc in range(16)])
                                bH = K.bank()
                                mm_group(bH, 128, n, [(wring[:, slH, kc, :], u1T[:, kc, a:z]) for kc in range(16)],
                                         [['w%d' % slH, 'u_%d' % kc] for kc in range(16)])
                                h_ = hi % 2
                                hi += 1
                                K.op('act', lambda e, bH=bH, h_=h_, n=n: e.copy(out=hs[:, h_, 0:n], in_=ps[bH][:, 0:n]),
                                     reads=[PS(bH)], writes=['hs%d' % h_])
                                K.op('dve', lambda e, bC=bC, h_=h_, n=n, a=a, z=z, ts_=ts_: e.tensor_tensor(
                                    out=tb_[:, ts_, 2 + a:2 + z], in0=ps[bC][:, 0:n], in1=hs[:, h_, 0:n], op=ALU.mult),
                                    reads=[PS(bC), 'hs%d' % h_], writes=['t%d' % ts_])
                            for ti, (a, z) in enumerate(TT):
                                n = z - a
                                bB = K.bank()
                                mm_group(bB, 128, n, [(wring[:, slB, kc, :], u1T[:, kc, a:z]) for kc in range(16)],
                                         [['w%d' % slB, 'u_%d' % kc] for kc in range(16)])
                                K.op('act', lambda e, bB=bB, n=n, a=a, z=z, ts_=ts_: e.copy(out=Bs[:, ts_, a:z],
                                                                                          in_=ps[bB][:, 0:n]),
                                     reads=[PS(bB)], writes=['Bs%d' % ts_])
                            K.op('dve', lambda e, ts_=ts_: e.tensor_scalar(out=tb_[:, ts_, 2:2 + HALO],
                                                                           in0=tb_[:, ts_, 2:2 + HALO],
                                                                           scalar1=halom[:, 0:1], scalar2=None,
                                                                           op0=ALU.mult),
                                 reads=['t%d' % ts_, 'halom'], writes=['t%d' % ts_])
                            K.op('dve', lambda e, ts_=ts_, c=c: e.tensor_scalar(out=yc[:, ts_, :],
                                                                                in0=tb_[:, ts_, 2:NT + 2],
                                                                                scalar1=cva[:, c, 2:3], scalar2=None,
                                                                                op0=ALU.mult),
                                 reads=['t%d' % ts_, 'cva'], writes=['yc%d' % ts_])
                            for jj in (1, 0):
                                K.op('dve', lambda e, ts_=ts_, c=c, jj=jj: e.scalar_tensor_tensor(
                                    out=yc[:, ts_, :], in0=tb_[:, ts_, jj:NT + jj], scalar=cva[:, c, jj:jj + 1],
                                    in1=yc[:, ts_, :], op0=ALU.mult, op1=ALU.add),
                                    reads=['t%d' % ts_, 'cva', 'yc%d' % ts_], writes=['yc%d' % ts_])
                            K.op('pool', lambda e, ts_=ts_, c=c: e.tensor_tensor(out=yaT[:, c, :], in0=Bs[:, ts_, :],
                                                                                 in1=yc[:, ts_, :], op=ALU.mult),
                                 reads=['Bs%d' % ts_, 'yc%d' % ts_], writes=['ya_%d' % c])
                        K.barrier()
                    for c in range(16):
                        bA = dense(yaT, lambda kc: 'ya_%d' % kc)
                        bG = dense(u1T, lambda kc: 'u_%d' % kc)
                        gated(c, bA, bG, True)
                    K.barrier()

            def layernorm(zT, ntile, cols, gi, bi, scope, tag, post):
                sq = sb("sq" + tag, [128, 2, 343], F32, scope)
                mean = sb("mean" + tag, [128, 343], F32, scope)
                rstd = sb("rstd" + tag, [128, 343], F32, scope)
                tmp = sb("tmp" + tag, [128, 2, 343], F32, scope)
                for ti, (a, z) in ntile:
                    n = z - a
                    b1 = K.bank()
                    mm_group(b1, 128, n, [(onesf[:, :], zT[:, c, a:z]) for c in range(16)],
                             [['onesf', 'z_%d' % c] for c in range(16)])
                    b2 = K.bank()
                    for c in range(16):
                        s_ = c % 2
                        K.op('act', lambda e, s_=s_, c=c, n=n, a=a, z=z: e.activation(out=sq[:, s_, 0:n],
                                                                                    in_=zT[:, c, a:z], func=AF.Square),
                             reads=['z_%d' % c], writes=['sq%d' % s_])
                        K.op('pe', lambda e, s_=s_, c=c, n=n: e.matmul(ps[b2][:, 0:n], lhsT=onesf[:, :],
                                                                       rhs=sq[:, s_, 0:n], start=(c == 0),
                                                                       stop=(c == 15)),
                             reads=['onesf', 'sq%d' % s_], writes=[PS(b2)], signal=True)
                    K.op('dve', lambda e, n=n: e.tensor_scalar(out=mean[:, 0:n], in0=ps[b1][:, 0:n], scalar1=1.0 / D,
                                                               scalar2=None, op0=ALU.mult),
                         reads=[PS(b1)], writes=['mean'])
                    K.op('dve', lambda e, n=n: e.tensor_tensor(out=rstd[:, 0:n], in0=mean[:, 0:n], in1=mean[:, 0:n],
                                                               op=ALU.mult), reads=['mean'], writes=['rstd'])
                    K.op('dve', lambda e, n=n: e.scalar_tensor_tensor(out=rstd[:, 0:n], in0=ps[b2][:, 0:n],
                                                                      scalar=1.0 / D, in1=rstd[:, 0:n],
                                                                      op0=ALU.mult, op1=ALU.subtract),
                         reads=[PS(b2), 'rstd'], writes=['rstd'])
                    K.op('act', lambda e, n=n: e.activation(out=rstd[:, 0:n], in_=rstd[:, 0:n], func=AF.Sqrt,
                                                            bias=epst[:, 0:1], scale=1.0),
                         reads=['rstd', 'epst'], writes=['rstd'])
                    K.op('dve', lambda e, n=n: e.reciprocal(out=rstd[:, 0:n], in_=rstd[:, 0:n]),
                         reads=['rstd'], writes=['rstd'])
                    for c in range(16):
                        s_ = c % 2
                        eng = 'dve' if c % 2 == 0 else 'pool'
                        K.op(eng, lambda e, s_=s_, c=c, n=n, a=a, z=z: e.tensor_tensor(
                            out=tmp[:, s_, 0:n], in0=zT[:, c, a:z], in1=mean[:, 0:n], op=ALU.subtract),
                            reads=['z_%d' % c, 'mean'], writes=['tmp%d' % s_])
                        K.op(eng, lambda e, s_=s_, n=n: e.tensor_tensor(
                            out=tmp[:, s_, 0:n], in0=tmp[:, s_, 0:n], in1=rstd[:, 0:n], op=ALU.mult),
                            reads=['tmp%d' % s_, 'rstd'], writes=['tmp%d' % s_])
                        K.op('dve', lambda e, s_=s_, c=c, n=n, a=a, z=z: e.tensor_scalar(
                            out=zT[:, c, a:z], in0=tmp[:, s_, 0:n], scalar1=lnp[:, gi, c:c + 1],
                            scalar2=lnp[:, bi, c:c + 1], op0=ALU.mult, op1=ALU.add),
                            reads=['tmp%d' % s_, 'lnp'], writes=['z_%d' % c])
                        post(c, ti, a, z)

            x1_toks = []
            u2_toks = []
            with ExitStack() as pl:
                zT = sb("zT", [128, 16, NT], F32, pl)
                xr2 = sb("xr2", [128, 2, NT], F32, pl)
                u2s = sb("u2s", [128, 2, 343], BF16, pl)
                for c in range(16):
                    sl = c % 2
                    K.dma('sp', xr2[:, sl, :], xo_d[c], 'xq%d' % sl, writes=['xq%d' % sl])
                    K.op('act', lambda e, sl=sl: e.mul(out=xr2[:, sl, :], in_=xr2[:, sl, :], mul=ALPHA),
                         reads=['xq%d' % sl], writes=['xq%d' % sl])
                    banks = dense(QT, lambda kc: 'm_%d' % kc)
                    for ti, (a, z) in enumerate(TT):
                        n = z - a
                        K.op('dve', lambda e, b=banks[ti], n=n, a=a, z=z, c=c, sl=sl: e.scalar_tensor_tensor(
                            out=zT[:, c, a:z], in0=ps[b][:, 0:n], scalar=modP[:, G_M + c:G_M + c + 1],
                            in1=xr2[:, sl, a:z], op0=ALU.mult, op1=ALU.add),
                            reads=[PS(banks[ti]), 'modP', 'xq%d' % sl], writes=['z_%d' % c])
                u2i = [0]

                def post1(c, ti, a, z):
                    n = z - a
                    s_ = u2i[0] % 2
                    u2i[0] += 1
                    K.op('act', lambda e: e.activation(out=u2s[:, s_, 0:n], in_=zT[:, c, a:z], func=AF.Identity,
                                                       bias=modT[:, SH_F + c:SH_F + c + 1],
                                                       scale=modP[:, SC_F + c:SC_F + c + 1]),
                         reads=['z_%d' % c, 'modT', 'modP'], writes=['u2s%d' % s_])
                    u2_toks.append(K.dma('sp', u2_d[c][:, a:z], u2s[:, s_, 0:n], 'u2s%d' % s_, reads=['u2s%d' % s_]))
                    x1_toks.append(K.dma('sp', x1_d[c][:, a:z], zT[:, c, a:z], 'x1st', reads=['z_%d' % c]))
                layernorm(zT, list(enumerate(TT)), NT, 0, 1, pl, "1", post1)
                K.barrier()
            K.barrier()

        with ExitStack() as pf:
            gT = sb("gT", [128, NFC, NT], BF16, pf)
            with ExitStack() as pu:
                u2T = sb("u2T", [128, 16, NT], BF16, pu)
                araw = sb("araw", [128, 2, NT + 2], F32, pu)
                ac = sb("ac", [128, 2, NT], F32, pu)
                K.wait('sp', u2_toks)
                for c in range(16):
                    K.dma('sp', u2T[:, c, :], u2_d[c], 'u2l', writes=['u2_%d' % c])
                for c in range(16):
                    K.res['u2_%d' % c][0] = ('u2l', K.cnt['u2l'])
                K.op('dve', lambda e: e.memset(araw[:, :, 0:2], 0.0), writes=['ar0', 'ar1'])
                for j in range(NFC):
                    as_ = j % 2
                    bA = dense(u2T, lambda kc: 'u2_%d' % kc)
                    for ti, (a, z) in enumerate(TT):
                        K.op('act', lambda e, b=bA[ti], a=a, z=z, as_=as_: e.copy(out=araw[:, as_, 2 + a:2 + z],
                                                                                in_=ps[b][:, 0:z - a]),
                             reads=[PS(bA[ti])], writes=['ar%d' % as_])
                    K.op('dve', lambda e, as_=as_: e.tensor_scalar(out=araw[:, as_, 2:2 + HALO],
                                                                   in0=araw[:, as_, 2:2 + HALO],
                                                                   scalar1=halom[:, 0:1], scalar2=None, op0=ALU.mult),
                         reads=['ar%d' % as_, 'halom'], writes=['ar%d' % as_])
                    K.op('pool', lambda e, as_=as_, j=j: e.tensor_scalar(out=ac[:, as_, :], in0=araw[:, as_, 2:NT + 2],
                                                                         scalar1=cvf[:, j, 2:3], scalar2=None,
                                                                         op0=ALU.mult),
                         reads=['ar%d' % as_, 'cvf'], writes=['ac%d' % as_])
                    for jj in (1, 0):
                        K.op('dve', lambda e, as_=as_, j=j, jj=jj: e.scalar_tensor_tensor(
                            out=ac[:, as_, :], in0=araw[:, as_, jj:NT + jj], scalar=cvf[:, j, jj:jj + 1],
                            in1=ac[:, as_, :], op0=ALU.mult, op1=ALU.add),
                            reads=['ar%d' % as_, 'cvf', 'ac%d' % as_], writes=['ac%d' % as_])
                    K.op('act', lambda e, as_=as_: e.activation(out=ac[:, as_, :], in_=ac[:, as_, :],
                                                                func=AF.Gelu_apprx_tanh),
                         reads=['ac%d' % as_], writes=['ac%d' % as_])
                    bB = dense(u2T, lambda kc: 'u2_%d' % kc)
                    for ti, (a, z) in enumerate(TT):
                        K.op('dve', lambda e, b=bB[ti], a=a, z=z, as_=as_, j=j: e.tensor_tensor(
                            out=gT[:, j, a:z], in0=ps[b][:, 0:z - a], in1=ac[:, as_, a:z], op=ALU.mult),
                            reads=[PS(bB[ti]), 'ac%d' % as_], writes=['g_%d' % j])
                K.barrier()
            with ExitStack() as pd:
                z2 = sb("z2", [128, 16, NT], F32, pd)
                x1r = sb("x1r", [128, NT], F32, pd)
                K.wait('sp', x1_toks)
                out_toks = []
                for c in range(16):
                    K.dma('sp', x1r[:, :], x1_d[c], 'x1r0', writes=['x1r0'])
                    K.op('act', lambda e: e.mul(out=x1r[:, :], in_=x1r[:, :], mul=ALPHA),
                         reads=['x1r0'], writes=['x1r0'])
                    slots = [next_w() for part in range(3)]
                    for ti, (a, z) in enumerate(TT):
                        n = z - a
                        b = K.bank()
                        pairs, rl = [], []
                        for part in range(3):
                            for kc in range(min(16, NFC - part * 16)):
                                pairs.append((wring[:, slots[part], kc, :], gT[:, part * 16 + kc, a:z]))
                                rl.append(['w%d' % slots[part], 'g_%d' % (part * 16 + kc)])
                        mm_group(b, 128, n, pairs, rl)
                        K.op('dve', lambda e, b=b, n=n, c=c, a=a, z=z: e.scalar_tensor_tensor(
                            out=z2[:, c, a:z], in0=ps[b][:, 0:n], scalar=modP[:, G_F + c:G_F + c + 1],
                            in1=x1r[:, a:z], op0=ALU.mult, op1=ALU.add),
                            reads=[PS(b), 'modP', 'x1r0'], writes=['z_%d' % c])
                for ti, (a, z) in enumerate(TT):
                    def post2(c, ti_, a_, z_):
                        if c == 15:
                            out_toks.append(K.dma('sp', out_d[:, :, a_:z_].rearrange("c p t -> p c t"), z2[:, :, a_:z_],
                                                  'outst', reads=['z_%d' % cc for cc in range(16)]))
                    with ExitStack() as pln:
                        layernorm(z2, [(ti, (a, z))], NT, 2, 3, pln, "2_%d" % ti, post2)
                        K.barrier()
                K.wait('sp', [('outst', K.cnt['outst'])])
                if debug:
                    K.wait('sp', [('dbg', K.cnt['dbg'])])
    return nc


_NC_CACHE = {}


def _prep(inputs):
    f32 = np.float32
    x = np.asarray(inputs['x'], f32)
    c = np.asarray(inputs['c'], f32)
    W = {k: np.asarray(inputs[k][0], f32) for k in ('w_cond', 'w_in', 'w_a', 'w_b', 'w_o', 'w_up', 'w_down')}
    items, seq = stream_plan()

    def fm(Wm, col0, ncols, kc0=0, nkc=16):
        blk = Wm[kc0 * 128:(kc0 + nkc) * 128, col0:col0 + ncols].reshape(nkc, 128, ncols).transpose(1, 0, 2)
        if nkc < 16:
            blk = np.concatenate([blk, np.zeros((128, 16 - nkc, ncols), f32)], axis=1)
        return blk
    ws = np.empty((len(items), 128, 16, 128), f32)
    for i, (nm, col0, kc0, nkc) in enumerate(items):
        ws[i] = fm(W[nm], col0, 128, kc0, nkc)
    wcond = np.ascontiguousarray(W['w_cond'].reshape(16, 128, 24, 512).transpose(2, 1, 0, 3))
    bcond = np.asarray(inputs['b_cond'], f32).reshape(1, 12288)
    wk = np.ascontiguousarray(fm(W['w_in'], O_K, 512))
    wv = np.ascontiguousarray(fm(W['w_in'], O_V, 512))
    wki = np.ascontiguousarray(fm(W['w_in'], O_KI, 64))
    wwi = np.ascontiguousarray(fm(W['w_in'], O_WI, 16))

    def pv(v):
        return np.ascontiguousarray(np.asarray(v, f32).reshape(-1, 128).T)
    cva = np.ascontiguousarray(np.stack([pv(inputs['conv_a'][0][j]) for j in range(3)], axis=2))
    cvf = np.ascontiguousarray(np.stack([pv(inputs['conv_f'][0][j]) for j in range(3)], axis=2))
    lnp = np.ascontiguousarray(np.stack([pv(inputs['ln1_g'][0]), pv(inputs['ln1_b'][0]),
                                         pv(inputs['ln2_g'][0]), pv(inputs['ln2_b'][0])], axis=1))
    kng = np.ascontiguousarray(np.broadcast_to(np.stack([np.asarray(inputs['idx_kn_g'][0], f32),
                                                         np.asarray(inputs['idx_kn_b'][0], f32)])[None], (128, 2, 64)))
    cidx = np.ascontiguousarray(np.broadcast_to((np.arange(64, dtype=f32) * 64.0)[None], (128, 64)))
    ident = np.eye(128, dtype=f32)
    maps = []
    for core in range(8):
        b, q = core // 4, core % 4
        s0 = q * 1024
        g = s0 - HALO + np.arange(NT)
        xo = np.zeros((NT, D), f32)
        valid = g >= 0
        xo[valid] = x[b, g[valid]]
        xo = np.ascontiguousarray(xo.T.reshape(16, 128, NT))
        xs = np.ascontiguousarray(x[b].T.reshape(16, 128, 16, 256).transpose(2, 1, 0, 3))
        cT = np.ascontiguousarray(c[b].reshape(16, 128).T)
        gq = g[2:2 + NQT * QTS].reshape(NQT, QTS)
        lim = np.clip((np.floor_divide(gq, 64) + 1) * 64, 64, S).astype(f32)
        qlim = np.full((128, NQT), float(S), f32)
        qlim[:QTS, :] = lim.T
        halom = np.full((128, 1), 0.0 if q == 0 else 1.0, f32)
        m = dict(xo=xo, xs=xs, cT=cT, wcond=wcond, bcond=bcond, ws=ws, wk=wk, wv=wv, wki=wki, wwi=wwi,
                 cva=cva, cvf=cvf, lnp=lnp, kng=kng, qlim=qlim, halom=halom, cidx=cidx, ident=ident)
        maps.append({"i_" + k_: v_ for k_, v_ in m.items()})
    return maps


def _assemble(results):
    out = np.empty((2, S, D), np.float32)
    for core in range(8):
        b, q = core // 4, core % 4
        o = np.asarray(results[core]["out"], np.float32)
        out[b, q * 1024:(q + 1) * 1024, :] = o.reshape(D, NT)[:, HALO:].T
    return out


def kernel(**inputs):
    if 'nc' not in _NC_CACHE:
        _NC_CACHE['nc'] = build_nc(False)
    maps = _prep(inputs)
    res = run_bass_kernel_spmd(_NC_CACHE['nc'], maps, core_ids=list(range(8)))
    return _assemble(res.results)
```

```python
import numpy as np
import ml_dtypes
from contextlib import ExitStack
import concourse.bass as bass
import concourse.mybir as mybir
from concourse.bass_utils import run_bass_kernel_spmd

F32 = mybir.dt.float32
BF16 = mybir.dt.bfloat16
AF = mybir.ActivationFunctionType
ALU = mybir.AluOpType
AX = mybir.AxisListType

D = 2048
S = 4096
NT = 1028
HALO = 4
TT = [(0, 343), (343, 686), (686, 1028)]
NQT = 9
QTS = 114
NQB = 3
QBS = 342
DFF = 5632
NFC = 44
ALPHA = 2.0 ** 0.25
LN_EPS = 1e-5
IDX_W_SCALE = 1024.0 ** -0.5
TOPK = 256.0
BIG = 1.0e6
NBIS = 16
WR = 8
O_B, O_C, O_H, O_Q, O_K, O_V, O_QI, O_KI, O_WI, O_GA, O_GB = 0, 2048, 4096, 6144, 8192, 8704, 9216, 10240, 10304, 10320, 12368


def stream_plan():
    items, seq = [], []

    def add(*d):
        items.append(d)
        seq.append(len(items) - 1)
    for h in range(16):
        add('w_in', O_Q + h * 128, 0, 16)
    for c in range(8):
        add('w_in', O_QI + c * 128, 0, 16)
    for c in range(16):
        add('w_b', c * 128, 0, 16)
        add('w_in', O_GB + c * 128, 0, 16)
    for c in range(16):
        add('w_in', O_C + c * 128, 0, 16)
        add('w_in', O_H + c * 128, 0, 16)
        add('w_in', O_B + c * 128, 0, 16)
    for c in range(16):
        add('w_a', c * 128, 0, 16)
        add('w_in', O_GA + c * 128, 0, 16)
    for c in range(16):
        add('w_o', c * 128, 0, 16)
    for j in range(NFC):
        add('w_up', j * 128, 0, 16)
        add('w_up', DFF + j * 128, 0, 16)
    base = len(items)
    for c in range(16):
        for part in range(3):
            items.append(('w_down', c * 128, part * 16, min(16, NFC - part * 16)))
    for c in range(16):
        for part in range(3):
            seq.append(base + c * 3 + part)
    return items, seq


class KB:
    def __init__(self, nc, es):
        self.nc, self.es = nc, es
        self.engs = {'pe': nc.tensor, 'act': nc.scalar, 'dve': nc.vector, 'pool': nc.gpsimd, 'sp': nc.sync}
        self.sems = {k: es.enter_context(nc.semaphore('s_' + k)) for k in self.engs}
        self.cnt = {k: 0 for k in self.engs}
        self.waited = {}
        self.res = {}
        self.pbank = 0

    def dsem(self, name):
        if name not in self.sems:
            self.sems[name] = self.es.enter_context(self.nc.semaphore('d_' + name))
            self.cnt[name] = 0
        return self.sems[name]

    def _deps(self, reads, writes):
        deps = []
        for r in reads:
            st = self.res.get(r)
            if st and st[0]:
                deps.append(st[0])
        for w in writes:
            st = self.res.get(w)
            if st:
                if st[0]:
                    deps.append(st[0])
                deps.extend(st[1])
        return deps

    def wait(self, eng, deps):
        for (key, val) in deps:
            if key == eng and eng == 'pe':
                continue
            if self.waited.get((eng, key), 0) >= val:
                continue
            self.engs[eng].wait_ge(self.sems[key], val)
            self.waited[(eng, key)] = val

    def _upd(self, tok, reads, writes):
        for r in reads:
            st = self.res.setdefault(r, [None, []])
            st[1].append(tok)
        for w in writes:
            self.res[w] = [tok, []]

    def op(self, eng, fn, reads=(), writes=(), signal=True):
        self.wait(eng, self._deps(reads, writes))
        ins = fn(self.engs[eng])
        if signal:
            self.cnt[eng] += 1
            ins.then_inc(self.sems[eng], 1)
            tok = (eng, self.cnt[eng])
        else:
            tok = (eng, self.cnt[eng] + 1)
        self._upd(tok, reads, writes)
        return tok

    def dma(self, queue, out, in_, sem, reads=(), writes=()):
        self.dsem(sem)
        self.wait(queue, self._deps(reads, writes))
        ins = self.engs[queue].dma_start(out=out, in_=in_)
        self.cnt[sem] += 16
        ins.then_inc(self.sems[sem], 16)
        tok = (sem, self.cnt[sem])
        self._upd(tok, reads, writes)
        return tok

    def barrier(self):
        toks = [(k, v) for k, v in self.cnt.items() if v > 0]
        for e in ('pe', 'act', 'dve', 'pool', 'sp'):
            self.wait(e, toks)

    def bank(self):
        b = self.pbank
        self.pbank = (self.pbank + 1) % 8
        return b


def build_nc(debug=False):
    nc = bass.Bass("TRN2", target_bir_lowering=False)
    items, seq = stream_plan()
    NI = len(items)

    def din(name, shape, dt=F32):
        return nc.dram_tensor("i_" + name, shape, dt, kind="ExternalInput").ap()
    xo_d = din("xo", [16, 128, NT])
    xs_d = din("xs", [16, 128, 16, 256])
    cT_d = din("cT", [128, 16])
    wc_d = din("wcond", [24, 128, 16, 512])
    bcT_d = din("bcondT", [128, 96])
    ws_d = din("ws", [NI, 128, 16, 128])
    wk_d = din("wk", [128, 16, 512])
    wv_d = din("wv", [128, 16, 512])
    wki_d = din("wki", [128, 16, 64])
    wwi_d = din("wwi", [128, 16, 16])
    cva_d = din("cva", [128, 16, 3])
    cvf_d = din("cvf", [128, NFC, 3])
    lnp_d = din("lnp", [128, 4, 16])
    kng_d = din("kng", [128, 2, 64])
    qlim_d = din("qlim", [128, NQT])
    halo_d = din("halom", [128, 1])
    cidx_d = din("cidx", [128, 64])
    ident_d = din("ident", [128, 128])
    okind = "ExternalOutput" if debug else "Internal"
    out_d = nc.dram_tensor("out", [16, 128, NT], F32, kind="ExternalOutput").ap()
    kT_d = nc.dram_tensor("kT_s", [4, 128, S], BF16, kind=okind).ap()
    v_d = nc.dram_tensor("v_s", [4, 128, 32, 128], BF16, kind=okind).ap()
    yb_d = nc.dram_tensor("yb_s", [16, 128, NT], BF16, kind=okind).ap()
    x1_d = nc.dram_tensor("x1_s", [16, 128, NT], F32, kind=okind).ap()
    mod_o = nc.dram_tensor("mod_o", [128, 96], F32, kind=okind).ap()

    with ExitStack() as es:
        K = KB(nc, es)

        def sb(name, shape, dt=F32, scope=es):
            return scope.enter_context(nc.sbuf_tensor(name, shape, dt))
        ps = [es.enter_context(nc.psum_tensor("ps%d" % i, [128, 512], F32)) for i in range(8)]

        def PS(b):
            return "ps%d" % b

        ident = sb("ident", [128, 128], BF16)
        identf = sb("identf", [128, 128], F32)
        onesb = sb("onesb", [128, 128], BF16)
        onesf = sb("onesf", [128, 128], F32)
        one11 = sb("one11", [1, 2], F32)
        epst = sb("epst", [128, 1], F32)
        cva = sb("cva", [128, 16, 3])
        cvf = sb("cvf", [128, NFC, 3])
        lnp = sb("lnp", [128, 4, 16])
        kng = sb("kng", [128, 2, 64])
        qlim = sb("qlim", [128, NQT])
        halom = sb("halom", [128, 1])
        cidx = sb("cidx", [128, 64])
        modT = sb("modT", [128, 96])
        modP = sb("modP", [128, 96])
        wring = sb("wring", [128, WR, 16, 128], BF16)
        pk = ExitStack()
        kiTE = sb("kiTE", [128, S], BF16, pk)
        kiTO = sb("kiTO", [128, S], BF16, pk)
        for (t, d_) in ((identf, ident_d), (cva, cva_d), (cvf, cvf_d), (lnp, lnp_d), (kng, kng_d), (qlim, qlim_d),
                        (halom, halo_d), (cidx, cidx_d)):
            K.dma('sp', t[:], d_, 'c_' + t.name, writes=[t.name])
        K.op('dve', lambda e: e.tensor_copy(out=ident[:], in_=identf[:]), reads=['identf'], writes=['ident'])
        K.op('dve', lambda e: e.memset(onesb[:], 1.0), writes=['onesb'])
        K.op('dve', lambda e: e.memset(onesf[:], 1.0), writes=['onesf'])
        K.op('dve', lambda e: e.memset(one11[:], 1.0), writes=['one11'])
        K.op('dve', lambda e: e.memset(epst[:], LN_EPS), writes=['epst'])

        wst = {'pos': 0, 'loaded': 0}

        def w_prefetch(upto):
            while wst['loaded'] < min(upto, len(seq)):
                i = wst['loaded']
                slot = i % WR
                K.dma('pool', wring[:, slot, :, :], ws_d[seq[i]], 'w%d' % slot, writes=['w%d' % slot])
                wst['loaded'] += 1

        def next_w():
            i = wst['pos']
            w_prefetch(i + 1)
            wst['pos'] += 1
            w_prefetch(i + WR - 2)
            return i % WR

        def mm_group(bank, M, n, pairs, reads_list, moff=0):
            last = len(pairs) - 1
            for i, (l, r) in enumerate(pairs):
                K.op('pe', lambda e, l=l, r=r, i=i: e.matmul(ps[bank][moff:moff + M, 0:n], lhsT=l, rhs=r,
                                                              start=(i == 0), stop=(i == last)),
                     reads=reads_list[i], writes=[PS(bank)], signal=(i == last))

        p0 = ExitStack()
        cTs = sb("cTs", [128, 16], F32, p0)
        cact = sb("cact", [128, 16], BF16, p0)
        wcb = sb("wcb", [128, 2, 16, 512], BF16, p0)
        bcT = sb("bcT", [128, 96], F32, p0)
        K.dma('sp', cTs[:], cT_d, 'c_cTs', writes=['cTs'])
        K.dma('sp', bcT[:], bcT_d, 'c_bcT', writes=['bcT'])
        K.op('act', lambda e: e.activation(out=cact[:], in_=cTs[:], func=AF.Silu), reads=['cTs'], writes=['cact'])

        def mod_block(nb):
            sl = nb % 2
            mres = 'modA' if nb < 8 else 'modB'
            K.dma('pool', wcb[:, sl, :, :], wc_d[nb], 'wc%d' % sl, writes=['wc%d' % sl])
            b = K.bank()
            for cc in range(4):
                for kc in range(16):
                    K.op('pe', lambda e, cc=cc, kc=kc: e.matmul(ps[b][:, cc:cc + 1],
                                                                lhsT=wcb[:, sl, kc, cc * 128:(cc + 1) * 128],
                                                                rhs=cact[:, kc:kc + 1], start=(kc == 0), stop=(kc == 15)),
                         reads=['cact', 'wc%d' % sl], writes=[PS(b)], signal=(kc == 15 and cc == 3))
            K.op('dve', lambda e: e.tensor_tensor(out=modT[:, nb * 4:(nb + 1) * 4], in0=ps[b][:, 0:4],
                                                  in1=bcT[:, nb * 4:(nb + 1) * 4], op=ALU.add),
                 reads=[PS(b), 'bcT'], writes=[mres])
            K.op('dve', lambda e: e.tensor_scalar(out=modP[:, nb * 4:(nb + 1) * 4], in0=modT[:, nb * 4:(nb + 1) * 4],
                                                  scalar1=1.0, scalar2=None, op0=ALU.add),
                 reads=[mres], writes=[mres])
        for nb in range(8):
            mod_block(nb)
        SH_M, SC_M, G_M, SH_F, SC_F, G_F = 0, 16, 32, 48, 64, 80

        def modulate(eng_i, out_ap, in_ap, kc, scb, shb, reads, writes):
            if eng_i % 2 == 0:
                K.op('dve', lambda e: e.tensor_scalar(out=out_ap, in0=in_ap, scalar1=modP[:, scb + kc:scb + kc + 1],
                                                      scalar2=modT[:, shb + kc:shb + kc + 1], op0=ALU.mult, op1=ALU.add),
                     reads=reads + (['modA'] if scb < 32 else ['modB']), writes=writes)
            else:
                K.op('act', lambda e: e.activation(out=out_ap, in_=in_ap, func=AF.Identity,
                                                   bias=modT[:, shb + kc:shb + kc + 1],
                                                   scale=modP[:, scb + kc:scb + kc + 1]),
                     reads=reads + (['modA'] if scb < 32 else ['modB']), writes=writes)

        store_toks = []
        with ExitStack() as p1:
            wk = sb("wk", [128, 16, 512], BF16, p1)
            wv = sb("wv", [128, 16, 512], BF16, p1)
            wki = sb("wki", [128, 16, 64], BF16, p1)
            xsb = sb("xsb", [128, 2, 16, 256], F32, p1)
            usb = sb("usb", [128, 2, 16, 256], BF16, p1)
            kst = sb("kst", [128, 2, 4, 256], BF16, p1)
            vst = sb("vst", [128, 2, 2, 512], BF16, p1)
            kraw = sb("kraw", [128, 2, 64], F32, p1)
            kn2 = sb("kn2", [128, 2, 2, 128], BF16, p1)
            K.op('dve', lambda e: e.memset(kn2[:], 0.0), writes=['ki0n', 'ki1n'])
            bst = sb("bst", [128, 2, 6], F32, p1)
            bmv = sb("bmv", [128, 2, 2], F32, p1)
            K.dma('pool', wk[:], wk_d, 'c_wk', writes=['wk'])
            K.dma('pool', wv[:], wv_d, 'c_wv', writes=['wv'])
            K.dma('pool', wki[:], wki_d, 'c_wki', writes=['wki'])
            w_prefetch(WR - 2)
            K.dma('sp', xsb[:, 0, :, :], xs_d[0], 'xs0', writes=['xs0'])
            for tb in range(16):
                sl = tb % 2
                if tb + 1 < 16:
                    K.dma('sp', xsb[:, 1 - sl, :, :], xs_d[tb + 1], 'xs%d' % (1 - sl), writes=['xs%d' % (1 - sl)])
                for kc in range(16):
                    modulate(kc, usb[:, sl, kc, :], xsb[:, sl, kc, :], kc, SC_M, SH_M,
                             ['xs%d' % sl], ['us%d_%d' % (sl, kc)])
                for n in range(4):
                    b = K.bank()
                    mm_group(b, 128, 256, [(wk[:, kc, n * 128:(n + 1) * 128], usb[:, sl, kc, :]) for kc in range(16)],
                             [['wk', 'us%d_%d' % (sl, kc)] for kc in range(16)])
                    K.op('act', lambda e, b=b, n=n: e.copy(out=kst[:, sl, n, :], in_=ps[b][:, 0:256]),
                         reads=[PS(b)], writes=['kst%d' % sl])
                store_toks.append(K.dma('sp', kT_d[:, :, tb * 256:(tb + 1) * 256].rearrange("n p t -> p n t"),
                                        kst[:, sl, :, :], 'kst%d' % sl, reads=['kst%d' % sl]))
                for sub in range(2):
                    tsl = slice(sub * 128, (sub + 1) * 128)
                    b = K.bank()
                    mm_group(b, 128, 512, [(usb[:, sl, kc, tsl], wv[:, kc, :]) for kc in range(16)],
                             [['wv', 'us%d_%d' % (sl, kc)] for kc in range(16)])
                    K.op('dve', lambda e, b=b, sub=sub: e.tensor_copy(out=vst[:, sl, sub, :], in_=ps[b][:, 0:512]),
                         reads=[PS(b)], writes=['vst%d' % sl])
                    b = K.bank()
                    mm_group(b, 128, 64, [(usb[:, sl, kc, tsl], wki[:, kc, :]) for kc in range(16)],
                             [['wki', 'us%d_%d' % (sl, kc)] for kc in range(16)])
                    r_ = 'ki%d' % sub
                    K.op('act', lambda e, b=b, sub=sub: e.copy(out=kraw[:, sub, :], in_=ps[b][:, 0:64]),
                         reads=[PS(b)], writes=[r_])
                    K.op('dve', lambda e, sub=sub: e.bn_stats(out=bst[:, sub, :], in_=kraw[:, sub, :]),
                         reads=[r_], writes=[r_ + 's'])
                    K.op('dve', lambda e, sub=sub: e.bn_aggr(out=bmv[:, sub, :], in_=bst[:, sub, :]),
                         reads=[r_ + 's'], writes=[r_ + 'm'])
                    K.op('act', lambda e, sub=sub: e.activation(out=bmv[:, sub, 1:2], in_=bmv[:, sub, 1:2], func=AF.Sqrt,
                                                                bias=epst[:, 0:1], scale=1.0),
                         reads=[r_ + 'm', 'epst'], writes=[r_ + 'm'])
                    K.op('dve', lambda e, sub=sub: e.reciprocal(out=bmv[:, sub, 1:2], in_=bmv[:, sub, 1:2]),
                         reads=[r_ + 'm'], writes=[r_ + 'm'])
                    K.op('dve', lambda e, sub=sub: e.tensor_scalar(out=kraw[:, sub, :], in0=kraw[:, sub, :],
                                                                   scalar1=bmv[:, sub, 0:1], scalar2=bmv[:, sub, 1:2],
                                                                   op0=ALU.subtract, op1=ALU.mult),
                         reads=[r_, r_ + 'm'], writes=[r_])
                    K.op('dve', lambda e, sub=sub: e.tensor_tensor(out=kraw[:, sub, :], in0=kraw[:, sub, :],
                                                                   in1=kng[:, 0, :], op=ALU.mult),
                         reads=[r_, 'kng'], writes=[r_])
                    K.op('dve', lambda e, sub=sub: e.tensor_tensor(out=kn2[:, sub, 0, 0:64], in0=kraw[:, sub, :],
                                                                   in1=kng[:, 1, :], op=ALU.add),
                         reads=[r_, 'kng'], writes=[r_ + 'n'])
                    K.op('dve', lambda e, sub=sub: e.tensor_copy(out=kn2[:, sub, 1, 64:128], in_=kn2[:, sub, 0, 0:64]),
                         reads=[r_ + 'n'], writes=[r_ + 'n'])
                    b = K.bank()
                    pT = ps[b][:].bitcast(BF16)
                    for eo in range(2):
                        K.op('pe', lambda e, pT=pT, sub=sub, eo=eo: e.transpose(out=pT[:, eo * 128:(eo + 1) * 128],
                                                                                in_=kn2[:, sub, eo, :],
                                                                                identity=ident[:, :]),
                             reads=[r_ + 'n', 'ident'], writes=[PS(b)], signal=(eo == 1))
                    t0 = tb * 256 + sub * 128
                    K.op('act', lambda e, pT=pT, t0=t0: e.copy(out=kiTE[:, t0:t0 + 128], in_=pT[:, 0:128]),
                         reads=[PS(b)], writes=['kiT'])
                    K.op('act', lambda e, pT=pT, t0=t0: e.copy(out=kiTO[:, t0:t0 + 128], in_=pT[:, 128:256]),
                         reads=[PS(b)], writes=['kiT'])
                for n in range(4):
                    store_toks.append(K.dma('sp', v_d[n][:, tb * 2:tb * 2 + 2, :], vst[:, sl, :, n * 128:(n + 1) * 128],
                                            'vst%d' % sl, reads=['vst%d' % sl]))
                mod_block(8 + tb)
            K.barrier()
        if debug:
            K.dma('sp', mod_o, modT[:], 'dbg', reads=['modA', 'modB'])
        p0.close()

        def build_u(uT, scope, scb, shb, src_d, tag):
            xr = sb("xr_" + tag, [128, 2, NT], F32, scope)
            for kc in range(16):
                sl = kc % 2
                K.dma('sp', xr[:, sl, :], src_d[kc], 'xr%d' % sl, writes=['xr%d' % sl])
                modulate(kc, uT[:, kc, :], xr[:, sl, :], kc, scb, shb, ['xr%d' % sl], ['u_%d' % kc])

        def dense(rhsT, rhs_res, nkc=16):
            slot = next_w()
            banks = []
            for (a, z) in TT:
                b = K.bank()
                mm_group(b, 128, z - a, [(wring[:, slot, kc, :], rhsT[:, kc, a:z]) for kc in range(nkc)],
                         [['w%d' % slot, rhs_res(kc)] for kc in range(nkc)])
                banks.append(b)
            return banks

        with ExitStack() as pA:
            qT = sb("qT", [128, 16, NT], BF16, pA)
            qiT = sb("qiT", [128, 8, NT], BF16, pA)
            wtm = sb("wtm", [128, NQT, 16], F32, pA)
            with ExitStack() as p2a:
                u1T = sb("u1T", [128, 16, NT], BF16, p2a)
                wwi = sb("wwi", [128, 16, 16], BF16, p2a)
                K.dma('pool', wwi[:], wwi_d, 'c_wwi', writes=['wwi'])
                build_u(u1T, p2a, SC_M, SH_M, xo_d, "a")
                for h in range(24):
                    banks = dense(u1T, lambda kc: 'u_%d' % kc)
                    for ti, (a, z) in enumerate(TT):
                        b = banks[ti]
                        dst = qT[:, h, a:z] if h < 16 else qiT[:, h - 16, a:z]
                        dres = 'q_%d' % h
                        if (h + ti) % 2 == 0:
                            K.op('act', lambda e, b=b, dst=dst, n=z - a: e.copy(out=dst, in_=ps[b][:, 0:n]),
                                 reads=[PS(b)], writes=[dres])
                        else:
                            K.op('dve', lambda e, b=b, dst=dst, n=z - a: e.tensor_copy(out=dst, in_=ps[b][:, 0:n]),
                                 reads=[PS(b)], writes=[dres])
                for qt in range(NQT):
                    j0 = 2 + qt * QTS
                    b = K.bank()
                    mm_group(b, QTS, 16, [(u1T[:, kc, j0:j0 + QTS], wwi[:, kc, :]) for kc in range(16)],
                             [['wwi', 'u_%d' % kc] for kc in range(16)])
                    K.op('act', lambda e, b=b, qt=qt: e.mul(out=wtm[0:QTS, qt, :], in_=ps[b][0:QTS, 0:16],
                                                            mul=IDX_W_SCALE),
                         reads=[PS(b)], writes=['wtm'])
                K.barrier()

            with ExitStack() as pt:
                score = sb("score", [128, 2, S], F32, pt)
                dg = sb("dg", [128, 16, QTS], BF16, pt)
                rb = sb("rb", [128, 4, 512], BF16, pt)
                mask = sb("mask", [128, S], BF16, pt)
                maskT = sb("maskT", [128, 32, QBS], BF16, pt)
                pen = sb("pen", [128, 2, 64], F32, pt)
                bis = sb("bis", [128, 8], F32, pt)
                nst = sb("nst", [128, NBIS], F32, pt)
                pwn = sb("pwn", [128, NBIS], F32, pt)
                c35 = sb("c35", [128, 1], F32, pt)
                knb = sb("knb", [128, 2, S], BF16, pt)
                vnb = sb("vnb", [128, 32, 128], BF16, pt)
                Eb = sb("Eb", [128, 4, QBS], BF16, pt)
                Em = sb("Em", [128, 4, QBS], BF16, pt)
                rz = sb("rz", [128, QBS], F32, pt)
                ybs = sb("ybs", [128, 2, QBS], BF16, pt)
                for k_ in range(NBIS):
                    K.op('dve', lambda e, k_=k_: e.memset(pwn[:, k_:k_ + 1], -(2.0 ** -(k_ + 1))), writes=['pwn'])
                K.op('dve', lambda e: e.memset(c35[:], float(S) - 2.0 * TOPK + 0.5), writes=['c35'])
                K.wait('sp', store_toks)
                kvload = [0]
                kvslot = {}
                yb_toks = []
                LOOK = 3
                SBK = [4, 5, 6, 7]
                P = slice(0, QTS)
                uctr = [0]

                def emit_acc(qt):
                    sb_ = qt % 2
                    j0 = 2 + qt * QTS
                    K.op('dve', lambda e: e.tensor_scalar(out=pen[P, sb_, :], in0=cidx[P, :],
                                                          scalar1=qlim[P, qt:qt + 1], scalar2=-BIG,
                                                          op0=ALU.is_ge, op1=ALU.mult),
                         reads=['cidx', 'qlim'], writes=['pen%d' % sb_])
                    for h in range(16):
                        K.op('dve', lambda e, h=h: e.tensor_scalar(out=dg[P, h, :], in0=ident[P, 0:QTS],
                                                                   scalar1=wtm[P, qt, h:h + 1], scalar2=None,
                                                                   op0=ALU.mult),
                             reads=['ident', 'wtm'], writes=['dg%d' % h])
                    for kb in range(8):
                        ks = slice(kb * 512, (kb + 1) * 512)
                        sbank = kb % 2
                        rings = {}

                        def emit_D(h):
                            u = uctr[0]
                            uctr[0] += 1
                            bank = 2 + (u % 6)
                            ring = u % 4
                            kz = kiTE if h % 2 == 0 else kiTO
                            K.op('pe', lambda e: e.matmul(ps[bank][P, 0:512], lhsT=qiT[:, h // 2, j0:j0 + QTS],
                                                          rhs=kz[:, ks], start=True, stop=True),
                                 reads=['q_%d' % (16 + h // 2), 'kiT'], writes=[PS(bank)])
                            if False:
                                K.op('act', lambda e: e.activation(out=rb[P, ring, :], in_=ps[bank][P, 0:512],
                                                                   func=AF.Relu),
                                     reads=[PS(bank)], writes=['rb%d' % ring])
                            else:
                                K.op('dve', lambda e: e.tensor_scalar(out=rb[P, ring, :], in0=ps[bank][P, 0:512],
                                                                      scalar1=0.0, scalar2=None, op0=ALU.max),
                                     reads=[PS(bank)], writes=['rb%d' % ring])
                            rings[h] = ring
                        for h in range(2):
                            emit_D(h)
                        for h in range(16):
                            if h + 2 < 16:
                                emit_D(h + 2)
                            K.op('pe', lambda e, h=h, r_=rings[h]: e.matmul(
                                ps[sbank][P, 0:512], lhsT=dg[P, h, :], rhs=rb[P, r_, :],
                                start=(h == 0), stop=(h == 15)),
                                reads=['dg%d' % h, 'rb%d' % rings[h]], writes=[PS(sbank)], signal=(h == 15))
                        K.op('dve', lambda e: e.tensor_tensor(
                            out=score[P, sb_, ks].rearrange("p (c j) -> p c j", j=64),
                            in0=ps[sbank][P, 0:512].rearrange("p (c j) -> p c j", j=64),
                            in1=pen[P, sb_, kb * 8:(kb + 1) * 8].unsqueeze(2).to_broadcast([QTS, 8, 64]), op=ALU.add),
                            reads=[PS(sbank), 'pen%d' % sb_], writes=['sc%d_%d' % (sb_, kb)])

                def emit_bis(qt):
                    sb_ = qt % 2
                    qi3 = qt % 3
                    allsc = ['sc%d_%d' % (sb_, kb) for kb in range(8)]
                    K.op('dve', lambda e: e.tensor_reduce(out=bis[P, 5:6], in_=score[P, sb_, :], axis=AX.X, op=ALU.max),
                         reads=allsc, writes=['bisA'])
                    K.op('dve', lambda e: e.tensor_reduce(out=bis[P, 6:7], in_=score[P, sb_, 0:256], axis=AX.X,
                                                          op=ALU.min), reads=allsc, writes=['bisB'])
                    K.op('dve', lambda e: e.tensor_scalar(out=bis[P, 6:7], in0=bis[P, 6:7], scalar1=-1000.0,
                                                          scalar2=None, op0=ALU.max), reads=['bisB'], writes=['bisB'])
                    K.op('dve', lambda e: e.scalar_tensor_tensor(out=bis[P, 0:1], in0=bis[P, 5:6], scalar=-0.5,
                                                                 in1=bis[P, 6:7], op0=ALU.mult, op1=ALU.subtract),
                         reads=['bisA', 'bisB'], writes=['bisN'])
                    K.op('dve', lambda e: e.scalar_tensor_tensor(out=bis[P, 0:1], in0=bis[P, 6:7], scalar=0.5,
                                                                 in1=bis[P, 0:1], op0=ALU.mult, op1=ALU.add),
                         reads=['bisB', 'bisN'], writes=['bisN'])
                    K.op('dve', lambda e: e.tensor_tensor(out=bis[P, 1:2], in0=bis[P, 5:6], in1=bis[P, 6:7],
                                                          op=ALU.subtract), reads=['bisA', 'bisB'], writes=['bisH'])
                    K.op('dve', lambda e: e.tensor_scalar(out=bis[P, 1:2], in0=bis[P, 1:2], scalar1=0.5,
                                                          scalar2=1e-3, op0=ALU.mult, op1=ALU.add),
                         reads=['bisH'], writes=['bisH'])
                    K.op('dve', lambda e: e.tensor_scalar(out=nst[P, :], in0=pwn[P, :], scalar1=bis[P, 1:2],
                                                          scalar2=None, op0=ALU.mult),
                         reads=['bisH', 'pwn'], writes=['nst'])
                    K.op('dve', lambda e: e.tensor_scalar(out=bis[P, 2:3], in0=bis[P, 1:2],
                                                          scalar1=2.0 ** -NBIS, scalar2=None, op0=ALU.mult),
                         reads=['bisH'], writes=['bisL'])
                    for it in range(NBIS):
                        K.op('act', lambda e: e.activation(out=mask[P, :], in_=score[P, sb_, :], func=AF.Sign,
                                                           bias=bis[P, 0:1], scale=1.0, accum_out=bis[P, 3:4]),
                             reads=allsc + ['bisN'], writes=['mask', 'bisC'])
                        K.op('act', lambda e: e.activation(out=bis[P, 4:5], in_=bis[P, 3:4], func=AF.Sign,
                                                           bias=c35[P, 0:1], scale=1.0),
                             reads=['bisC', 'c35'], writes=['bisS'])
                        K.op('act', lambda e, it=it: e.activation(out=bis[P, 0:1], in_=bis[P, 4:5],
                                                                  func=AF.Identity, bias=bis[P, 0:1],
                                                                  scale=nst[P, it:it + 1]),
                             reads=['bisS', 'nst', 'bisN'], writes=['bisN'])

                def emit_fin(qt):
                    sb_ = qt % 2
                    qi3 = qt % 3
                    allsc = ['sc%d_%d' % (sb_, kb) for kb in range(8)]
                    K.op('dve', lambda e: e.scalar_tensor_tensor(out=bis[P, 7:8], in0=bis[P, 0:1], scalar=-1.0,
                                                                 in1=bis[P, 2:3], op0=ALU.mult, op1=ALU.subtract),
                         reads=['bisN', 'bisL'], writes=['bisT'])
                    K.op('dve', lambda e: e.tensor_scalar(out=mask[P, :], in0=score[P, sb_, :], scalar1=bis[P, 7:8],
                                                          scalar2=None, op0=ALU.is_ge),
                         reads=allsc + ['bisT'], writes=['mask'])
                    for g in range(8):
                        b = K.bank()
                        pT = ps[b][:].bitcast(BF16)
                        for k4 in range(4):
                            kt = g * 4 + k4
                            K.op('pe', lambda e, pT=pT, k4=k4, kt=kt: e.transpose(
                                out=pT[:, k4 * QTS:(k4 + 1) * QTS], in_=mask[P, kt * 128:(kt + 1) * 128],
                                identity=ident[P, 0:QTS]),
                                reads=['mask', 'ident'], writes=[PS(b)], signal=(k4 == 3))
                        src = pT[:, 0:4 * QTS].rearrange("p (k q) -> p k q", q=QTS)
                        dst = maskT[:, g * 4:(g + 1) * 4, qi3 * QTS:(qi3 + 1) * QTS]
                        if g % 2 == 0:
                            K.op('act', lambda e, src=src, dst=dst: e.copy(out=dst, in_=src),
                                 reads=[PS(b)], writes=['mT%d' % qi3])
                        else:
                            K.op('dve', lambda e, src=src, dst=dst: e.tensor_copy(out=dst, in_=src),
                                 reads=[PS(b)], writes=['mT%d' % qi3])

                emit_acc(0)
                for qb in range(NQB):
                    q0 = 2 + qb * QBS
                    for qi3 in range(3):
                        qt = qb * 3 + qi3
                        emit_bis(qt)
                        if qt + 1 < NQT:
                            emit_acc(qt + 1)
                        emit_fin(qt)
                    its = [(h, kt) for h in range(16) for kt in range(32)]

                    def emit_S(i, q0=q0):
                        h, kt = its[i]
                        n = h // 4
                        if kt == 0 and h % 4 == 0:
                            ksl = kvload[0] % 2
                            kvload[0] += 1
                            kvslot[n] = ksl
                            K.dma('sp', knb[:, ksl, :], kT_d[n], 'kn%d' % ksl, writes=['kn%d' % ksl])
                        ksl = kvslot[n]
                        bs_ = SBK[i % 4]
                        es_ = i % 4
                        K.op('pe', lambda e: e.matmul(ps[bs_][:, 0:QBS], lhsT=knb[:, ksl, kt * 128:(kt + 1) * 128],
                                                      rhs=qT[:, h, q0:q0 + QBS], start=True, stop=True),
                             reads=['kn%d' % ksl, 'q_%d' % h], writes=[PS(bs_)])
                        K.op('act', lambda e: e.activation(out=Eb[:, es_, :], in_=ps[bs_][:, 0:QBS], func=AF.Exp,
                                                           scale=128.0 ** -0.5),
                             reads=[PS(bs_)], writes=['E%d' % es_])
                        K.op('dve', lambda e: e.tensor_tensor(out=Em[:, es_, :], in0=Eb[:, es_, :],
                                                              in1=maskT[:, kt, :], op=ALU.mult),
                             reads=['E%d' % es_, 'mT0', 'mT1', 'mT2'], writes=['Em%d' % es_])
                    for i in range(LOOK):
                        emit_S(i)
                    for i in range(len(its)):
                        if i + LOOK < len(its):
                            emit_S(i + LOOK)
                        h, kt = its[i]
                        n = h // 4
                        ksl = kvslot[n]
                        es_ = i % 4
                        bo, bz = (0, 1) if h % 2 == 0 else (2, 3)
                        if kt == 0 and h % 4 == 0:
                            K.dma('sp', vnb[:, :, :], v_d[n], 'vn0', writes=['vn0'])
                        K.op('pe', lambda e, bo=bo, kt=kt, ksl=ksl, es_=es_: e.matmul(
                            ps[bo][:, 0:QBS], lhsT=vnb[:, kt, :], rhs=Em[:, es_, :],
                            start=(kt == 0), stop=(kt == 31)),
                            reads=['vn0', 'Em%d' % es_], writes=[PS(bo)], signal=False)
                        K.op('pe', lambda e, bz=bz, kt=kt, es_=es_: e.matmul(
                            ps[bz][:, 0:QBS], lhsT=onesb[:, :], rhs=Em[:, es_, :],
                            start=(kt == 0), stop=(kt == 31)),
                            reads=['onesb', 'Em%d' % es_], writes=[PS(bz)], signal=True)
                        if kt == 31:
                            K.op('dve', lambda e, bz=bz: e.reciprocal(out=rz[:, :], in_=ps[bz][:, 0:QBS]),
                                 reads=[PS(bz)], writes=['rz'])
                            ys = h % 2
                            K.op('dve', lambda e, bo=bo, ys=ys: e.tensor_tensor(out=ybs[:, ys, :], in0=ps[bo][:, 0:QBS],
                                                                                in1=rz[:, :], op=ALU.mult),
                                 reads=[PS(bo), 'rz'], writes=['ybs%d' % ys])
                            yb_toks.append(K.dma('sp', yb_d[h][:, q0:q0 + QBS], ybs[:, ys, :], 'ybs%d' % ys,
                                                 reads=['ybs%d' % ys]))
                K.barrier()

        K.barrier()
        pk.close()
        u2_d = nc.dram_tensor("u2_s", [16, 128, NT], BF16, kind="Internal").ap()
        with ExitStack() as pQ:
            QT = sb("QT", [128, 16, NT], BF16, pQ)
            with ExitStack() as pm:
                u1T = sb("u1Tb", [128, 16, NT], BF16, pm)
                build_u(u1T, pm, SC_M, SH_M, xo_d, "b")
                sg = sb("sg", [128, 2, 343], F32, pm)
                sgi = [0]

                def gated(c, banksA, banksG, accumulate):
                    for ti, (a, z) in enumerate(TT):
                        n = z - a
                        s_ = sgi[0] % 2
                        sgi[0] += 1
                        K.op('act', lambda e, b=banksG[ti], s_=s_, n=n: e.activation(out=sg[:, s_, 0:n],
                                                                                   in_=ps[b][:, 0:n], func=AF.Sigmoid),
                             reads=[PS(banksG[ti])], writes=['sg%d' % s_])
                        if not accumulate:
                            K.op('dve', lambda e, b=banksA[ti], s_=s_, n=n, a=a, z=z: e.tensor_tensor(
                                out=QT[:, c, a:z], in0=ps[b][:, 0:n], in1=sg[:, s_, 0:n], op=ALU.mult),
                                reads=[PS(banksA[ti]), 'sg%d' % s_], writes=['m_%d' % c])
                        else:
                            K.op('dve', lambda e, b=banksA[ti], s_=s_, n=n: e.tensor_tensor(
                                out=sg[:, s_, 0:n], in0=ps[b][:, 0:n], in1=sg[:, s_, 0:n], op=ALU.mult),
                                reads=[PS(banksA[ti]), 'sg%d' % s_], writes=['sg%d' % s_])
                            K.op('dve', lambda e, s_=s_, n=n, a=a, z=z: e.tensor_tensor(
                                out=QT[:, c, a:z], in0=sg[:, s_, 0:n], in1=QT[:, c, a:z], op=ALU.add),
                                reads=['sg%d' % s_, 'm_%d' % c], writes=['m_%d' % c])

                with ExitStack() as pm2:
                    ybT = sb("ybT", [128, 16, NT], BF16, pm2)
                    K.wait('sp', yb_toks)
                    for h in range(16):
                        K.op('pool', lambda e, h=h: e.memset(ybT[:, h, 0:2], 0.0), writes=['yb_%d' % h])
                    for h in range(16):
                        K.dma('sp', ybT[:, h, 2:NT], yb_d[h][:, 2:NT], 'ybl', writes=['yb_%d' % h])
                    for h in range(16):
                        K.res['yb_%d' % h][0] = ('ybl', K.cnt['ybl'])
                    for c in range(16):
                        bA = dense(ybT, lambda kc: 'yb_%d' % kc)
                        bG = dense(u1T, lambda kc: 'u_%d' % kc)
                        gated(c, bA, bG, False)
                    K.barrier()
                with ExitStack() as pm1:
                    yaT = sb("yaT", [128, 16, NT], BF16, pm1)
                    with ExitStack() as p2b:
                        tb_ = sb("tbuf", [128, 2, NT + 2], F32, p2b)
                        hs = sb("hs", [128, 2, 343], F32, p2b)
                        Bs = sb("Bs", [128, 2, NT], F32, p2b)
                        yc = sb("yc", [128, 2, NT], F32, p2b)
                        K.op('dve', lambda e: e.memset(tb_[:, :, 0:2], 0.0), writes=['t0', 't1'])
                        hi = 0
                        for c in range(16):
                            ts_ = c % 2
                            slC = next_w()
                            slH = next_w()
                            slB = next_w()
                            for ti, (a, z) in enumerate(TT):
                                n = z - a
                                bC = K.bank()
                                mm_group(bC, 128, n, [(wring[:, slC, kc, :], u1T[:, kc, a:z]) for kc in range(16)],
                                         [['w%d' % slC, 'u_%d' % kc] for kc in range(16)])
                                bH = K.bank()
                                mm_group(bH, 128, n, [(wring[:, slH, kc, :], u1T[:, kc, a:z]) for kc in range(16)],
                                         [['w%d' % slH, 'u_%d' % kc] for kc in range(16)])
                                h_ = hi % 2
                                hi += 1
                                K.op('act', lambda e, bH=bH, h_=h_, n=n: e.copy(out=hs[:, h_, 0:n], in_=ps[bH][:, 0:n]),
                                     reads=[PS(bH)], writes=['hs%d' % h_])
                                K.op('dve', lambda e, bC=bC, h_=h_, n=n, a=a, z=z, ts_=ts_: e.tensor_tensor(
                                    out=tb_[:, ts_, 2 + a:2 + z], in0=ps[bC][:, 0:n], in1=hs[:, h_, 0:n], op=ALU.mult),
                                    reads=[PS(bC), 'hs%d' % h_], writes=['t%d' % ts_])
                            for ti, (a, z) in enumerate(TT):
                                n = z - a
                                bB = K.bank()
                                mm_group(bB, 128, n, [(wring[:, slB, kc, :], u1T[:, kc, a:z]) for kc in range(16)],
                                         [['w%d' % slB, 'u_%d' % kc] for kc in range(16)])
                                K.op('act', lambda e, bB=bB, n=n, a=a, z=z, ts_=ts_: e.copy(out=Bs[:, ts_, a:z],
                                                                                          in_=ps[bB][:, 0:n]),
                                     reads=[PS(bB)], writes=['Bs%d' % ts_])
                            K.op('dve', lambda e, ts_=ts_: e.tensor_scalar(out=tb_[:, ts_, 2:2 + HALO],
                                                                           in0=tb_[:, ts_, 2:2 + HALO],
                                                                           scalar1=halom[:, 0:1], scalar2=None,
                                                                           op0=ALU.mult),
                                 reads=['t%d' % ts_, 'halom'], writes=['t%d' % ts_])
                            K.op('dve', lambda e, ts_=ts_, c=c: e.tensor_scalar(out=yc[:, ts_, :],
                                                                                in0=tb_[:, ts_, 2:NT + 2],
                                                                                scalar1=cva[:, c, 2:3], scalar2=None,
                                                                                op0=ALU.mult),
                                 reads=['t%d' % ts_, 'cva'], writes=['yc%d' % ts_])
                            for jj in (1, 0):
                                K.op('dve', lambda e, ts_=ts_, c=c, jj=jj: e.scalar_tensor_tensor(
                                    out=yc[:, ts_, :], in0=tb_[:, ts_, jj:NT + jj], scalar=cva[:, c, jj:jj + 1],
                                    in1=yc[:, ts_, :], op0=ALU.mult, op1=ALU.add),
                                    reads=['t%d' % ts_, 'cva', 'yc%d' % ts_], writes=['yc%d' % ts_])
                            K.op('dve', lambda e, ts_=ts_, c=c: e.tensor_tensor(out=yaT[:, c, :], in0=Bs[:, ts_, :],
                                                                                 in1=yc[:, ts_, :], op=ALU.mult),
                                 reads=['Bs%d' % ts_, 'yc%d' % ts_], writes=['ya_%d' % c])
                        K.barrier()
                    for c in range(16):
                        bA = dense(yaT, lambda kc: 'ya_%d' % kc)
                        bG = dense(u1T, lambda kc: 'u_%d' % kc)
                        gated(c, bA, bG, True)
                    K.barrier()

            def layernorm(zT, ntile, cols, gi, bi, scope, tag, post):
                sq = sb("sq" + tag, [128, 2, 343], F32, scope)
                mean = sb("mean" + tag, [128, 343], F32, scope)
                rstd = sb("rstd" + tag, [128, 343], F32, scope)
                tmp = sb("tmp" + tag, [128, 2, 343], F32, scope)
                for ti, (a, z) in ntile:
                    n = z - a
                    b1 = K.bank()
                    mm_group(b1, 128, n, [(onesf[:, :], zT[:, c, a:z]) for c in range(16)],
                             [['onesf', 'z_%d' % c] for c in range(16)])
                    b2 = K.bank()
                    for c in range(16):
                        s_ = c % 2
                        K.op('act', lambda e, s_=s_, c=c, n=n, a=a, z=z: e.activation(out=sq[:, s_, 0:n],
                                                                                    in_=zT[:, c, a:z], func=AF.Square),
                             reads=['z_%d' % c], writes=['sq%d' % s_])
                        K.op('pe', lambda e, s_=s_, c=c, n=n: e.matmul(ps[b2][:, 0:n], lhsT=onesf[:, :],
                                                                       rhs=sq[:, s_, 0:n], start=(c == 0),
                                                                       stop=(c == 15)),
                             reads=['onesf', 'sq%d' % s_], writes=[PS(b2)], signal=True)
                    K.op('dve', lambda e, n=n: e.tensor_scalar(out=mean[:, 0:n], in0=ps[b1][:, 0:n], scalar1=1.0 / D,
                                                               scalar2=None, op0=ALU.mult),
                         reads=[PS(b1)], writes=['mean'])
                    K.op('dve', lambda e, n=n: e.tensor_tensor(out=rstd[:, 0:n], in0=mean[:, 0:n], in1=mean[:, 0:n],
                                                               op=ALU.mult), reads=['mean'], writes=['rstd'])
                    K.op('dve', lambda e, n=n: e.scalar_tensor_tensor(out=rstd[:, 0:n], in0=ps[b2][:, 0:n],
                                                                      scalar=1.0 / D, in1=rstd[:, 0:n],
                                                                      op0=ALU.mult, op1=ALU.subtract),
                         reads=[PS(b2), 'rstd'], writes=['rstd'])
                    K.op('act', lambda e, n=n: e.activation(out=rstd[:, 0:n], in_=rstd[:, 0:n], func=AF.Sqrt,
                                                            bias=epst[:, 0:1], scale=1.0),
                         reads=['rstd', 'epst'], writes=['rstd'])
                    K.op('dve', lambda e, n=n: e.reciprocal(out=rstd[:, 0:n], in_=rstd[:, 0:n]),
                         reads=['rstd'], writes=['rstd'])
                    for c in range(16):
                        s_ = c % 2
                        eng = 'dve'
                        K.op(eng, lambda e, s_=s_, c=c, n=n, a=a, z=z: e.tensor_tensor(
                            out=tmp[:, s_, 0:n], in0=zT[:, c, a:z], in1=mean[:, 0:n], op=ALU.subtract),
                            reads=['z_%d' % c, 'mean'], writes=['tmp%d' % s_])
                        K.op(eng, lambda e, s_=s_, n=n: e.tensor_tensor(
                            out=tmp[:, s_, 0:n], in0=tmp[:, s_, 0:n], in1=rstd[:, 0:n], op=ALU.mult),
                            reads=['tmp%d' % s_, 'rstd'], writes=['tmp%d' % s_])
                        K.op('dve', lambda e, s_=s_, c=c, n=n, a=a, z=z: e.tensor_scalar(
                            out=zT[:, c, a:z], in0=tmp[:, s_, 0:n], scalar1=lnp[:, gi, c:c + 1],
                            scalar2=lnp[:, bi, c:c + 1], op0=ALU.mult, op1=ALU.add),
                            reads=['tmp%d' % s_, 'lnp'], writes=['z_%d' % c])
                        post(c, ti, a, z)

            x1_toks = []
            u2_toks = []
            with ExitStack() as pl:
                zT = sb("zT", [128, 16, NT], F32, pl)
                xr2 = sb("xr2", [128, 2, NT], F32, pl)
                u2s = sb("u2s", [128, 2, 343], BF16, pl)
                for c in range(16):
                    sl = c % 2
                    K.dma('sp', xr2[:, sl, :], xo_d[c], 'xq%d' % sl, writes=['xq%d' % sl])
                    K.op('act', lambda e, sl=sl: e.mul(out=xr2[:, sl, :], in_=xr2[:, sl, :], mul=ALPHA),
                         reads=['xq%d' % sl], writes=['xq%d' % sl])
                    banks = dense(QT, lambda kc: 'm_%d' % kc)
                    for ti, (a, z) in enumerate(TT):
                        n = z - a
                        K.op('dve', lambda e, b=banks[ti], n=n, a=a, z=z, c=c, sl=sl: e.scalar_tensor_tensor(
                            out=zT[:, c, a:z], in0=ps[b][:, 0:n], scalar=modP[:, G_M + c:G_M + c + 1],
                            in1=xr2[:, sl, a:z], op0=ALU.mult, op1=ALU.add),
                            reads=[PS(banks[ti]), 'modA', 'modB', 'xq%d' % sl], writes=['z_%d' % c])
                u2i = [0]

                def post1(c, ti, a, z):
                    n = z - a
                    s_ = u2i[0] % 2
                    u2i[0] += 1
                    K.op('act', lambda e: e.activation(out=u2s[:, s_, 0:n], in_=zT[:, c, a:z], func=AF.Identity,
                                                       bias=modT[:, SH_F + c:SH_F + c + 1],
                                                       scale=modP[:, SC_F + c:SC_F + c + 1]),
                         reads=['z_%d' % c, 'modA', 'modB'], writes=['u2s%d' % s_])
                    u2_toks.append(K.dma('sp', u2_d[c][:, a:z], u2s[:, s_, 0:n], 'u2s%d' % s_, reads=['u2s%d' % s_]))
                    x1_toks.append(K.dma('sp', x1_d[c][:, a:z], zT[:, c, a:z], 'x1st', reads=['z_%d' % c]))
                layernorm(zT, list(enumerate(TT)), NT, 0, 1, pl, "1", post1)
                K.barrier()
            K.barrier()

        with ExitStack() as pf:
            gT = sb("gT", [128, NFC, NT], BF16, pf)
            with ExitStack() as pu:
                u2T = sb("u2T", [128, 16, NT], BF16, pu)
                araw = sb("araw", [128, 2, NT + 2], F32, pu)
                ac = sb("ac", [128, 2, NT], F32, pu)
                K.wait('sp', u2_toks)
                for c in range(16):
                    K.dma('sp', u2T[:, c, :], u2_d[c], 'u2l', writes=['u2_%d' % c])
                for c in range(16):
                    K.res['u2_%d' % c][0] = ('u2l', K.cnt['u2l'])
                K.op('dve', lambda e: e.memset(araw[:, :, 0:2], 0.0), writes=['ar0', 'ar1'])
                for j in range(NFC):
                    as_ = j % 2
                    bA = dense(u2T, lambda kc: 'u2_%d' % kc)
                    for ti, (a, z) in enumerate(TT):
                        K.op('act', lambda e, b=bA[ti], a=a, z=z, as_=as_: e.copy(out=araw[:, as_, 2 + a:2 + z],
                                                                                in_=ps[b][:, 0:z - a]),
                             reads=[PS(bA[ti])], writes=['ar%d' % as_])
                    K.op('dve', lambda e, as_=as_: e.tensor_scalar(out=araw[:, as_, 2:2 + HALO],
                                                                   in0=araw[:, as_, 2:2 + HALO],
                                                                   scalar1=halom[:, 0:1], scalar2=None, op0=ALU.mult),
                         reads=['ar%d' % as_, 'halom'], writes=['ar%d' % as_])
                    K.op('act', lambda e, as_=as_, j=j: e.activation(out=ac[:, as_, :], in_=araw[:, as_, 2:NT + 2],
                                                                        func=AF.Identity, scale=cvf[:, j, 2:3]),
                         reads=['ar%d' % as_, 'cvf'], writes=['ac%d' % as_])
                    for jj in (1, 0):
                        K.op('dve', lambda e, as_=as_, j=j, jj=jj: e.scalar_tensor_tensor(
                            out=ac[:, as_, :], in0=araw[:, as_, jj:NT + jj], scalar=cvf[:, j, jj:jj + 1],
                            in1=ac[:, as_, :], op0=ALU.mult, op1=ALU.add),
                            reads=['ar%d' % as_, 'cvf', 'ac%d' % as_], writes=['ac%d' % as_])
                    K.op('act', lambda e, as_=as_: e.activation(out=ac[:, as_, :], in_=ac[:, as_, :],
                                                                func=AF.Gelu_apprx_tanh),
                         reads=['ac%d' % as_], writes=['ac%d' % as_])
                    bB = dense(u2T, lambda kc: 'u2_%d' % kc)
                    for ti, (a, z) in enumerate(TT):
                        K.op('dve', lambda e, b=bB[ti], a=a, z=z, as_=as_, j=j: e.tensor_tensor(
                            out=gT[:, j, a:z], in0=ps[b][:, 0:z - a], in1=ac[:, as_, a:z], op=ALU.mult),
                            reads=[PS(bB[ti]), 'ac%d' % as_], writes=['g_%d' % j])
                K.barrier()
            with ExitStack() as pd:
                z2 = sb("z2", [128, 16, NT], F32, pd)
                x1r = sb("x1r", [128, NT], F32, pd)
                K.wait('sp', x1_toks)
                out_toks = []
                for c in range(16):
                    K.dma('sp', x1r[:, :], x1_d[c], 'x1r0', writes=['x1r0'])
                    K.op('act', lambda e: e.mul(out=x1r[:, :], in_=x1r[:, :], mul=ALPHA),
                         reads=['x1r0'], writes=['x1r0'])
                    slots = [next_w() for part in range(3)]
                    for ti, (a, z) in enumerate(TT):
                        n = z - a
                        b = K.bank()
                        pairs, rl = [], []
                        for part in range(3):
                            for kc in range(min(16, NFC - part * 16)):
                                pairs.append((wring[:, slots[part], kc, :], gT[:, part * 16 + kc, a:z]))
                                rl.append(['w%d' % slots[part], 'g_%d' % (part * 16 + kc)])
                        mm_group(b, 128, n, pairs, rl)
                        K.op('dve', lambda e, b=b, n=n, c=c, a=a, z=z: e.scalar_tensor_tensor(
                            out=z2[:, c, a:z], in0=ps[b][:, 0:n], scalar=modP[:, G_F + c:G_F + c + 1],
                            in1=x1r[:, a:z], op0=ALU.mult, op1=ALU.add),
                            reads=[PS(b), 'modA', 'modB', 'x1r0'], writes=['z_%d' % c])
                for ti, (a, z) in enumerate(TT):
                    def post2(c, ti_, a_, z_):
                        if c == 15:
                            out_toks.append(K.dma('sp', out_d[:, :, a_:z_].rearrange("c p t -> p c t"), z2[:, :, a_:z_],
                                                  'outst', reads=['z_%d' % cc for cc in range(16)]))
                    with ExitStack() as pln:
                        layernorm(z2, [(ti, (a, z))], NT, 2, 3, pln, "2_%d" % ti, post2)
                        K.barrier()
                K.wait('sp', [('outst', K.cnt['outst'])])
                if debug:
                    K.wait('sp', [('dbg', K.cnt['dbg'])])
    return nc


_NC_CACHE = {}


def _prep(inputs):
    f32 = np.float32
    x = np.asarray(inputs['x'], f32)
    c = np.asarray(inputs['c'], f32)
    W = {k: np.asarray(inputs[k][0], f32) for k in ('w_cond', 'w_in', 'w_a', 'w_b', 'w_o', 'w_up', 'w_down')}
    items, seq = stream_plan()

    def fm(Wm, col0, ncols, kc0=0, nkc=16):
        blk = Wm[kc0 * 128:(kc0 + nkc) * 128, col0:col0 + ncols].reshape(nkc, 128, ncols).transpose(1, 0, 2)
        if nkc < 16:
            blk = np.concatenate([blk, np.zeros((128, 16 - nkc, ncols), f32)], axis=1)
        return blk
    ws = np.empty((len(items), 128, 16, 128), f32)
    for i, (nm, col0, kc0, nkc) in enumerate(items):
        ws[i] = fm(W[nm], col0, 128, kc0, nkc)
    wcond = np.ascontiguousarray(W['w_cond'].reshape(16, 128, 24, 512).transpose(2, 1, 0, 3))
    bcond = np.ascontiguousarray(np.asarray(inputs['b_cond'], f32).reshape(96, 128).T)
    wk = np.ascontiguousarray(fm(W['w_in'], O_K, 512))
    wv = np.ascontiguousarray(fm(W['w_in'], O_V, 512))
    wki = np.ascontiguousarray(fm(W['w_in'], O_KI, 64))
    wwi = np.ascontiguousarray(fm(W['w_in'], O_WI, 16))

    def pv(v):
        return np.ascontiguousarray(np.asarray(v, f32).reshape(-1, 128).T)
    cva = np.ascontiguousarray(np.stack([pv(inputs['conv_a'][0][j]) for j in range(3)], axis=2))
    cvf = np.ascontiguousarray(np.stack([pv(inputs['conv_f'][0][j]) for j in range(3)], axis=2))
    lnp = np.ascontiguousarray(np.stack([pv(inputs['ln1_g'][0]), pv(inputs['ln1_b'][0]),
                                         pv(inputs['ln2_g'][0]), pv(inputs['ln2_b'][0])], axis=1))
    kng = np.ascontiguousarray(np.broadcast_to(np.stack([np.asarray(inputs['idx_kn_g'][0], f32),
                                                         np.asarray(inputs['idx_kn_b'][0], f32)])[None], (128, 2, 64)))
    cidx = np.ascontiguousarray(np.broadcast_to((np.arange(64, dtype=f32) * 64.0)[None], (128, 64)))
    ident = np.eye(128, dtype=f32)
    maps = []
    for core in range(8):
        b, q = core // 4, core % 4
        s0 = q * 1024
        g = s0 - HALO + np.arange(NT)
        xo = np.zeros((NT, D), f32)
        valid = g >= 0
        xo[valid] = x[b, g[valid]]
        xo = np.ascontiguousarray(xo.T.reshape(16, 128, NT))
        xs = np.ascontiguousarray(x[b].T.reshape(16, 128, 16, 256).transpose(2, 1, 0, 3))
        cT = np.ascontiguousarray(c[b].reshape(16, 128).T)
        gq = g[2:2 + NQT * QTS].reshape(NQT, QTS)
        lim = np.clip((np.floor_divide(gq, 64) + 1) * 64, 64, S).astype(f32)
        qlim = np.full((128, NQT), float(S), f32)
        qlim[:QTS, :] = lim.T
        halom = np.full((128, 1), 0.0 if q == 0 else 1.0, f32)
        m = dict(xo=xo, xs=xs, cT=cT, wcond=wcond, bcondT=bcond, ws=ws, wk=wk, wv=wv, wki=wki, wwi=wwi,
                 cva=cva, cvf=cvf, lnp=lnp, kng=kng, qlim=qlim, halom=halom, cidx=cidx, ident=ident)
        maps.append({"i_" + k_: v_ for k_, v_ in m.items()})
    return maps


def _assemble(results):
    out = np.empty((2, S, D), np.float32)
    for core in range(8):
        b, q = core // 4, core % 4
        o = np.asarray(results[core]["out"], np.float32)
        out[b, q * 1024:(q + 1) * 1024, :] = o.reshape(D, NT)[:, HALO:].T
    return out


def kernel(**inputs):
    if 'nc' not in _NC_CACHE:
        _NC_CACHE['nc'] = build_nc(False)
    maps = _prep(inputs)
    res = run_bass_kernel_spmd(_NC_CACHE['nc'], maps, core_ids=list(range(8)))
    return _assemble(res.results)
```

```python
import numpy as np
import ml_dtypes
from contextlib import ExitStack
import concourse.bass as bass
import concourse.mybir as mybir
from concourse.bass_utils import run_bass_kernel_spmd

F32 = mybir.dt.float32
BF16 = mybir.dt.bfloat16
AF = mybir.ActivationFunctionType
ALU = mybir.AluOpType
AX = mybir.AxisListType

D = 2048
S = 4096
NT = 1028
HALO = 4
TT = [(0, 343), (343, 686), (686, 1028)]
NQT = 9
QTS = 114
NQB = 3
QBS = 342
DFF = 5632
NFC = 44
ALPHA = 2.0 ** 0.25
LN_EPS = 1e-5
IDX_W_SCALE = 1024.0 ** -0.5
TOPK = 256.0
BIG = 1.0e6
NBIS = 16
WR = 8
O_B, O_C, O_H, O_Q, O_K, O_V, O_QI, O_KI, O_WI, O_GA, O_GB = 0, 2048, 4096, 6144, 8192, 8704, 9216, 10240, 10304, 10320, 12368


def stream_plan():
    items, seq = [], []

    def add(*d):
        items.append(d)
        seq.append(len(items) - 1)
    for h in range(16):
        add('w_in', O_Q + h * 128, 0, 16)
    for c in range(8):
        add('w_in', O_QI + c * 128, 0, 16)
    for c in range(16):
        add('w_b', c * 128, 0, 16)
        add('w_in', O_GB + c * 128, 0, 16)
    for c in range(16):
        add('w_in', O_C + c * 128, 0, 16)
        add('w_in', O_H + c * 128, 0, 16)
        add('w_in', O_B + c * 128, 0, 16)
    for c in range(16):
        add('w_a', c * 128, 0, 16)
        add('w_in', O_GA + c * 128, 0, 16)
    for c in range(16):
        add('w_o', c * 128, 0, 16)
    for j in range(NFC):
        add('w_up', j * 128, 0, 16)
        add('w_up', DFF + j * 128, 0, 16)
    base = len(items)
    for c in range(16):
        for part in range(3):
            items.append(('w_down', c * 128, part * 16, min(16, NFC - part * 16)))
    for c in range(16):
        for part in range(3):
            seq.append(base + c * 3 + part)
    return items, seq


class KB:
    def __init__(self, nc, es):
        self.nc, self.es = nc, es
        self.engs = {'pe': nc.tensor, 'act': nc.scalar, 'dve': nc.vector, 'pool': nc.gpsimd, 'sp': nc.sync}
        self.sems = {k: es.enter_context(nc.semaphore('s_' + k)) for k in self.engs}
        self.cnt = {k: 0 for k in self.engs}
        self.waited = {}
        self.res = {}
        self.pbank = 0

    def dsem(self, name):
        if name not in self.sems:
            self.sems[name] = self.es.enter_context(self.nc.semaphore('d_' + name))
            self.cnt[name] = 0
        return self.sems[name]

    def _deps(self, reads, writes):
        deps = []
        for r in reads:
            st = self.res.get(r)
            if st and st[0]:
                deps.append(st[0])
        for w in writes:
            st = self.res.get(w)
            if st:
                if st[0]:
                    deps.append(st[0])
                deps.extend(st[1])
        return deps

    def wait(self, eng, deps):
        mx = {}
        for (key, val) in deps:
            mx[key] = max(mx.get(key, 0), val)
        for (key, val) in mx.items():
            if key == eng and eng == 'pe':
                continue
            if self.waited.get((eng, key), 0) >= val:
                continue
            self.engs[eng].wait_ge(self.sems[key], val)
            self.waited[(eng, key)] = val

    def _upd(self, tok, reads, writes):
        for r in reads:
            st = self.res.setdefault(r, [None, []])
            st[1].append(tok)
        for w in writes:
            self.res[w] = [tok, []]

    def op(self, eng, fn, reads=(), writes=(), signal=True):
        self.wait(eng, self._deps(reads, writes))
        ins = fn(self.engs[eng])
        if signal:
            self.cnt[eng] += 1
            ins.then_inc(self.sems[eng], 1)
            tok = (eng, self.cnt[eng])
        else:
            tok = (eng, self.cnt[eng] + 1)
        self._upd(tok, reads, writes)
        return tok

    def dma(self, queue, out, in_, sem, reads=(), writes=()):
        self.dsem(sem)
        self.wait(queue, self._deps(reads, writes))
        ins = self.engs[queue].dma_start(out=out, in_=in_)
        self.cnt[sem] += 16
        ins.then_inc(self.sems[sem], 16)
        tok = (sem, self.cnt[sem])
        self._upd(tok, reads, writes)
        return tok

    def barrier(self):
        toks = [(k, v) for k, v in self.cnt.items() if v > 0]
        for e in ('pe', 'act', 'dve', 'pool', 'sp'):
            self.wait(e, toks)

    def bank(self):
        b = self.pbank
        self.pbank = (self.pbank + 1) % 8
        return b


def build_nc(debug=False):
    nc = bass.Bass("TRN2", target_bir_lowering=False)
    items, seq = stream_plan()
    NI = len(items)

    def din(name, shape, dt=F32):
        return nc.dram_tensor("i_" + name, shape, dt, kind="ExternalInput").ap()
    xo_d = din("xo", [16, 128, NT])
    xs_d = din("xs", [16, 128, 16, 256])
    cT_d = din("cT", [128, 16])
    wc_d = din("wcond", [24, 128, 16, 512])
    bcT_d = din("bcondT", [128, 96])
    ws_d = din("ws", [NI, 128, 16, 128])
    wk_d = din("wk", [128, 16, 512])
    wv_d = din("wv", [128, 16, 512])
    wki_d = din("wki", [128, 16, 64])
    wwi_d = din("wwi", [128, 16, 16])
    cva_d = din("cva", [128, 16, 3])
    cvf_d = din("cvf", [128, NFC, 3])
    lnp_d = din("lnp", [128, 4, 16])
    kng_d = din("kng", [128, 2, 64])
    qlim_d = din("qlim", [128, NQT])
    halo_d = din("halom", [128, 1])
    cidx_d = din("cidx", [128, 64])
    ident_d = din("ident", [128, 128])
    okind = "ExternalOutput" if debug else "Internal"
    out_d = nc.dram_tensor("out", [16, 128, NT], F32, kind="ExternalOutput").ap()
    kT_d = nc.dram_tensor("kT_s", [4, 128, S], BF16, kind=okind).ap()
    v_d = nc.dram_tensor("v_s", [4, 128, 32, 128], BF16, kind=okind).ap()
    yb_d = nc.dram_tensor("yb_s", [16, 128, NT], BF16, kind=okind).ap()
    x1_d = nc.dram_tensor("x1_s", [16, 128, NT], F32, kind=okind).ap()
    mod_o = nc.dram_tensor("mod_o", [128, 96], F32, kind=okind).ap()

    with ExitStack() as es:
        K = KB(nc, es)

        def sb(name, shape, dt=F32, scope=es):
            return scope.enter_context(nc.sbuf_tensor(name, shape, dt))
        ps = [es.enter_context(nc.psum_tensor("ps%d" % i, [128, 512], F32)) for i in range(8)]

        def PS(b):
            return "ps%d" % b

        ident = sb("ident", [128, 128], BF16)
        identf = sb("identf", [128, 128], F32)
        onesb = sb("onesb", [128, 128], BF16)
        onesf = sb("onesf", [128, 128], F32)
        one11 = sb("one11", [1, 2], F32)
        epst = sb("epst", [128, 1], F32)
        cva = sb("cva", [128, 16, 3])
        cvf = sb("cvf", [128, NFC, 3])
        lnp = sb("lnp", [128, 4, 16])
        kng = sb("kng", [128, 2, 64])
        qlim = sb("qlim", [128, NQT])
        halom = sb("halom", [128, 1])
        cidx = sb("cidx", [128, 64])
        modT = sb("modT", [128, 96])
        modP = sb("modP", [128, 96])
        wring = sb("wring", [128, WR, 16, 128], BF16)
        pk = ExitStack()
        kiTE = sb("kiTE", [128, S], BF16, pk)
        kiTO = sb("kiTO", [128, S], BF16, pk)
        for (t, d_) in ((identf, ident_d), (cva, cva_d), (cvf, cvf_d), (lnp, lnp_d), (kng, kng_d), (qlim, qlim_d),
                        (halom, halo_d), (cidx, cidx_d)):
            K.dma('sp', t[:], d_, 'c_' + t.name, writes=[t.name])
        K.op('dve', lambda e: e.tensor_copy(out=ident[:], in_=identf[:]), reads=['identf'], writes=['ident'])
        K.op('dve', lambda e: e.memset(onesb[:], 1.0), writes=['onesb'])
        K.op('dve', lambda e: e.memset(onesf[:], 1.0), writes=['onesf'])
        K.op('dve', lambda e: e.memset(one11[:], 1.0), writes=['one11'])
        K.op('dve', lambda e: e.memset(epst[:], LN_EPS), writes=['epst'])

        wst = {'pos': 0, 'loaded': 0}

        def w_prefetch(upto):
            while wst['loaded'] < min(upto, len(seq)):
                i = wst['loaded']
                slot = i % WR
                K.dma('pool', wring[:, slot, :, :], ws_d[seq[i]], 'w%d' % slot, writes=['w%d' % slot])
                wst['loaded'] += 1

        def next_w():
            i = wst['pos']
            w_prefetch(i + 1)
            wst['pos'] += 1
            w_prefetch(i + WR - 2)
            return i % WR

        def mm_group(bank, M, n, pairs, reads_list, moff=0):
            last = len(pairs) - 1
            for i, (l, r) in enumerate(pairs):
                K.op('pe', lambda e, l=l, r=r, i=i: e.matmul(ps[bank][moff:moff + M, 0:n], lhsT=l, rhs=r,
                                                              start=(i == 0), stop=(i == last)),
                     reads=reads_list[i], writes=[PS(bank)], signal=(i == last))

        p0 = ExitStack()
        cTs = sb("cTs", [128, 16], F32, p0)
        cact = sb("cact", [128, 16], BF16, p0)
        wcb = sb("wcb", [128, 2, 16, 512], BF16, p0)
        bcT = sb("bcT", [128, 96], F32, p0)
        K.dma('sp', cTs[:], cT_d, 'c_cTs', writes=['cTs'])
        K.dma('sp', bcT[:], bcT_d, 'c_bcT', writes=['bcT'])
        K.op('act', lambda e: e.activation(out=cact[:], in_=cTs[:], func=AF.Silu), reads=['cTs'], writes=['cact'])

        def mod_block(nb):
            sl = nb % 2
            mres = 'modA' if nb < 8 else 'modB'
            K.dma('pool', wcb[:, sl, :, :], wc_d[nb], 'wc%d' % sl, writes=['wc%d' % sl])
            b = K.bank()
            for cc in range(4):
                for kc in range(16):
                    K.op('pe', lambda e, cc=cc, kc=kc: e.matmul(ps[b][:, cc:cc + 1],
                                                                lhsT=wcb[:, sl, kc, cc * 128:(cc + 1) * 128],
                                                                rhs=cact[:, kc:kc + 1], start=(kc == 0), stop=(kc == 15)),
                         reads=['cact', 'wc%d' % sl], writes=[PS(b)], signal=(kc == 15 and cc == 3))
            K.op('dve', lambda e: e.tensor_tensor(out=modT[:, nb * 4:(nb + 1) * 4], in0=ps[b][:, 0:4],
                                                  in1=bcT[:, nb * 4:(nb + 1) * 4], op=ALU.add),
                 reads=[PS(b), 'bcT'], writes=[mres])
            K.op('dve', lambda e: e.tensor_scalar(out=modP[:, nb * 4:(nb + 1) * 4], in0=modT[:, nb * 4:(nb + 1) * 4],
                                                  scalar1=1.0, scalar2=None, op0=ALU.add),
                 reads=[mres], writes=[mres])
        for nb in range(8):
            mod_block(nb)
        SH_M, SC_M, G_M, SH_F, SC_F, G_F = 0, 16, 32, 48, 64, 80

        def modulate(eng_i, out_ap, in_ap, kc, scb, shb, reads, writes):
            if eng_i % 2 == 0:
                K.op('dve', lambda e: e.tensor_scalar(out=out_ap, in0=in_ap, scalar1=modP[:, scb + kc:scb + kc + 1],
                                                      scalar2=modT[:, shb + kc:shb + kc + 1], op0=ALU.mult, op1=ALU.add),
                     reads=reads + (['modA'] if scb < 32 else ['modB']), writes=writes)
            else:
                K.op('act', lambda e: e.activation(out=out_ap, in_=in_ap, func=AF.Identity,
                                                   bias=modT[:, shb + kc:shb + kc + 1],
                                                   scale=modP[:, scb + kc:scb + kc + 1]),
                     reads=reads + (['modA'] if scb < 32 else ['modB']), writes=writes)

        store_toks = []
        with ExitStack() as p1:
            wk = sb("wk", [128, 16, 512], BF16, p1)
            wv = sb("wv", [128, 16, 512], BF16, p1)
            wki = sb("wki", [128, 16, 64], BF16, p1)
            xsb = sb("xsb", [128, 2, 16, 256], F32, p1)
            usb = sb("usb", [128, 2, 16, 256], BF16, p1)
            kst = sb("kst", [128, 2, 4, 256], BF16, p1)
            vst = sb("vst", [128, 2, 2, 512], BF16, p1)
            kraw = sb("kraw", [128, 2, 64], F32, p1)
            kn2 = sb("kn2", [128, 2, 2, 128], BF16, p1)
            K.op('dve', lambda e: e.memset(kn2[:], 0.0), writes=['ki0n', 'ki1n'])
            bst = sb("bst", [128, 2, 6], F32, p1)
            bmv = sb("bmv", [128, 2, 2], F32, p1)
            K.dma('pool', wk[:], wk_d, 'c_wk', writes=['wk'])
            K.dma('pool', wv[:], wv_d, 'c_wv', writes=['wv'])
            K.dma('pool', wki[:], wki_d, 'c_wki', writes=['wki'])
            w_prefetch(WR - 2)
            K.dma('sp', xsb[:, 0, :, :], xs_d[0], 'xs0', writes=['xs0'])
            for tb in range(16):
                sl = tb % 2
                if tb + 1 < 16:
                    K.dma('sp', xsb[:, 1 - sl, :, :], xs_d[tb + 1], 'xs%d' % (1 - sl), writes=['xs%d' % (1 - sl)])
                for kc in range(16):
                    modulate(kc, usb[:, sl, kc, :], xsb[:, sl, kc, :], kc, SC_M, SH_M,
                             ['xs%d' % sl], ['us%d_%d' % (sl, kc)])
                for n in range(4):
                    b = K.bank()
                    mm_group(b, 128, 256, [(wk[:, kc, n * 128:(n + 1) * 128], usb[:, sl, kc, :]) for kc in range(16)],
                             [['wk', 'us%d_%d' % (sl, kc)] for kc in range(16)])
                    K.op('act', lambda e, b=b, n=n: e.copy(out=kst[:, sl, n, :], in_=ps[b][:, 0:256]),
                         reads=[PS(b)], writes=['kst%d' % sl])
                store_toks.append(K.dma('sp', kT_d[:, :, tb * 256:(tb + 1) * 256].rearrange("n p t -> p n t"),
                                        kst[:, sl, :, :], 'kst%d' % sl, reads=['kst%d' % sl]))
                for sub in range(2):
                    tsl = slice(sub * 128, (sub + 1) * 128)
                    b = K.bank()
                    mm_group(b, 128, 512, [(usb[:, sl, kc, tsl], wv[:, kc, :]) for kc in range(16)],
                             [['wv', 'us%d_%d' % (sl, kc)] for kc in range(16)])
                    K.op('dve', lambda e, b=b, sub=sub: e.tensor_copy(out=vst[:, sl, sub, :], in_=ps[b][:, 0:512]),
                         reads=[PS(b)], writes=['vst%d' % sl])
                    b = K.bank()
                    mm_group(b, 128, 64, [(usb[:, sl, kc, tsl], wki[:, kc, :]) for kc in range(16)],
                             [['wki', 'us%d_%d' % (sl, kc)] for kc in range(16)])
                    r_ = 'ki%d' % sub
                    K.op('act', lambda e, b=b, sub=sub: e.copy(out=kraw[:, sub, :], in_=ps[b][:, 0:64]),
                         reads=[PS(b)], writes=[r_])
                    K.op('dve', lambda e, sub=sub: e.bn_stats(out=bst[:, sub, :], in_=kraw[:, sub, :]),
                         reads=[r_], writes=[r_ + 's'])
                    K.op('dve', lambda e, sub=sub: e.bn_aggr(out=bmv[:, sub, :], in_=bst[:, sub, :]),
                         reads=[r_ + 's'], writes=[r_ + 'm'])
                    K.op('act', lambda e, sub=sub: e.activation(out=bmv[:, sub, 1:2], in_=bmv[:, sub, 1:2], func=AF.Sqrt,
                                                                bias=epst[:, 0:1], scale=1.0),
                         reads=[r_ + 'm', 'epst'], writes=[r_ + 'm'])
                    K.op('dve', lambda e, sub=sub: e.reciprocal(out=bmv[:, sub, 1:2], in_=bmv[:, sub, 1:2]),
                         reads=[r_ + 'm'], writes=[r_ + 'm'])
                    K.op('dve', lambda e, sub=sub: e.tensor_scalar(out=kraw[:, sub, :], in0=kraw[:, sub, :],
                                                                   scalar1=bmv[:, sub, 0:1], scalar2=bmv[:, sub, 1:2],
                                                                   op0=ALU.subtract, op1=ALU.mult),
                         reads=[r_, r_ + 'm'], writes=[r_])
                    K.op('dve', lambda e, sub=sub: e.tensor_tensor(out=kraw[:, sub, :], in0=kraw[:, sub, :],
                                                                   in1=kng[:, 0, :], op=ALU.mult),
                         reads=[r_, 'kng'], writes=[r_])
                    K.op('dve', lambda e, sub=sub: e.tensor_tensor(out=kn2[:, sub, 0, 0:64], in0=kraw[:, sub, :],
                                                                   in1=kng[:, 1, :], op=ALU.add),
                         reads=[r_, 'kng'], writes=[r_ + 'n'])
                    K.op('dve', lambda e, sub=sub: e.tensor_copy(out=kn2[:, sub, 1, 64:128], in_=kn2[:, sub, 0, 0:64]),
                         reads=[r_ + 'n'], writes=[r_ + 'n'])
                    b = K.bank()
                    pT = ps[b][:].bitcast(BF16)
                    for eo in range(2):
                        K.op('pe', lambda e, pT=pT, sub=sub, eo=eo: e.transpose(out=pT[:, eo * 128:(eo + 1) * 128],
                                                                                in_=kn2[:, sub, eo, :],
                                                                                identity=ident[:, :]),
                             reads=[r_ + 'n', 'ident'], writes=[PS(b)], signal=(eo == 1))
                    t0 = tb * 256 + sub * 128
                    K.op('act', lambda e, pT=pT, t0=t0: e.copy(out=kiTE[:, t0:t0 + 128], in_=pT[:, 0:128]),
                         reads=[PS(b)], writes=['kiT'])
                    K.op('act', lambda e, pT=pT, t0=t0: e.copy(out=kiTO[:, t0:t0 + 128], in_=pT[:, 128:256]),
                         reads=[PS(b)], writes=['kiT'])
                for n in range(4):
                    store_toks.append(K.dma('sp', v_d[n][:, tb * 2:tb * 2 + 2, :], vst[:, sl, :, n * 128:(n + 1) * 128],
                                            'vst%d' % sl, reads=['vst%d' % sl]))
                mod_block(8 + tb)
            K.barrier()
        if debug:
            K.dma('sp', mod_o, modT[:], 'dbg', reads=['modA', 'modB'])
        p0.close()

        def build_u(uT, scope, scb, shb, src_d, tag):
            xr = sb("xr_" + tag, [128, 2, NT], F32, scope)
            for kc in range(16):
                sl = kc % 2
                K.dma('sp', xr[:, sl, :], src_d[kc], 'xr%d' % sl, writes=['xr%d' % sl])
                modulate(kc, uT[:, kc, :], xr[:, sl, :], kc, scb, shb, ['xr%d' % sl], ['u_%d' % kc])

        def dense(rhsT, rhs_res, nkc=16):
            slot = next_w()
            banks = []
            for (a, z) in TT:
                b = K.bank()
                mm_group(b, 128, z - a, [(wring[:, slot, kc, :], rhsT[:, kc, a:z]) for kc in range(nkc)],
                         [['w%d' % slot, rhs_res(kc)] for kc in range(nkc)])
                banks.append(b)
            return banks

        with ExitStack() as pA:
            qT = sb("qT", [128, 16, NT], BF16, pA)
            qiT = sb("qiT", [128, 8, NT], BF16, pA)
            wtm = sb("wtm", [128, NQT, 16], F32, pA)
            with ExitStack() as p2a:
                u1T = sb("u1T", [128, 16, NT], BF16, p2a)
                wwi = sb("wwi", [128, 16, 16], BF16, p2a)
                K.dma('pool', wwi[:], wwi_d, 'c_wwi', writes=['wwi'])
                build_u(u1T, p2a, SC_M, SH_M, xo_d, "a")
                for h in range(24):
                    banks = dense(u1T, lambda kc: 'u_%d' % kc)
                    for ti, (a, z) in enumerate(TT):
                        b = banks[ti]
                        dst = qT[:, h, a:z] if h < 16 else qiT[:, h - 16, a:z]
                        dres = 'q_%d' % h
                        if (h + ti) % 2 == 0:
                            K.op('act', lambda e, b=b, dst=dst, n=z - a: e.copy(out=dst, in_=ps[b][:, 0:n]),
                                 reads=[PS(b)], writes=[dres])
                        else:
                            K.op('dve', lambda e, b=b, dst=dst, n=z - a: e.tensor_copy(out=dst, in_=ps[b][:, 0:n]),
                                 reads=[PS(b)], writes=[dres])
                for qt in range(NQT):
                    j0 = 2 + qt * QTS
                    b = K.bank()
                    mm_group(b, QTS, 16, [(u1T[:, kc, j0:j0 + QTS], wwi[:, kc, :]) for kc in range(16)],
                             [['wwi', 'u_%d' % kc] for kc in range(16)])
                    K.op('act', lambda e, b=b, qt=qt: e.mul(out=wtm[0:QTS, qt, :], in_=ps[b][0:QTS, 0:16],
                                                            mul=IDX_W_SCALE),
                         reads=[PS(b)], writes=['wtm'])
                K.barrier()

            with ExitStack() as pt:
                score = sb("score", [128, 2, S], F32, pt)
                dg = sb("dg", [128, 16, QTS], BF16, pt)
                rb = sb("rb", [128, 4, 512], BF16, pt)
                mask = sb("mask", [128, S], BF16, pt)
                maskT = sb("maskT", [128, 32, QBS], BF16, pt)
                pen = sb("pen", [128, 2, 64], F32, pt)
                bis = sb("bis", [128, 8], F32, pt)
                nst = sb("nst", [128, NBIS], F32, pt)
                pwn = sb("pwn", [128, NBIS], F32, pt)
                c35 = sb("c35", [128, 1], F32, pt)
                knb = sb("knb", [128, 2, S], BF16, pt)
                vnb = sb("vnb", [128, 32, 128], BF16, pt)
                Eb = sb("Eb", [128, 4, QBS], BF16, pt)
                Em = sb("Em", [128, 4, QBS], BF16, pt)
                rz = sb("rz", [128, QBS], F32, pt)
                ybs = sb("ybs", [128, 2, QBS], BF16, pt)
                for k_ in range(NBIS):
                    K.op('dve', lambda e, k_=k_: e.memset(pwn[:, k_:k_ + 1], -(2.0 ** -(k_ + 1))), writes=['pwn'])
                K.op('dve', lambda e: e.memset(c35[:], float(S) - 2.0 * TOPK + 0.5), writes=['c35'])
                K.wait('sp', store_toks)
                kvload = [0]
                kvslot = {}
                yb_toks = []
                LOOK = 3
                SBK = [4, 5, 6, 7]
                P = slice(0, QTS)
                uctr = [0]

                def emit_acc(qt):
                    sb_ = qt % 2
                    j0 = 2 + qt * QTS
                    K.op('dve', lambda e: e.tensor_scalar(out=pen[P, sb_, :], in0=cidx[P, :],
                                                          scalar1=qlim[P, qt:qt + 1], scalar2=-BIG,
                                                          op0=ALU.is_ge, op1=ALU.mult),
                         reads=['cidx', 'qlim'], writes=['pen%d' % sb_])
                    for h in range(16):
                        K.op('dve', lambda e, h=h: e.tensor_scalar(out=dg[P, h, :], in0=ident[P, 0:QTS],
                                                                   scalar1=wtm[P, qt, h:h + 1], scalar2=None,
                                                                   op0=ALU.mult),
                             reads=['ident', 'wtm'], writes=['dg%d' % h])
                    for kb in range(8):
                        ks = slice(kb * 512, (kb + 1) * 512)
                        sbank = kb % 2
                        rings = {}

                        def emit_D(h):
                            u = uctr[0]
                            uctr[0] += 1
                            bank = 2 + (u % 6)
                            ring = u % 4
                            kz = kiTE if h % 2 == 0 else kiTO
                            K.op('pe', lambda e: e.matmul(ps[bank][P, 0:512], lhsT=qiT[:, h // 2, j0:j0 + QTS],
                                                          rhs=kz[:, ks], start=True, stop=True),
                                 reads=['q_%d' % (16 + h // 2), 'kiT'], writes=[PS(bank)])
                            if False:
                                K.op('act', lambda e: e.activation(out=rb[P, ring, :], in_=ps[bank][P, 0:512],
                                                                   func=AF.Relu),
                                     reads=[PS(bank)], writes=['rb%d' % ring])
                            else:
                                K.op('dve', lambda e: e.tensor_scalar(out=rb[P, ring, :], in0=ps[bank][P, 0:512],
                                                                      scalar1=0.0, scalar2=None, op0=ALU.max),
                                     reads=[PS(bank)], writes=['rb%d' % ring])
                            rings[h] = ring
                        for h in range(2):
                            emit_D(h)
                        for h in range(16):
                            if h + 2 < 16:
                                emit_D(h + 2)
                            K.op('pe', lambda e, h=h, r_=rings[h]: e.matmul(
                                ps[sbank][P, 0:512], lhsT=dg[P, h, :], rhs=rb[P, r_, :],
                                start=(h == 0), stop=(h == 15)),
                                reads=['dg%d' % h, 'rb%d' % rings[h]], writes=[PS(sbank)], signal=(h == 15))
                        K.op('dve', lambda e: e.tensor_tensor(
                            out=score[P, sb_, ks].rearrange("p (c j) -> p c j", j=64),
                            in0=ps[sbank][P, 0:512].rearrange("p (c j) -> p c j", j=64),
                            in1=pen[P, sb_, kb * 8:(kb + 1) * 8].unsqueeze(2).to_broadcast([QTS, 8, 64]), op=ALU.add),
                            reads=[PS(sbank), 'pen%d' % sb_], writes=['sc%d_%d' % (sb_, kb)])

                def emit_bis(qt):
                    sb_ = qt % 2
                    qi3 = qt % 3
                    allsc = ['sc%d_%d' % (sb_, kb) for kb in range(8)]
                    K.op('dve', lambda e: e.tensor_reduce(out=bis[P, 5:6], in_=score[P, sb_, :], axis=AX.X, op=ALU.max),
                         reads=allsc, writes=['bisA'])
                    K.op('dve', lambda e: e.tensor_reduce(out=bis[P, 6:7], in_=score[P, sb_, 0:256], axis=AX.X,
                                                          op=ALU.min), reads=allsc, writes=['bisB'])
                    K.op('dve', lambda e: e.tensor_scalar(out=bis[P, 6:7], in0=bis[P, 6:7], scalar1=-1000.0,
                                                          scalar2=None, op0=ALU.max), reads=['bisB'], writes=['bisB'])
                    K.op('dve', lambda e: e.scalar_tensor_tensor(out=bis[P, 0:1], in0=bis[P, 5:6], scalar=-0.5,
                                                                 in1=bis[P, 6:7], op0=ALU.mult, op1=ALU.subtract),
                         reads=['bisA', 'bisB'], writes=['bisN'])
                    K.op('dve', lambda e: e.scalar_tensor_tensor(out=bis[P, 0:1], in0=bis[P, 6:7], scalar=0.5,
                                                                 in1=bis[P, 0:1], op0=ALU.mult, op1=ALU.add),
                         reads=['bisB', 'bisN'], writes=['bisN'])
                    K.op('dve', lambda e: e.tensor_tensor(out=bis[P, 1:2], in0=bis[P, 5:6], in1=bis[P, 6:7],
                                                          op=ALU.subtract), reads=['bisA', 'bisB'], writes=['bisH'])
                    K.op('dve', lambda e: e.tensor_scalar(out=bis[P, 1:2], in0=bis[P, 1:2], scalar1=0.5,
                                                          scalar2=1e-3, op0=ALU.mult, op1=ALU.add),
                         reads=['bisH'], writes=['bisH'])
                    K.op('dve', lambda e: e.tensor_scalar(out=nst[P, :], in0=pwn[P, :], scalar1=bis[P, 1:2],
                                                          scalar2=None, op0=ALU.mult),
                         reads=['bisH', 'pwn'], writes=['nst'])
                    K.op('dve', lambda e: e.tensor_scalar(out=bis[P, 2:3], in0=bis[P, 1:2],
                                                          scalar1=2.0 ** -NBIS, scalar2=None, op0=ALU.mult),
                         reads=['bisH'], writes=['bisL'])
                    for it in range(NBIS):
                        K.op('act', lambda e: e.activation(out=mask[P, :], in_=score[P, sb_, :], func=AF.Sign,
                                                           bias=bis[P, 0:1], scale=1.0, accum_out=bis[P, 3:4]),
                             reads=allsc + ['bisN'], writes=['mask', 'bisC'])
                        K.op('act', lambda e: e.activation(out=bis[P, 4:5], in_=bis[P, 3:4], func=AF.Sign,
                                                           bias=c35[P, 0:1], scale=1.0),
                             reads=['bisC', 'c35'], writes=['bisS'])
                        K.op('act', lambda e, it=it: e.activation(out=bis[P, 0:1], in_=bis[P, 4:5],
                                                                  func=AF.Identity, bias=bis[P, 0:1],
                                                                  scale=nst[P, it:it + 1]),
                             reads=['bisS', 'nst', 'bisN'], writes=['bisN'])

                def emit_fin(qt):
                    sb_ = qt % 2
                    qi3 = qt % 3
                    allsc = ['sc%d_%d' % (sb_, kb) for kb in range(8)]
                    K.op('dve', lambda e: e.scalar_tensor_tensor(out=bis[P, 7:8], in0=bis[P, 0:1], scalar=-1.0,
                                                                 in1=bis[P, 2:3], op0=ALU.mult, op1=ALU.subtract),
                         reads=['bisN', 'bisL'], writes=['bisT'])
                    K.op('dve', lambda e: e.tensor_scalar(out=mask[P, :], in0=score[P, sb_, :], scalar1=bis[P, 7:8],
                                                          scalar2=None, op0=ALU.is_ge),
                         reads=allsc + ['bisT'], writes=['mask'])
                    for g in range(8):
                        b = K.bank()
                        pT = ps[b][:].bitcast(BF16)
                        for k4 in range(4):
                            kt = g * 4 + k4
                            K.op('pe', lambda e, pT=pT, k4=k4, kt=kt: e.transpose(
                                out=pT[:, k4 * QTS:(k4 + 1) * QTS], in_=mask[P, kt * 128:(kt + 1) * 128],
                                identity=ident[P, 0:QTS]),
                                reads=['mask', 'ident'], writes=[PS(b)], signal=(k4 == 3))
                        src = pT[:, 0:4 * QTS].rearrange("p (k q) -> p k q", q=QTS)
                        dst = maskT[:, g * 4:(g + 1) * 4, qi3 * QTS:(qi3 + 1) * QTS]
                        if g % 2 == 0:
                            K.op('act', lambda e, src=src, dst=dst: e.copy(out=dst, in_=src),
                                 reads=[PS(b)], writes=['mT%d' % qi3])
                        else:
                            K.op('dve', lambda e, src=src, dst=dst: e.tensor_copy(out=dst, in_=src),
                                 reads=[PS(b)], writes=['mT%d' % qi3])

                emit_acc(0)
                for qb in range(NQB):
                    q0 = 2 + qb * QBS
                    for qi3 in range(3):
                        qt = qb * 3 + qi3
                        emit_bis(qt)
                        if qt + 1 < NQT:
                            emit_acc(qt + 1)
                        emit_fin(qt)
                    its = [(h, kt) for h in range(16) for kt in range(32)]

                    def emit_S(i, q0=q0):
                        h, kt = its[i]
                        n = h // 4
                        if kt == 0 and h % 4 == 0:
                            ksl = kvload[0] % 2
                            kvload[0] += 1
                            kvslot[n] = ksl
                            K.dma('sp', knb[:, ksl, :], kT_d[n], 'kn%d' % ksl, writes=['kn%d' % ksl])
                        ksl = kvslot[n]
                        bs_ = SBK[i % 4]
                        es_ = i % 4
                        K.op('pe', lambda e: e.matmul(ps[bs_][:, 0:QBS], lhsT=knb[:, ksl, kt * 128:(kt + 1) * 128],
                                                      rhs=qT[:, h, q0:q0 + QBS], start=True, stop=True),
                             reads=['kn%d' % ksl, 'q_%d' % h], writes=[PS(bs_)])
                        K.op('act', lambda e: e.activation(out=Eb[:, es_, :], in_=ps[bs_][:, 0:QBS], func=AF.Exp,
                                                           scale=128.0 ** -0.5),
                             reads=[PS(bs_)], writes=['E%d' % es_])
                        K.op('dve', lambda e: e.tensor_tensor(out=Em[:, es_, :], in0=Eb[:, es_, :],
                                                              in1=maskT[:, kt, :], op=ALU.mult),
                             reads=['E%d' % es_, 'mT0', 'mT1', 'mT2'], writes=['Em%d' % es_])
                    for i in range(LOOK):
                        emit_S(i)
                    for i in range(len(its)):
                        if i + LOOK < len(its):
                            emit_S(i + LOOK)
                        h, kt = its[i]
                        n = h // 4
                        ksl = kvslot[n]
                        es_ = i % 4
                        bo, bz = (0, 1) if h % 2 == 0 else (2, 3)
                        if kt == 0 and h % 4 == 0:
                            K.dma('sp', vnb[:, :, :], v_d[n], 'vn0', writes=['vn0'])
                        K.op('pe', lambda e, bo=bo, kt=kt, ksl=ksl, es_=es_: e.matmul(
                            ps[bo][:, 0:QBS], lhsT=vnb[:, kt, :], rhs=Em[:, es_, :],
                            start=(kt == 0), stop=(kt == 31)),
                            reads=['vn0', 'Em%d' % es_], writes=[PS(bo)], signal=False)
                        K.op('pe', lambda e, bz=bz, kt=kt, es_=es_: e.matmul(
                            ps[bz][:, 0:QBS], lhsT=onesb[:, :], rhs=Em[:, es_, :],
                            start=(kt == 0), stop=(kt == 31)),
                            reads=['onesb', 'Em%d' % es_], writes=[PS(bz)], signal=True)
                        if kt == 31:
                            K.op('dve', lambda e, bz=bz: e.reciprocal(out=rz[:, :], in_=ps[bz][:, 0:QBS]),
                                 reads=[PS(bz)], writes=['rz'])
                            ys = h % 2
                            K.op('dve', lambda e, bo=bo, ys=ys: e.tensor_tensor(out=ybs[:, ys, :], in0=ps[bo][:, 0:QBS],
                                                                                in1=rz[:, :], op=ALU.mult),
                                 reads=[PS(bo), 'rz'], writes=['ybs%d' % ys])
                            yb_toks.append(K.dma('sp', yb_d[h][:, q0:q0 + QBS], ybs[:, ys, :], 'ybs%d' % ys,
                                                 reads=['ybs%d' % ys]))
                K.barrier()

        K.barrier()
        pk.close()
        u2_d = nc.dram_tensor("u2_s", [16, 128, NT], BF16, kind="Internal").ap()
        with ExitStack() as pQ:
            QT = sb("QT", [128, 16, NT], BF16, pQ)
            with ExitStack() as pm:
                u1T = sb("u1Tb", [128, 16, NT], BF16, pm)
                build_u(u1T, pm, SC_M, SH_M, xo_d, "b")
                sg = sb("sg", [128, 2, 343], F32, pm)
                sgi = [0]

                def gated(c, banksA, banksG, accumulate):
                    for ti, (a, z) in enumerate(TT):
                        n = z - a
                        s_ = sgi[0] % 2
                        sgi[0] += 1
                        K.op('act', lambda e, b=banksG[ti], s_=s_, n=n: e.activation(out=sg[:, s_, 0:n],
                                                                                   in_=ps[b][:, 0:n], func=AF.Sigmoid),
                             reads=[PS(banksG[ti])], writes=['sg%d' % s_])
                        if not accumulate:
                            K.op('dve', lambda e, b=banksA[ti], s_=s_, n=n, a=a, z=z: e.tensor_tensor(
                                out=QT[:, c, a:z], in0=ps[b][:, 0:n], in1=sg[:, s_, 0:n], op=ALU.mult),
                                reads=[PS(banksA[ti]), 'sg%d' % s_], writes=['m_%d' % c])
                        else:
                            K.op('dve', lambda e, b=banksA[ti], s_=s_, n=n: e.tensor_tensor(
                                out=sg[:, s_, 0:n], in0=ps[b][:, 0:n], in1=sg[:, s_, 0:n], op=ALU.mult),
                                reads=[PS(banksA[ti]), 'sg%d' % s_], writes=['sg%d' % s_])
                            K.op('dve', lambda e, s_=s_, n=n, a=a, z=z: e.tensor_tensor(
                                out=QT[:, c, a:z], in0=sg[:, s_, 0:n], in1=QT[:, c, a:z], op=ALU.add),
                                reads=['sg%d' % s_, 'm_%d' % c], writes=['m_%d' % c])

                with ExitStack() as pm2:
                    ybT = sb("ybT", [128, 16, NT], BF16, pm2)
                    K.wait('sp', yb_toks)
                    for h in range(16):
                        K.op('pool', lambda e, h=h: e.memset(ybT[:, h, 0:2], 0.0), writes=['yb_%d' % h])
                    for h in range(16):
                        K.dma('sp', ybT[:, h, 2:NT], yb_d[h][:, 2:NT], 'ybl', writes=['yb_%d' % h])
                    for h in range(16):
                        K.res['yb_%d' % h][0] = ('ybl', K.cnt['ybl'])
                    for c in range(16):
                        bA = dense(ybT, lambda kc: 'yb_%d' % kc)
                        bG = dense(u1T, lambda kc: 'u_%d' % kc)
                        gated(c, bA, bG, False)
                    K.barrier()
                with ExitStack() as pm1:
                    yaT = sb("yaT", [128, 16, NT], BF16, pm1)
                    with ExitStack() as p2b:
                        tb_ = sb("tbuf", [128, 2, NT + 2], F32, p2b)
                        hs = sb("hs", [128, 2, 343], F32, p2b)
                        Bs = sb("Bs", [128, 2, NT], F32, p2b)
                        yc = sb("yc", [128, 2, NT], F32, p2b)
                        K.op('dve', lambda e: e.memset(tb_[:, :, 0:2], 0.0), writes=['t0', 't1'])
                        hi = 0
                        for c in range(16):
                            ts_ = c % 2
                            slC = next_w()
                            slH = next_w()
                            slB = next_w()
                            for ti, (a, z) in enumerate(TT):
                                n = z - a
                                bC = K.bank()
                                mm_group(bC, 128, n, [(wring[:, slC, kc, :], u1T[:, kc, a:z]) for kc in range(16)],
                                         [['w%d' % slC, 'u_%d' % kc] for kc in range(16)])
                                bH = K.bank()
                                mm_group(bH, 128, n, [(wring[:, slH, kc, :], u1T[:, kc, a:z]) for kc in range(16)],
                                         [['w%d' % slH, 'u_%d' % kc] for kc in range(16)])
                                h_ = hi % 2
                                hi += 1
                                K.op('act', lambda e, bH=bH, h_=h_, n=n: e.copy(out=hs[:, h_, 0:n], in_=ps[bH][:, 0:n]),
                                     reads=[PS(bH)], writes=['hs%d' % h_])
                                K.op('dve', lambda e, bC=bC, h_=h_, n=n, a=a, z=z, ts_=ts_: e.tensor_tensor(
                                    out=tb_[:, ts_, 2 + a:2 + z], in0=ps[bC][:, 0:n], in1=hs[:, h_, 0:n], op=ALU.mult),
                                    reads=[PS(bC), 'hs%d' % h_], writes=['t%d' % ts_])
                            for ti, (a, z) in enumerate(TT):
                                n = z - a
                                bB = K.bank()
                                mm_group(bB, 128, n, [(wring[:, slB, kc, :], u1T[:, kc, a:z]) for kc in range(16)],
                                         [['w%d' % slB, 'u_%d' % kc] for kc in range(16)])
                                K.op('act', lambda e, bB=bB, n=n, a=a, z=z, ts_=ts_: e.copy(out=Bs[:, ts_, a:z],
                                                                                          in_=ps[bB][:, 0:n]),
                                     reads=[PS(bB)], writes=['Bs%d' % ts_])
                            K.op('dve', lambda e, ts_=ts_: e.tensor_scalar(out=tb_[:, ts_, 2:2 + HALO],
                                                                           in0=tb_[:, ts_, 2:2 + HALO],
                                                                           scalar1=halom[:, 0:1], scalar2=None,
                                                                           op0=ALU.mult),
                                 reads=['t%d' % ts_, 'halom'], writes=['t%d' % ts_])
                            K.op('dve', lambda e, ts_=ts_, c=c: e.tensor_scalar(out=yc[:, ts_, :],
                                                                                in0=tb_[:, ts_, 2:NT + 2],
                                                                                scalar1=cva[:, c, 2:3], scalar2=None,
                                                                                op0=ALU.mult),
                                 reads=['t%d' % ts_, 'cva'], writes=['yc%d' % ts_])
                            for jj in (1, 0):
                                K.op('dve', lambda e, ts_=ts_, c=c, jj=jj: e.scalar_tensor_tensor(
                                    out=yc[:, ts_, :], in0=tb_[:, ts_, jj:NT + jj], scalar=cva[:, c, jj:jj + 1],
                                    in1=yc[:, ts_, :], op0=ALU.mult, op1=ALU.add),
                                    reads=['t%d' % ts_, 'cva', 'yc%d' % ts_], writes=['yc%d' % ts_])
                            K.op('dve', lambda e, ts_=ts_, c=c: e.tensor_tensor(out=yaT[:, c, :], in0=Bs[:, ts_, :],
                                                                                 in1=yc[:, ts_, :], op=ALU.mult),
                                 reads=['Bs%d' % ts_, 'yc%d' % ts_], writes=['ya_%d' % c])
                        K.barrier()
                    for c in range(16):
                        bA = dense(yaT, lambda kc: 'ya_%d' % kc)
                        bG = dense(u1T, lambda kc: 'u_%d' % kc)
                        gated(c, bA, bG, True)
                    K.barrier()

            def layernorm(zT, ntile, cols, gi, bi, scope, tag, post):
                sq = sb("sq" + tag, [128, 2, 343], F32, scope)
                mean = sb("mean" + tag, [128, 343], F32, scope)
                rstd = sb("rstd" + tag, [128, 343], F32, scope)
                tmp = sb("tmp" + tag, [128, 2, 343], F32, scope)
                for ti, (a, z) in ntile:
                    n = z - a
                    b1 = K.bank()
                    mm_group(b1, 128, n, [(onesf[:, :], zT[:, c, a:z]) for c in range(16)],
                             [['onesf', 'z_%d' % c] for c in range(16)])
                    b2 = K.bank()
                    for c in range(16):
                        s_ = c % 2
                        K.op('act', lambda e, s_=s_, c=c, n=n, a=a, z=z: e.activation(out=sq[:, s_, 0:n],
                                                                                    in_=zT[:, c, a:z], func=AF.Square),
                             reads=['z_%d' % c], writes=['sq%d' % s_])
                        K.op('pe', lambda e, s_=s_, c=c, n=n: e.matmul(ps[b2][:, 0:n], lhsT=onesf[:, :],
                                                                       rhs=sq[:, s_, 0:n], start=(c == 0),
                                                                       stop=(c == 15)),
                             reads=['onesf', 'sq%d' % s_], writes=[PS(b2)], signal=True)
                    K.op('dve', lambda e, n=n: e.tensor_scalar(out=mean[:, 0:n], in0=ps[b1][:, 0:n], scalar1=1.0 / D,
                                                               scalar2=None, op0=ALU.mult),
                         reads=[PS(b1)], writes=['mean'])
                    K.op('dve', lambda e, n=n: e.tensor_tensor(out=rstd[:, 0:n], in0=mean[:, 0:n], in1=mean[:, 0:n],
                                                               op=ALU.mult), reads=['mean'], writes=['rstd'])
                    K.op('dve', lambda e, n=n: e.scalar_tensor_tensor(out=rstd[:, 0:n], in0=ps[b2][:, 0:n],
                                                                      scalar=1.0 / D, in1=rstd[:, 0:n],
                                                                      op0=ALU.mult, op1=ALU.subtract),
                         reads=[PS(b2), 'rstd'], writes=['rstd'])
                    K.op('act', lambda e, n=n: e.activation(out=rstd[:, 0:n], in_=rstd[:, 0:n], func=AF.Sqrt,
                                                            bias=epst[:, 0:1], scale=1.0),
                         reads=['rstd', 'epst'], writes=['rstd'])
                    K.op('dve', lambda e, n=n: e.reciprocal(out=rstd[:, 0:n], in_=rstd[:, 0:n]),
                         reads=['rstd'], writes=['rstd'])
                    for c in range(16):
                        s_ = c % 2
                        eng = 'dve'
                        K.op(eng, lambda e, s_=s_, c=c, n=n, a=a, z=z: e.tensor_tensor(
                            out=tmp[:, s_, 0:n], in0=zT[:, c, a:z], in1=mean[:, 0:n], op=ALU.subtract),
                            reads=['z_%d' % c, 'mean'], writes=['tmp%d' % s_])
                        K.op(eng, lambda e, s_=s_, n=n: e.tensor_tensor(
                            out=tmp[:, s_, 0:n], in0=tmp[:, s_, 0:n], in1=rstd[:, 0:n], op=ALU.mult),
                            reads=['tmp%d' % s_, 'rstd'], writes=['tmp%d' % s_])
                        K.op('dve', lambda e, s_=s_, c=c, n=n, a=a, z=z: e.tensor_scalar(
                            out=zT[:, c, a:z], in0=tmp[:, s_, 0:n], scalar1=lnp[:, gi, c:c + 1],
                            scalar2=lnp[:, bi, c:c + 1], op0=ALU.mult, op1=ALU.add),
                            reads=['tmp%d' % s_, 'lnp'], writes=['z_%d' % c])
                        post(c, ti, a, z)

            x1_toks = []
            u2_toks = []
            with ExitStack() as pl:
                zT = sb("zT", [128, 16, NT], F32, pl)
                xr2 = sb("xr2", [128, 2, NT], F32, pl)
                u2s = sb("u2s", [128, 2, 343], BF16, pl)
                for c in range(16):
                    sl = c % 2
                    K.dma('sp', xr2[:, sl, :], xo_d[c], 'xq%d' % sl, writes=['xq%d' % sl])
                    K.op('act', lambda e, sl=sl: e.mul(out=xr2[:, sl, :], in_=xr2[:, sl, :], mul=ALPHA),
                         reads=['xq%d' % sl], writes=['xq%d' % sl])
                    banks = dense(QT, lambda kc: 'm_%d' % kc)
                    for ti, (a, z) in enumerate(TT):
                        n = z - a
                        K.op('dve', lambda e, b=banks[ti], n=n, a=a, z=z, c=c, sl=sl: e.scalar_tensor_tensor(
                            out=zT[:, c, a:z], in0=ps[b][:, 0:n], scalar=modP[:, G_M + c:G_M + c + 1],
                            in1=xr2[:, sl, a:z], op0=ALU.mult, op1=ALU.add),
                            reads=[PS(banks[ti]), 'modA', 'modB', 'xq%d' % sl], writes=['z_%d' % c])
                u2i = [0]

                def post1(c, ti, a, z):
                    n = z - a
                    s_ = u2i[0] % 2
                    u2i[0] += 1
                    K.op('act', lambda e: e.activation(out=u2s[:, s_, 0:n], in_=zT[:, c, a:z], func=AF.Identity,
                                                       bias=modT[:, SH_F + c:SH_F + c + 1],
                                                       scale=modP[:, SC_F + c:SC_F + c + 1]),
                         reads=['z_%d' % c, 'modA', 'modB'], writes=['u2s%d' % s_])
                    u2_toks.append(K.dma('sp', u2_d[c][:, a:z], u2s[:, s_, 0:n], 'u2s%d' % s_, reads=['u2s%d' % s_]))
                    x1_toks.append(K.dma('sp', x1_d[c][:, a:z], zT[:, c, a:z], 'x1st', reads=['z_%d' % c]))
                layernorm(zT, list(enumerate(TT)), NT, 0, 1, pl, "1", post1)
                K.barrier()
            K.barrier()

        with ExitStack() as pf:
            gT = sb("gT", [128, NFC, NT], BF16, pf)
            with ExitStack() as pu:
                u2T = sb("u2T", [128, 16, NT], BF16, pu)
                araw = sb("araw", [128, 2, NT + 2], F32, pu)
                ac = sb("ac", [128, 2, NT], F32, pu)
                K.wait('sp', u2_toks)
                for c in range(16):
                    K.dma('sp', u2T[:, c, :], u2_d[c], 'u2l', writes=['u2_%d' % c])
                for c in range(16):
                    K.res['u2_%d' % c][0] = ('u2l', K.cnt['u2l'])
                K.op('dve', lambda e: e.memset(araw[:, :, 0:2], 0.0), writes=['ar0', 'ar1'])
                for j in range(NFC):
                    as_ = j % 2
                    bA = dense(u2T, lambda kc: 'u2_%d' % kc)
                    for ti, (a, z) in enumerate(TT):
                        K.op('act', lambda e, b=bA[ti], a=a, z=z, as_=as_: e.copy(out=araw[:, as_, 2 + a:2 + z],
                                                                                in_=ps[b][:, 0:z - a]),
                             reads=[PS(bA[ti])], writes=['ar%d' % as_])
                    K.op('dve', lambda e, as_=as_: e.tensor_scalar(out=araw[:, as_, 2:2 + HALO],
                                                                   in0=araw[:, as_, 2:2 + HALO],
                                                                   scalar1=halom[:, 0:1], scalar2=None, op0=ALU.mult),
                         reads=['ar%d' % as_, 'halom'], writes=['ar%d' % as_])
                    K.op('act', lambda e, as_=as_, j=j: e.activation(out=ac[:, as_, :], in_=araw[:, as_, 2:NT + 2],
                                                                        func=AF.Identity, scale=cvf[:, j, 2:3]),
                         reads=['ar%d' % as_, 'cvf'], writes=['ac%d' % as_])
                    for jj in (1, 0):
                        K.op('dve', lambda e, as_=as_, j=j, jj=jj: e.scalar_tensor_tensor(
                            out=ac[:, as_, :], in0=araw[:, as_, jj:NT + jj], scalar=cvf[:, j, jj:jj + 1],
                            in1=ac[:, as_, :], op0=ALU.mult, op1=ALU.add),
                            reads=['ar%d' % as_, 'cvf', 'ac%d' % as_], writes=['ac%d' % as_])
                    K.op('act', lambda e, as_=as_: e.activation(out=ac[:, as_, :], in_=ac[:, as_, :],
                                                                func=AF.Gelu_apprx_tanh),
                         reads=['ac%d' % as_], writes=['ac%d' % as_])
                    bB = dense(u2T, lambda kc: 'u2_%d' % kc)
                    for ti, (a, z) in enumerate(TT):
                        K.op('dve', lambda e, b=bB[ti], a=a, z=z, as_=as_, j=j: e.tensor_tensor(
                            out=gT[:, j, a:z], in0=ps[b][:, 0:z - a], in1=ac[:, as_, a:z], op=ALU.mult),
                            reads=[PS(bB[ti]), 'ac%d' % as_], writes=['g_%d' % j])
                K.barrier()
            with ExitStack() as pd:
                z2 = sb("z2", [128, 16, NT], F32, pd)
                x1r = sb("x1r", [128, NT], F32, pd)
                K.wait('sp', x1_toks)
                out_toks = []
                for c in range(16):
                    K.dma('sp', x1r[:, :], x1_d[c], 'x1r0', writes=['x1r0'])
                    K.op('act', lambda e: e.mul(out=x1r[:, :], in_=x1r[:, :], mul=ALPHA),
                         reads=['x1r0'], writes=['x1r0'])
                    slots = [next_w() for part in range(3)]
                    for ti, (a, z) in enumerate(TT):
                        n = z - a
                        b = K.bank()
                        pairs, rl = [], []
                        for part in range(3):
                            for kc in range(min(16, NFC - part * 16)):
                                pairs.append((wring[:, slots[part], kc, :], gT[:, part * 16 + kc, a:z]))
                                rl.append(['w%d' % slots[part], 'g_%d' % (part * 16 + kc)])
                        mm_group(b, 128, n, pairs, rl)
                        K.op('dve', lambda e, b=b, n=n, c=c, a=a, z=z: e.scalar_tensor_tensor(
                            out=z2[:, c, a:z], in0=ps[b][:, 0:n], scalar=modP[:, G_F + c:G_F + c + 1],
                            in1=x1r[:, a:z], op0=ALU.mult, op1=ALU.add),
                            reads=[PS(b), 'modA', 'modB', 'x1r0'], writes=['z_%d' % c])
                for ti, (a, z) in enumerate(TT):
                    def post2(c, ti_, a_, z_):
                        if c == 15:
                            out_toks.append(K.dma('sp', out_d[:, :, a_:z_].rearrange("c p t -> p c t"), z2[:, :, a_:z_],
                                                  'outst', reads=['z_%d' % cc for cc in range(16)]))
                    with ExitStack() as pln:
                        layernorm(z2, [(ti, (a, z))], NT, 2, 3, pln, "2_%d" % ti, post2)
                        K.barrier()
                K.wait('sp', [('outst', K.cnt['outst'])])
                if debug:
                    K.wait('sp', [('dbg', K.cnt['dbg'])])
    return nc


_NC_CACHE = {}


def _prep(inputs):
    f32 = np.float32
    x = np.asarray(inputs['x'], f32)
    c = np.asarray(inputs['c'], f32)
    W = {k: np.asarray(inputs[k][0], f32) for k in ('w_cond', 'w_in', 'w_a', 'w_b', 'w_o', 'w_up', 'w_down')}
    items, seq = stream_plan()

    def fm(Wm, col0, ncols, kc0=0, nkc=16):
        blk = Wm[kc0 * 128:(kc0 + nkc) * 128, col0:col0 + ncols].reshape(nkc, 128, ncols).transpose(1, 0, 2)
        if nkc < 16:
            blk = np.concatenate([blk, np.zeros((128, 16 - nkc, ncols), f32)], axis=1)
        return blk
    ws = np.empty((len(items), 128, 16, 128), f32)
    for i, (nm, col0, kc0, nkc) in enumerate(items):
        ws[i] = fm(W[nm], col0, 128, kc0, nkc)
    wcond = np.ascontiguousarray(W['w_cond'].reshape(16, 128, 24, 512).transpose(2, 1, 0, 3))
    bcond = np.ascontiguousarray(np.asarray(inputs['b_cond'], f32).reshape(96, 128).T)
    wk = np.ascontiguousarray(fm(W['w_in'], O_K, 512))
    wv = np.ascontiguousarray(fm(W['w_in'], O_V, 512))
    wki = np.ascontiguousarray(fm(W['w_in'], O_KI, 64))
    wwi = np.ascontiguousarray(fm(W['w_in'], O_WI, 16))

    def pv(v):
        return np.ascontiguousarray(np.asarray(v, f32).reshape(-1, 128).T)
    cva = np.ascontiguousarray(np.stack([pv(inputs['conv_a'][0][j]) for j in range(3)], axis=2))
    cvf = np.ascontiguousarray(np.stack([pv(inputs['conv_f'][0][j]) for j in range(3)], axis=2))
    lnp = np.ascontiguousarray(np.stack([pv(inputs['ln1_g'][0]), pv(inputs['ln1_b'][0]),
                                         pv(inputs['ln2_g'][0]), pv(inputs['ln2_b'][0])], axis=1))
    kng = np.ascontiguousarray(np.broadcast_to(np.stack([np.asarray(inputs['idx_kn_g'][0], f32),
                                                         np.asarray(inputs['idx_kn_b'][0], f32)])[None], (128, 2, 64)))
    cidx = np.ascontiguousarray(np.broadcast_to((np.arange(64, dtype=f32) * 64.0)[None], (128, 64)))
    ident = np.eye(128, dtype=f32)
    maps = []
    for core in range(8):
        b, q = core // 4, core % 4
        s0 = q * 1024
        g = s0 - HALO + np.arange(NT)
        xo = np.zeros((NT, D), f32)
        valid = g >= 0
        xo[valid] = x[b, g[valid]]
        xo = np.ascontiguousarray(xo.T.reshape(16, 128, NT))
        xs = np.ascontiguousarray(x[b].T.reshape(16, 128, 16, 256).transpose(2, 1, 0, 3))
        cT = np.ascontiguousarray(c[b].reshape(16, 128).T)
        gq = g[2:2 + NQT * QTS].reshape(NQT, QTS)
        lim = np.clip((np.floor_divide(gq, 64) + 1) * 64, 64, S).astype(f32)
        qlim = np.full((128, NQT), float(S), f32)
        qlim[:QTS, :] = lim.T
        halom = np.full((128, 1), 0.0 if q == 0 else 1.0, f32)
        m = dict(xo=xo, xs=xs, cT=cT, wcond=wcond, bcondT=bcond, ws=ws, wk=wk, wv=wv, wki=wki, wwi=wwi,
                 cva=cva, cvf=cvf, lnp=lnp, kng=kng, qlim=qlim, halom=halom, cidx=cidx, ident=ident)
        maps.append({"i_" + k_: v_ for k_, v_ in m.items()})
    return maps


def _assemble(results):
    out = np.empty((2, S, D), np.float32)
    for core in range(8):
        b, q = core // 4, core % 4
        o = np.asarray(results[core]["out"], np.float32)
        out[b, q * 1024:(q + 1) * 1024, :] = o.reshape(D, NT)[:, HALO:].T
    return out


def kernel(**inputs):
    if 'nc' not in _NC_CACHE:
        _NC_CACHE['nc'] = build_nc(False)
    maps = _prep(inputs)
    res = run_bass_kernel_spmd(_NC_CACHE['nc'], maps, core_ids=list(range(8)))
    return _assemble(res.results)
```
